# Optimizing a Trainium2 kernel written in Bass

```python
import math
import jax
import jax.numpy as jnp
from jax import lax
import numpy as np

D_MODEL = 1024
BATCH = 2
SEQ = 8192
DEPTH = 2

HEAD_DIM = 64
BRANCH_WIDTH = D_MODEL // 2
N_BRANCHES = 4
N_HEADS = BRANCH_WIDTH // HEAD_DIM
SSM_HEADS = N_HEADS
SSM_GROUPS = 2
SSM_STATE = 128
SSM_CONV = 4
SSM_CHUNK = 256
SSM_BC = SSM_GROUPS * SSM_STATE
SSM_CONV_DIM = BRANCH_WIDTH + 2 * SSM_BC
MOBA_BLOCK = 256
MOBA_TOPK = 3
MOBA_Q_CHUNK = 64
FOX_Q_BLOCK = 128
SWA_WINDOW = 128
SWA_KV_HEADS = 2
REL_BUCKETS = 32
REL_MAX_EXACT = REL_BUCKETS // 2
REL_MAX_DISTANCE = 1024
N_BIAS_HEADS = 2 * N_HEADS
D_FF = 256 * math.ceil(8 * D_MODEL / 3 / 256)
NORM_EPS = 1e-6
SPLIT_SIZES = (
    BRANCH_WIDTH,
    SSM_CONV_DIM,
    SSM_HEADS,
    3 * BRANCH_WIDTH,
    3 * BRANCH_WIDTH,
    N_HEADS,
    BRANCH_WIDTH,
    2 * SWA_KV_HEADS * HEAD_DIM,
    N_BRANCHES * D_MODEL,
)
D_IN_PROJ = sum(SPLIT_SIZES)
SPLIT_POINTS = [int(v) for v in np.cumsum(SPLIT_SIZES)[:-1]]

kernel_name = 'hybrid_ssd_moba_fox_swa_gated_block'


def rmsnorm(x, w):
    xf = x.astype(jnp.float32)
    y = xf * lax.rsqrt(jnp.mean(xf * xf, axis=-1, keepdims=True) + NORM_EPS)
    return (y * w.astype(jnp.float32)).astype(x.dtype)


def rel_bucket(dist):
    dist = jnp.maximum(dist, 0)
    d = jnp.maximum(dist, 1).astype(jnp.float32)
    large = REL_MAX_EXACT + (jnp.log(d / REL_MAX_EXACT) / math.log(REL_MAX_DISTANCE / REL_MAX_EXACT)
                             * (REL_BUCKETS - REL_MAX_EXACT)).astype(jnp.int32)
    large = jnp.minimum(large, REL_BUCKETS - 1)
    return jnp.where(dist < REL_MAX_EXACT, dist, large)


def causal_depthwise_conv(u, w, b):
    c = u.shape[-1]
    out = lax.conv_general_dilated(u, w.astype(u.dtype)[:, None, :], window_strides=(1,),
                                   padding=((w.shape[0] - 1, 0),),
                                   dimension_numbers=('NWC', 'WIO', 'NWC'),
                                   feature_group_count=c)
    return out + b.astype(u.dtype)


def ssd_chunked_scan(xs, bm, cm, dt_raw, dt_bias, a_log, d_skip):
    f32 = jnp.float32
    bsz, s_len, n_h, p_dim = xs.shape
    n_g, n_s = bm.shape[2], bm.shape[3]
    r = n_h // n_g
    lc = SSM_CHUNK
    nc = -(-s_len // lc)
    pad = nc * lc - s_len
    xs, bm, cm = xs.astype(f32), bm.astype(f32), cm.astype(f32)
    dt = jax.nn.softplus(dt_raw.astype(f32) + dt_bias.astype(f32))
    a = dt * (-jnp.exp(a_log.astype(f32)))

    def padseq(u):
        return jnp.pad(u, [(0, 0), (0, pad)] + [(0, 0)] * (u.ndim - 2))

    xc = padseq(xs * dt[..., None]).reshape(bsz, nc, lc, n_g, r, p_dim)
    bc = padseq(bm).reshape(bsz, nc, lc, n_g, n_s)
    cc = padseq(cm).reshape(bsz, nc, lc, n_g, n_s)
    acum_l = jnp.cumsum(padseq(a).reshape(bsz, nc, lc, n_g, r), axis=2)
    acum = jnp.moveaxis(acum_l, 2, -1)
    causal = jnp.tril(jnp.ones((lc, lc), dtype=bool))
    decay_in = jnp.exp(jnp.where(causal, acum[..., :, None] - acum[..., None, :], -jnp.inf))
    cb = jnp.einsum('bclgn,bcsgn->bcgls', cc, bc)
    y_diag = jnp.einsum('bcgrls,bcsgrp->bclgrp', cb[:, :, :, None] * decay_in, xc)
    decay_to_end = jnp.exp(acum_l[:, :, -1:] - acum_l)
    states = jnp.einsum('bclgn,bclgrp->bcgrpn', bc, xc * decay_to_end[..., None])
    chunk_decay = jnp.exp(acum[..., -1])

    def step(h, inp):
        st, dec = inp
        return h * dec[..., None, None] + st, h

    h0 = jnp.zeros((bsz, n_g, r, p_dim, n_s), f32)
    _, h_in = lax.scan(step, h0, (jnp.moveaxis(states, 1, 0), jnp.moveaxis(chunk_decay, 1, 0)))
    h_in = jnp.moveaxis(h_in, 0, 1)
    y_off = jnp.einsum('bclgn,bcgrpn->bclgrp', cc, h_in) * jnp.exp(acum_l)[..., None]
    y = (y_diag + y_off).reshape(bsz, nc * lc, n_h, p_dim)[:, :s_len]
    return y + d_skip.astype(f32)[:, None] * xs


def mamba2_branch(z, xbc, dt_raw, conv_w, conv_b, dt_bias, a_log, d_skip, norm_w):
    bsz, s_len, _ = z.shape
    xbc = jax.nn.silu(causal_depthwise_conv(xbc, conv_w, conv_b))
    xs, bm, cm = jnp.split(xbc, [BRANCH_WIDTH, BRANCH_WIDTH + SSM_BC], axis=-1)
    y = ssd_chunked_scan(xs.reshape(bsz, s_len, SSM_HEADS, HEAD_DIM),
                         bm.reshape(bsz, s_len, SSM_GROUPS, SSM_STATE),
                         cm.reshape(bsz, s_len, SSM_GROUPS, SSM_STATE),
                         dt_raw, dt_bias, a_log, d_skip)
    y = y.reshape(bsz, s_len, BRANCH_WIDTH).astype(z.dtype)
    return rmsnorm(y * jax.nn.silu(z), norm_w)


def moba_attention(q, k, v, bias_tab):
    f32 = jnp.float32
    bsz, s_len, n_h, hd = q.shape
    nblk = -(-s_len // MOBA_BLOCK)
    s_pad = nblk * MOBA_BLOCK
    topk = min(MOBA_TOPK, nblk)
    scale = hd ** -0.5
    qh = q.transpose(0, 2, 1, 3)

    def blocks(u):
        u = jnp.pad(u, ((0, 0), (0, s_pad - s_len), (0, 0), (0, 0)))
        return u.reshape(bsz, nblk, MOBA_BLOCK, n_h, hd).transpose(0, 3, 1, 2, 4)

    kb, vb = blocks(k), blocks(v)
    k_mean = jnp.mean(kb.astype(f32), axis=3)
    q_blk = jnp.arange(s_len) // MOBA_BLOCK
    gate = jnp.einsum('bhsd,bhnd->bhsn', qh.astype(f32), k_mean)
    fully_past = jnp.arange(nblk)[None, :] < q_blk[:, None]
    gate = jnp.where(fully_past, gate, -jnp.inf)
    _, sel = lax.top_k(gate, topk)
    sel_valid = jnp.arange(topk)[None, :] < q_blk[:, None]
    bias_ht = bias_tab.T.astype(f32)
    b_idx = jnp.arange(bsz)[:, None, None, None]
    h_idx = jnp.arange(n_h)[None, :, None, None]
    blk_off = jnp.arange(MOBA_BLOCK)

    def chunk(c):
        start = c * MOBA_Q_CHUNK
        qc = lax.dynamic_slice_in_dim(qh, start, MOBA_Q_CHUNK, axis=2)
        selc = lax.dynamic_slice_in_dim(sel, start, MOBA_Q_CHUNK, axis=2)
        validc = lax.dynamic_slice_in_dim(sel_valid, start, MOBA_Q_CHUNK, axis=0)
        qpos = start + jnp.arange(MOBA_Q_CHUNK)
        kg = kb[b_idx, h_idx, selc]
        vg = vb[b_idx, h_idx, selc]
        s_sel = jnp.einsum('bhqd,bhqkjd->bhqkj', qc, kg).astype(f32) * scale
        kpos = selc[..., None] * MOBA_BLOCK + blk_off
        s_sel = s_sel + bias_ht[h_idx[..., None], rel_bucket(qpos[:, None, None] - kpos)]
        s_sel = jnp.where(validc[None, None, :, :, None], s_sel, -jnp.inf)
        own = start // MOBA_BLOCK
        ko = lax.dynamic_index_in_dim(kb, own, axis=2, keepdims=False)
        vo = lax.dynamic_index_in_dim(vb, own, axis=2, keepdims=False)
        dist_own = qpos[:, None] - (own * MOBA_BLOCK + blk_off)[None, :]
        s_own = jnp.einsum('bhqd,bhjd->bhqj', qc, ko).astype(f32) * scale
        s_own = s_own + bias_ht[:, rel_bucket(dist_own)][None]
        s_own = jnp.where(dist_own >= 0, s_own, -jnp.inf)
        logits = jnp.concatenate([s_sel.reshape(bsz, n_h, MOBA_Q_CHUNK, topk * MOBA_BLOCK), s_own], axis=-1)
        p = jax.nn.softmax(logits, axis=-1)
        p_sel = p[..., :topk * MOBA_BLOCK].reshape(bsz, n_h, MOBA_Q_CHUNK, topk, MOBA_BLOCK).astype(v.dtype)
        p_own = p[..., topk * MOBA_BLOCK:].astype(v.dtype)
        return (jnp.einsum('bhqkj,bhqkjd->bhqd', p_sel, vg)
                + jnp.einsum('bhqj,bhjd->bhqd', p_own, vo))

    out = lax.map(chunk, jnp.arange(s_len // MOBA_Q_CHUNK))
    return out.transpose(1, 0, 3, 2, 4).reshape(bsz, s_len, n_h * hd)


def forgetting_attention(q, k, v, f_logit):
    f32 = jnp.float32
    bsz, s_len, n_h, hd = q.shape
    scale = hd ** -0.5
    cum = jnp.cumsum(jax.nn.log_sigmoid(f_logit.astype(f32)), axis=1).transpose(0, 2, 1)
    qh, kh, vh = (u.transpose(0, 2, 1, 3) for u in (q, k, v))
    kpos = jnp.arange(s_len)

    def block(i):
        start = i * FOX_Q_BLOCK
        qb = lax.dynamic_slice_in_dim(qh, start, FOX_Q_BLOCK, axis=2)
        cq = lax.dynamic_slice_in_dim(cum, start, FOX_Q_BLOCK, axis=2)
        s = (jnp.einsum('bhqd,bhkd->bhqk', qb, kh).astype(f32) * scale
             + (cq[..., :, None] - cum[..., None, :]))
        causal = (start + jnp.arange(FOX_Q_BLOCK))[:, None] >= kpos[None, :]
        p = jax.nn.softmax(jnp.where(causal, s, -jnp.inf), axis=-1)
        return jnp.einsum('bhqk,bhkd->bhqd', p.astype(v.dtype), vh)

    out = lax.map(block, jnp.arange(s_len // FOX_Q_BLOCK))
    return out.transpose(1, 0, 3, 2, 4).reshape(bsz, s_len, n_h * hd)


def sliding_window_attention(q, k, v, sinks, bias_tab):
    f32 = jnp.float32
    bsz, s_len, n_q, hd = q.shape
    n_kv = k.shape[2]
    grp = n_q // n_kv
    w = SWA_WINDOW
    nb = s_len // w
    scale = hd ** -0.5
    qb = q.reshape(bsz, nb, w, n_kv, grp, hd)

    def band(u):
        ub = u.reshape(bsz, nb, w, n_kv, hd)
        prev = jnp.pad(ub, ((0, 0), (1, 0), (0, 0), (0, 0), (0, 0)))[:, :-1]
        return jnp.concatenate([prev, ub], axis=2)

    kk, vv = band(k), band(v)
    s = jnp.einsum('bnqkgd,bnjkd->bnkgqj', qb, kk).astype(f32) * scale
    qi = jnp.arange(w)[:, None]
    kj = jnp.arange(2 * w)[None, :]
    dist = qi + w - kj
    in_window = (dist >= 0) & (dist < w)
    key_pos = jnp.arange(nb)[:, None, None] * w + kj[None] - w
    mask = in_window[None] & (key_pos >= 0)
    bias = bias_tab.astype(f32)[rel_bucket(dist)].transpose(2, 0, 1).reshape(n_kv, grp, w, 2 * w)
    s = jnp.where(mask[None, :, None, None], s + bias, -jnp.inf)
    sink = jnp.broadcast_to(sinks.astype(f32).reshape(n_kv, grp, 1, 1), s.shape[:-1] + (1,))
    p = jax.nn.softmax(jnp.concatenate([s, sink], axis=-1), axis=-1)[..., :-1]
    out = jnp.einsum('bnkgqj,bnjkd->bnqkgd', p.astype(v.dtype), vv)
    return out.reshape(bsz, s_len, n_q * hd)


def hybrid_mixer(h, w_in, conv_w, conv_b, dt_bias, a_log, d_skip, ssm_norm_w,
                 forget_bias, sinks, rel_bias, w_branch, w_out):
    bsz, s_len, _ = h.shape
    proj = h @ w_in
    (m_z, m_xbc, m_dt, b_qkv, c_qkv, c_f, d_q, d_kv, gate_logits) = jnp.split(proj, SPLIT_POINTS, axis=-1)

    def heads(u, n):
        return u.reshape(bsz, s_len, n, HEAD_DIM)

    y_a = mamba2_branch(m_z, m_xbc, m_dt, conv_w, conv_b, dt_bias, a_log, d_skip, ssm_norm_w)
    bq, bk, bv = jnp.split(b_qkv, 3, axis=-1)
    y_b = moba_attention(heads(bq, N_HEADS), heads(bk, N_HEADS), heads(bv, N_HEADS), rel_bias[:, :N_HEADS])
    cq, ck, cv = jnp.split(c_qkv, 3, axis=-1)
    y_c = forgetting_attention(heads(cq, N_HEADS), heads(ck, N_HEADS), heads(cv, N_HEADS), c_f + forget_bias)
    dk, dv = jnp.split(d_kv, 2, axis=-1)
    y_d = sliding_window_attention(heads(d_q, N_HEADS), heads(dk, SWA_KV_HEADS), heads(dv, SWA_KV_HEADS),
                                   sinks, rel_bias[:, N_HEADS:])
    branches = jnp.stack([y_a, y_b, y_c, y_d], axis=2)
    branch_out = jnp.einsum('bsiw,iwd->bsid', branches, w_branch)
    gates = jax.nn.sigmoid(gate_logits.reshape(bsz, s_len, N_BRANCHES, D_MODEL))
    return jnp.sum(gates * branch_out, axis=2) @ w_out


def swiglu(h, w_gate, w_up, w_down):
    return (jax.nn.silu(h @ w_gate) * (h @ w_up)) @ w_down


def setup_inputs(seed: int = 0) -> dict:
    key = jax.random.key(seed)
    ks = jax.random.split(key, 20)
    nrm = jax.random.normal
    x = nrm(ks[0], (BATCH, SEQ, D_MODEL), jnp.float32)
    w_in = nrm(ks[1], (DEPTH, D_MODEL, D_IN_PROJ), jnp.float32) * D_MODEL ** -0.5
    conv_w = nrm(ks[2], (DEPTH, SSM_CONV, SSM_CONV_DIM), jnp.float32) * SSM_CONV ** -0.5
    conv_b = 0.02 * nrm(ks[3], (DEPTH, SSM_CONV_DIM), jnp.float32)
    dt0 = jnp.exp(jax.random.uniform(ks[4], (DEPTH, SSM_HEADS), jnp.float32,
                                     minval=math.log(1e-3), maxval=math.log(1e-1)))
    dt_bias = dt0 + jnp.log(-jnp.expm1(-dt0))
    a_log = jnp.log(jax.random.uniform(ks[5], (DEPTH, SSM_HEADS), jnp.float32, minval=1.0, maxval=16.0))
    d_skip = 1.0 + 0.1 * nrm(ks[6], (DEPTH, SSM_HEADS), jnp.float32)
    ssm_norm_w = 1.0 + 0.05 * nrm(ks[7], (DEPTH, BRANCH_WIDTH), jnp.float32)
    forget_bias = jax.random.uniform(ks[8], (DEPTH, N_HEADS), jnp.float32, minval=1.0, maxval=6.0)
    sinks = nrm(ks[9], (DEPTH, N_HEADS), jnp.float32)
    rel_bias = 0.3 * nrm(ks[10], (REL_BUCKETS, N_BIAS_HEADS), jnp.float32)
    w_branch = nrm(ks[11], (DEPTH, N_BRANCHES, BRANCH_WIDTH, D_MODEL), jnp.float32) * BRANCH_WIDTH ** -0.5
    w_out = nrm(ks[12], (DEPTH, D_MODEL, D_MODEL), jnp.float32) * D_MODEL ** -0.5
    norm_mix = 1.0 + 0.05 * nrm(ks[13], (DEPTH, D_MODEL), jnp.float32)
    norm_ffn = 1.0 + 0.05 * nrm(ks[14], (DEPTH, D_MODEL), jnp.float32)
    w_ffn_gate = nrm(ks[15], (DEPTH, D_MODEL, D_FF), jnp.float32) * D_MODEL ** -0.5
    w_ffn_up = nrm(ks[16], (DEPTH, D_MODEL, D_FF), jnp.float32) * D_MODEL ** -0.5
    w_ffn_down = nrm(ks[17], (DEPTH, D_FF, D_MODEL), jnp.float32) * D_FF ** -0.5
    norm_final = 1.0 + 0.05 * nrm(ks[18], (D_MODEL,), jnp.float32)
    return {'x': x, 'w_in': w_in, 'conv_w': conv_w, 'conv_b': conv_b, 'dt_bias': dt_bias,
            'a_log': a_log, 'd_skip': d_skip, 'ssm_norm_w': ssm_norm_w, 'forget_bias': forget_bias,
            'sinks': sinks, 'rel_bias': rel_bias, 'w_branch': w_branch, 'w_out': w_out,
            'norm_mix': norm_mix, 'norm_ffn': norm_ffn, 'w_ffn_gate': w_ffn_gate,
            'w_ffn_up': w_ffn_up, 'w_ffn_down': w_ffn_down, 'norm_final': norm_final}


def reference(x, w_in, conv_w, conv_b, dt_bias, a_log, d_skip, ssm_norm_w, forget_bias,
              sinks, rel_bias, w_branch, w_out, norm_mix, norm_ffn, w_ffn_gate,
              w_ffn_up, w_ffn_down, norm_final):
    for l in range(DEPTH):
        h = rmsnorm(x, norm_mix[l])
        x = x + hybrid_mixer(h, w_in[l], conv_w[l], conv_b[l], dt_bias[l], a_log[l], d_skip[l],
                             ssm_norm_w[l], forget_bias[l], sinks[l], rel_bias, w_branch[l], w_out[l])
        h = rmsnorm(x, norm_ffn[l])
        x = x + swiglu(h, w_ffn_gate[l], w_ffn_up[l], w_ffn_down[l])
    return rmsnorm(x, norm_final)
```

```python
import numpy as np
import ml_dtypes
from contextlib import ExitStack
import concourse.bass as bass
import concourse.mybir as mybir
from concourse.bass_utils import run_bass_kernel_spmd

F32 = mybir.dt.float32
BF16 = mybir.dt.bfloat16
AF = mybir.ActivationFunctionType
ALU = mybir.AluOpType
AX = mybir.AxisListType
NPBF = ml_dtypes.bfloat16

import os
DBG = bool(os.environ.get('FWDBG'))
ENGS = ("pe", "act", "dve", "pool", "sp")


class Prog:
    NDS = 24

    def __init__(self, name="k"):
        self.nc = bass.Bass("TRN2", target_bir_lowering=False)
        self.es = ExitStack()
        self.ops = {e: [] for e in ENGS}
        self.lastw = {}
        self.readers = {}
        self.dma_rr = 0
        self.dma_val = [0] * self.NDS
        self.out_tokens = []
        self._n = 0
        self.phase_es = None
        self.kp = ""
        self.dp = ""
        self.dov = {}

    def dram(self, name, shape, dt, kind="ExternalInput"):
        if name in self.dov:
            return self.dov[name]
        return self.nc.dram_tensor(self.dp + name, list(shape), dt, kind=kind).ap()

    def sb(self, shape, dt, name=None):
        self._n += 1
        es = self.phase_es if self.phase_es is not None else self.es
        return es.enter_context(self.nc.sbuf_tensor(name or f"sb{self._n}", list(shape), dt))

    def ps(self, shape, dt, name=None):
        self._n += 1
        es = self.phase_es if self.phase_es is not None else self.es
        return es.enter_context(self.nc.psum_tensor(name or f"ps{self._n}", list(shape), dt))

    def begin_phase(self, tag):
        assert self.phase_es is None
        self.phase_es = ExitStack()
        self.kp = tag + ":"

    def end_phase(self):
        self.barrier()
        self.phase_es.close()
        self.phase_es = None
        self.kp = ""

    def barrier(self):
        toks = set()
        for e in ENGS:
            n = 0
            last = None
            for i, o in enumerate(self.ops[e]):
                if o["fn"] is not None and o["dma"] is None:
                    last = i
            if last is not None:
                toks.add(("e", e, last))
        for k in range(self.NDS):
            if self.dma_val[k] > 0:
                toks.add(("d", k, self.dma_val[k]))
        for e in ENGS:
            deps = set(t for t in toks if not (t[0] == "e" and t[1] == e))
            self.ops[e].append(dict(fn=None, deps=deps, dma=None))

    def _deps(self, eng, reads, writes):
        raw, war = set(), set()
        for k in reads:
            t = self.lastw.get(k)
            if t is not None:
                raw.add(t)
            if k.split(":")[-1].startswith("ps"):
                rd = self.readers.get(k)
                if rd:
                    for e2, idx in rd[0].items():
                        if e2 != eng:
                            raw.add(("e", e2, idx))
        for k in writes:
            t = self.lastw.get(k)
            if t is not None:
                raw.add(t)
            rd = self.readers.get(k)
            if rd:
                for e2, idx in rd[0].items():
                    war.add(("e", e2, idx))
                for t in rd[1]:
                    war.add(t)
        deps = set()
        for t in raw:
            if t[0] == "e" and t[1] == eng and eng == "pe":
                continue
            deps.add(t)
        for t in war:
            if t[0] == "e" and t[1] == eng:
                continue
            deps.add(t)
        return deps

    def _commit(self, tok, reads, writes):
        for k in reads:
            rd = self.readers.setdefault(k, [{}, []])
            if tok[0] == "e":
                rd[0][tok[1]] = tok[2]
            else:
                rd[1].append(tok)
        for k in writes:
            self.lastw[k] = tok
            self.readers[k] = [{}, []]

    def _k(self, keys):
        return [k if k.startswith("g:") else self.kp + k for k in keys]

    def op(self, eng, fn, reads=(), writes=()):
        reads, writes = self._k(reads), self._k(writes)
        deps = self._deps(eng, reads, writes)
        idx = len(self.ops[eng])
        tok = ("e", eng, idx)
        self.ops[eng].append(dict(fn=fn, deps=deps, dma=None))
        self._commit(tok, reads, writes)
        return tok

    def dma(self, eng, out, in_, reads=(), writes=(), is_out=False, **kw):
        reads, writes = self._k(reads), self._k(writes)
        deps = self._deps(eng, reads, writes)
        k = self.dma_rr
        self.dma_rr = (self.dma_rr + 1) % self.NDS
        prev = self.dma_val[k]
        self.dma_val[k] = prev + 16
        tok = ("d", k, prev + 16)
        fn = lambda e, out=out, in_=in_, kw=kw: e.dma_start(out=out, in_=in_, **kw)
        self.ops[eng].append(dict(fn=fn, deps=deps, dma=(k, prev)))
        self._commit(tok, reads, writes)
        if is_out:
            self.out_tokens.append(tok)
        return tok

    def cc(self, kind, in_ap, out_ap, groups, reads=(), writes=()):
        deps = self._deps("pool", reads, writes)
        k = self.dma_rr
        self.dma_rr = (self.dma_rr + 1) % self.NDS
        prev = self.dma_val[k]
        self.dma_val[k] = prev + 16
        tok = ("d", k, prev + 16)
        fn = lambda e: e.collective_compute(kind, ALU.bypass, groups, [in_ap], [out_ap])
        self.ops["pool"].append(dict(fn=fn, deps=deps, dma=(k, prev)))
        self._commit(tok, reads, writes)
        return tok

    def emit(self):
        nc = self.nc
        self.ops["sp"].append(dict(fn=None, deps=set(self.out_tokens), dma=None))
        ms = {e: set() for e in ENGS}
        for e in ENGS:
            for o in self.ops[e]:
                for t in o["deps"]:
                    if t[0] == "e":
                        ms[t[1]].add(t[2])
        rank = {e: {idx: i + 1 for i, idx in enumerate(sorted(ms[e]))} for e in ENGS}
        esem = {e: self.es.enter_context(nc.semaphore(f"s_{e}")) for e in ENGS}
        dsem = [self.es.enter_context(nc.semaphore(f"d_{i}")) for i in range(self.NDS)]
        ops = self.ops

        def run(e, eng):
            seen = {}
            for idx, o in enumerate(ops[e]):
                need = {}
                for t in o["deps"]:
                    if t[0] == "e":
                        key = ("e", t[1])
                        c = rank[t[1]][t[2]]
                    else:
                        key = ("d", t[1])
                        c = t[2]
                    if c > need.get(key, 0):
                        need[key] = c
                if o["dma"] is not None:
                    k, prev = o["dma"]
                    if prev > need.get(("d", k), 0):
                        need[("d", k)] = prev
                for key, c in need.items():
                    if c > seen.get(key, 0):
                        s = esem[key[1]] if key[0] == "e" else dsem[key[1]]
                        eng.wait_ge(s, c)
                        seen[key] = c
                        if DBG: print(f"[{e}] #{idx} wait {key} >= {c}")
                if o["fn"] is None:
                    continue
                ins = o["fn"](eng)
                if DBG: print(f"[{e}] #{idx} issue dma={o['dma']} inc={idx in ms[e]} rank={rank[e].get(idx)}")
                if o["dma"] is not None:
                    ins.then_inc(dsem[o["dma"][0]], 16)
                elif idx in ms[e]:
                    ins.then_inc(esem[e], 1)

        with nc.Block() as block:
            @block.tensor
            def _(eng):
                run("pe", eng)

            @block.scalar
            def _(eng):
                run("act", eng)

            @block.vector
            def _(eng):
                run("dve", eng)

            @block.gpsimd
            def _(eng):
                run("pool", eng)

            @block.sync
            def _(eng):
                run("sp", eng)
        self.es.close()
        return nc

    def mm(self, out, lhsT, rhs, start=True, stop=True, reads=(), writes=()):
        return self.op("pe", lambda e: e.matmul(out, lhsT, rhs, start=start, stop=stop), reads, writes)

    def tr(self, out, in_, ident, reads=(), writes=()):
        return self.op("pe", lambda e: e.transpose(out, in_, ident), reads, writes)

    def act(self, out, in_, func, reads=(), writes=(), **kw):
        return self.op("act", lambda e: e.activation(out, in_, func, **kw), reads, writes)


EPS = 1e-6


def norm_rows(p, x_ap_fn, nwbc, ident, hT_sb, ntiles, D, tag, xt_keys=None):
    pass


def build_norm(NT=2048, D=1024):
    p = Prog("norm")
    norm_phase(p, NT, D)
    return p.emit()


def norm_phase(p, NT=2048, D=1024):
    x = p.dram("x", [NT, D], F32)
    nw = p.dram("nw", [D], F32)
    ident_d = p.dram("ident", [128, 128], BF16)
    hT = p.dram("hT", [D, NT], BF16, kind="ExternalOutput")
    nch = D // 128
    ntiles = NT // 128
    xt = [p.sb([128, D], F32) for _ in range(2)]
    junk = p.sb([128, D], BF16)
    hb = [p.sb([128, D], BF16) for _ in range(2)]
    nwbc = p.sb([128, D], F32)
    ident = p.sb([128, 128], BF16)
    ss = [p.sb([128, 1], F32) for _ in range(2)]
    rs = [p.sb([128, 1], F32) for _ in range(2)]
    hT_sb = p.sb([128, nch, NT], BF16)
    pst = [p.ps([128, nch * 128], BF16) for _ in range(2)]

    p.dma("sp", nwbc[:], nw.partition_broadcast(128), writes=["nwbc"])
    p.dma("sp", ident[:], ident_d, writes=["ident"])
    for i in range(ntiles):
        s = i % 2
        p.dma("sp", xt[s][:], x[i * 128:(i + 1) * 128, :], writes=[f"xt{s}"])
        p.act(junk[:], xt[s][:], AF.Square, accum_out=ss[s][:], reads=[f"xt{s}"], writes=["junk", f"ss{s}"])
        p.op("dve", lambda e, s=s: e.tensor_scalar(rs[s][:], ss[s][:], 1.0 / D, EPS, ALU.mult, ALU.add),
             reads=[f"ss{s}"], writes=[f"rs{s}"])
        p.act(rs[s][:], rs[s][:], AF.Sqrt, reads=[f"rs{s}"], writes=[f"rs{s}"])
        p.op("dve", lambda e, s=s: e.reciprocal(rs[s][:], rs[s][:]), reads=[f"rs{s}"], writes=[f"rs{s}"])
        p.op("dve", lambda e, s=s: e.scalar_tensor_tensor(hb[s][:], xt[s][:], rs[s][:, 0:1], nwbc[:], ALU.mult, ALU.mult),
             reads=[f"xt{s}", f"rs{s}", "nwbc"], writes=[f"hb{s}"])
        for c in range(nch):
            p.tr(pst[s][:, c * 128:(c + 1) * 128], hb[s][:, c * 128:(c + 1) * 128], ident[:],
                 reads=[f"hb{s}", "ident"], writes=[f"pst{s}"])
        p.op("act", lambda e, s=s, i=i: e.copy(hT_sb[:, :, i * 128:(i + 1) * 128],
                                                 pst[s][:].rearrange("p (c t) -> p c t", c=nch)),
             reads=[f"pst{s}"], writes=["hT_sb"])
    p.dma("sp", hT.rearrange("(c p) t -> p c t", p=128), hT_sb[:], reads=["hT_sb"], is_out=True)


import os
VAR = os.environ.get('VAR', '')

S = 8192
D = 1024
NCH = 8
TCH = 512
NTC = S // TCH
NT = S // 128
NEG = -30000.0
MOBA_W = 1792
MOBA_DMIN = -384
SWA_W = 1024


def t5_bucket(d):
    d = np.asarray(d)
    dd = np.maximum(d, 1).astype(np.float32)
    large = 16 + (np.log(dd / np.float32(16)) / np.float32(np.log(1024 / 16)) * np.float32(16)).astype(np.int32)
    large = np.minimum(large, 31)
    return np.where(d < 16, np.maximum(d, 0), large)


def attn_consts(mode):
    c = {}
    c["ident"] = np.eye(128, dtype=NPBF)
    c["identf"] = np.eye(128, dtype=np.float32)
    if mode == "fox":
        c["tri"] = np.triu(np.ones((128, 128), np.float32))
        k = np.arange(128)[:, None]
        q = np.arange(512)[None, :]
        m = np.stack([np.where(q >= k + 128 * d, 0.0, NEG) for d in range(4)], 1)
        c["maskT"] = m.astype(NPBF)
    elif mode == "swa":
        c["jrev"] = np.eye(128, dtype=NPBF)[::-1].copy()
        n = SWA_W + 127
        dist = np.arange(n) - 127 + MOBA_DMIN
        oh = np.zeros((33, n), np.float32)
        b = t5_bucket(dist)
        for i in range(n):
            if 0 <= dist[i] < 128:
                oh[b[i], i] = 1.0
            else:
                oh[32, i] = NEG
        c["oh"] = oh
        kind = np.zeros((2, S), np.float32)
        kind[0, :] = 1.0
        c["kind"] = kind.astype(NPBF)
        vs = np.zeros((2, 2, 128), np.float32)
        vs[0, 0, 64:128] = 1.0
        vs[1, 1, 0:64] = 1.0
        c["vsink"] = vs.astype(NPBF)
    else:
        c["jrev"] = np.eye(128, dtype=NPBF)[::-1].copy()
        n = MOBA_W + 127
        dist = np.arange(n) - 127 + MOBA_DMIN
        oh = np.zeros((33, n), np.float32)
        b = t5_bucket(dist)
        for i in range(n):
            if dist[i] >= 0:
                oh[b[i], i] += 1.0
                oh[31, i] -= 1.0
            else:
                oh[32, i] = NEG
        c["oh"] = oh
        ind = np.zeros((33, S), np.float32)
        for j in range(32):
            ind[j, j * 256:(j + 1) * 256] = 1.0
        ind[32, :] = 1.0
        c["kind"] = ind.astype(NPBF)
    return c


def build_attn(mode, stage=9):
    p = Prog(mode)
    attn_phase(p, mode)
    return p.emit()


def attn_phase(p, mode, stage=9):
    fox = mode == "fox"
    swa = mode == "swa"
    UW = SWA_W if swa else MOBA_W
    KIN = 2 if swa else 33
    NTOK = 386 if fox else 384
    hT = p.dram("hT", [D, S], BF16)
    wqk = p.dram("wqk", [D, 256], F32)
    wtok = p.dram("wtok", [D, NTOK], F32)
    ident_d = p.dram("ident", [128, 128], BF16)
    identf_d = p.dram("identf", [128, 128], F32)
    yT = p.dram("yT", [128, S], F32, kind="ExternalOutput")
    if fox:
        fb_d = p.dram("fb", [2], F32)
        tri_d = p.dram("tri", [128, 128], F32)
        maskT_d = p.dram("maskT", [128, 4, 512], BF16)
    else:
        jrev_d = p.dram("jrev", [128, 128], BF16)
        oh_d = p.dram("oh", [33, UW + 127], F32)
        tab_d = p.dram("tab", [33, 2], F32)
        kind_d = p.dram("kind", [KIN, S], BF16)
        vec_d = p.dram("vecscr", [2, UW + 127], F32, kind="Internal")
        if swa:
            ksink_d = p.dram("ksink", [66, 2], F32)

    hT_v = hT.rearrange("(c p) t -> p c t", p=128)
    hc = [p.sb([128, NCH, TCH], BF16) for _ in range(2)]
    wqk_sb = p.sb([128, NCH, 256], BF16)
    wtok_sb = p.sb([128, NCH, NTOK], BF16)
    ident = p.sb([128, 128], BF16)
    identf = p.sb([128, 128], F32)
    QTa = [p.sb([128, S], BF16) for _ in range(2)]
    KTa = [p.sb([128, S], BF16) for _ in range(2)]
    V = p.sb([128, NT, 2, 128], BF16)
    nrm = p.sb([128, NT, 4], F32)
    sq = [p.sb([128, 256], F32) for _ in range(2)]
    NA = 96 if fox else (2 if swa else 33)
    A = [p.sb([128, NT, NA], BF16) for _ in range(2)]
    pT = [p.sb([128, TCH], BF16) for _ in range(3)]
    rl = [p.sb([128, TCH], F32) for _ in range(2)]
    yst = [p.sb([128, TCH], F32) for _ in range(2)]
    ones_f = p.sb([128, 128], F32)
    sbnd = p.sb([128, NT, 2], F32)
    small = p.sb([128, 64], F32)
    psA = [p.ps([128, 512], F32) for _ in range(2)]
    psO = [p.ps([128, 512], F32) for _ in range(2)]
    psT = [p.ps([128, 512], F32) for _ in range(2)]
    psB = p.ps([128, 1024], BF16)
    psM = p.ps([128, 512], F32)

    p.dma("sp", ident[:], ident_d, writes=["ident"])
    p.dma("sp", identf[:], identf_d, writes=["identf"])
    stg = p.sb([128, NCH, NTOK], F32)
    p.dma("sp", stg[:, :, 0:256], wqk.rearrange("(c p) n -> p c n", p=128), writes=["stg"])
    p.op("pool", lambda e: e.tensor_copy(wqk_sb[:], stg[:, :, 0:256]), reads=["stg"], writes=["wqk"])
    p.dma("sp", stg[:], wtok.rearrange("(c p) n -> p c n", p=128), writes=["stg"])
    p.op("pool", lambda e: e.tensor_copy(wtok_sb[:], stg[:]), reads=["stg"], writes=["wtok"])
    p.op("pool", lambda e: e.memset(V[:], 1.0), writes=["V"])
    p.op("pool", lambda e: e.memset(ones_f[:], 1.0), writes=["ones_f"])
    for h in range(2):
        p.op("pool", lambda e, h=h: e.memset(A[h][:], 0.0), writes=[f"A{h}"])
    if fox:
        f_sb = p.sb([128, NT, 2], F32)
        tri = p.sb([128, 128], F32)
        maskT = p.sb([128, 4, 512], BF16)
        fbb = p.sb([128, 2], F32)
        p.dma("sp", tri[:], tri_d, writes=["tri"])
        p.dma("sp", maskT[:], maskT_d, writes=["maskT"])
        p.dma("sp", fbb[:], fb_d.partition_broadcast(128), writes=["fbb"])
    else:
        jrev = p.sb([128, 128], BF16)
        U = [p.sb([128, UW], BF16) for _ in range(2)]
        oh = p.sb([33, UW + 127], F32)
        tab = p.sb([33, 2], F32)
        vec_sb = p.sb([2, UW + 127], F32)
        if swa:
            ksf = p.sb([66, 2], F32)
            ksink = p.sb([66, 2], BF16)
            vsink = p.sb([2, 2, 128], BF16)
            psink = p.sb([2, TCH], BF16)
            vsink_d = p.dram("vsink", [2, 2, 128], BF16)
            p.dma("sp", vsink[:], vsink_d, writes=["vsink"])
            p.dma("sp", ksf[:], ksink_d, writes=["ksf"])
            p.op("dve", lambda e: e.tensor_copy(ksink[:], ksf[:]), reads=["ksf"], writes=["ksink"])

        kmT = [p.sb([64, 32], F32) for _ in range(2)]
        qf = [p.sb([64, TCH], F32) for _ in range(2)]
        gsb = p.sb([128, 32], F32)
        top8 = p.sb([128, 8], F32)
        mb01 = p.sb([128, 32], F32)
        p.dma("sp", jrev[:], jrev_d, writes=["jrev"])
        p.dma("sp", oh[:], oh_d, writes=["oh"])
        p.dma("sp", tab[:], tab_d, writes=["tab"])
        for h in range(2):
            p.dma("sp", KTa[h][64:64 + KIN, :], kind_d, writes=[f"KTa{h}"])
            p.op("pool", lambda e, h=h: e.memset(kmT[h][:], 0.0), writes=[f"kmT{h}"])
        W = UW + 127
        for c0 in range(0, W, 512):
            c1 = min(W, c0 + 512)
            p.mm(psM[0:2, 0:c1 - c0], tab[:, :], oh[:, c0:c1], reads=["tab", "oh"], writes=["psM"])
            p.op("dve", lambda e, c0=c0, c1=c1: e.tensor_copy(vec_sb[:, c0:c1], psM[0:2, 0:c1 - c0]),
                 reads=["psM"], writes=["vec_sb"])
        p.dma("sp", vec_d, vec_sb[:], reads=["vec_sb"], writes=["vec_d"])
        for h in range(2):
            src = bass.AP(vec_d.tensor, h * W, [[1, 128], [1, UW]])
            stgf = stg[:].rearrange("p c n -> p (c n)")
            p.dma("sp", stgf[:, 0:UW], src, reads=["vec_d"], writes=["stg"])
            p.op("pool", lambda e, h=h, stgf=stgf: e.tensor_copy(U[h][:], stgf[:, 0:UW]), reads=["stg"], writes=[f"U{h}"])

    for tc in range(NTC):
        s = tc % 2
        p.dma("sp", hc[s][:], hT_v[:, :, tc * TCH:(tc + 1) * TCH], writes=[f"hc{s}"])
        for g in range(4):
            if VAR in ('B', 'C', 'D'):
                break
            ps = psA[g % 2]
            for c in range(NCH):
                p.mm(ps[0:64, :], wqk_sb[:, c, g * 64:(g + 1) * 64], hc[s][:, c, :], start=(c == 0), stop=(c == NCH - 1),
                     reads=[f"hc{s}", "wqk"], writes=[f"psA{g % 2}"])
            h = g % 2
            cs = slice(tc * TCH, (tc + 1) * TCH)
            if g < 2:
                p.op("act", lambda e, ps=ps, h=h, cs=cs: e.mul(QTa[h][0:64, cs], ps[0:64, :], 0.125),
                     reads=[f"psA{g % 2}"], writes=[f"QTa{h}"])
                if not fox and not swa:
                    p.op("dve", lambda e, ps=ps, h=h: e.tensor_copy(qf[h][:], ps[0:64, :]),
                         reads=[f"psA{g % 2}"], writes=[f"qf{h}"])
            else:
                p.op("dve", lambda e, ps=ps, h=h, cs=cs: e.tensor_copy(KTa[h][0:64, cs], ps[0:64, :]),
                     reads=[f"psA{g % 2}"], writes=[f"KTa{h}"])
                if not fox and not swa:
                    p.op("dve", lambda e, ps=ps, h=h, tc=tc: e.tensor_reduce(
                        kmT[h][:, 2 * tc:2 * tc + 2], ps[0:64, :].rearrange("p (b t) -> p b t", b=2), AX.X, ALU.add),
                        reads=[f"psA{g % 2}"], writes=[f"kmT{h}"])
        for j in range(4):
            if VAR == 'A':
                break
            i = tc * 4 + j
            ps = psT[j % 2]
            for c in range(NCH):
                p.mm(ps[:, 0:NTOK], hc[s][:, c, j * 128:(j + 1) * 128], wtok_sb[:, c, :], start=(c == 0), stop=(c == NCH - 1),
                     reads=[f"hc{s}", "wtok"], writes=[f"psT{j % 2}"])
            if os.environ.get("EXP") == "8":
                p.act(sq[j % 2][:], ps[:, 128:384], AF.Square, reads=[f"psT{j % 2}"], writes=[f"sq{j % 2}"])
                continue
            p.op("dve", lambda e, ps=ps, i=i: e.tensor_copy(V[:, i, 0, 0:64], ps[:, 0:64]),
                 reads=[f"psT{j % 2}"], writes=["V"])
            p.op("dve", lambda e, ps=ps, i=i: e.tensor_copy(V[:, i, 1, 64:128], ps[:, 64:128]),
                 reads=[f"psT{j % 2}"], writes=["V"])
            if VAR == 'B':
                continue
            EXP = os.environ.get("EXP", "")
            if EXP == "1":
                p.act(sq[j % 2][:], ps[:, 0:256], AF.Square, reads=[f"psT{j % 2}"], writes=[f"sq{j % 2}"])
            elif EXP == "3":
                p.act(sq[j % 2][:], ps[:, 128:384], AF.Square, reads=[f"psT{j % 2}", "V"], writes=[f"sq{j % 2}"])
            elif EXP == "5":
                p.act(sq[j % 2][:], ps[:, 128:384], AF.Square, reads=[f"psT{j % 2}"], writes=[f"sqx{i}"])
            elif EXP == "6":
                p.op("act", lambda e, ps=ps, j=j: e.mul(sq[j % 2][:], ps[:, 128:384], 1.0), reads=[f"psT{j % 2}"], writes=[f"sq{j % 2}"])
            elif EXP == "7":
                p.op("act", lambda e, ps=ps, j=j: e.mul(sq[j % 2][0:64, :], ps[0:64, 128:384], 1.0), reads=[f"psT{j % 2}"], writes=[f"sq{j % 2}"])
            elif EXP == "4":
                p.op("dve", lambda e, ps=ps, j=j: e.tensor_copy(sq[j % 2][:], ps[:, 128:384]), reads=[f"psT{j % 2}"], writes=[f"sq{j % 2}"])
            else:
                p.act(sq[j % 2][:], ps[:, 128:384], AF.Square, reads=[f"psT{j % 2}"], writes=[f"sq{j % 2}"])
            if VAR == 'C':
                continue
            p.op("dve", lambda e, i=i, j=j: e.tensor_reduce(nrm[:, i, :], sq[j % 2][:].rearrange("p (g d) -> p g d", d=64), AX.X, ALU.add),
                 reads=[f"sq{j % 2}"], writes=["nrm"])
            if fox:
                p.op("dve", lambda e, ps=ps, i=i: e.tensor_copy(f_sb[:, i, :], ps[:, 384:386]),
                     reads=[f"psT{j % 2}"], writes=["f_sb"])
        if not fox and not swa:
            for h in range(2):
                for j in range(4):
                    i = tc * 4 + j
                    n = i // 2
                    p.mm(psM[:, 0:32], qf[h][:, j * 128:(j + 1) * 128], kmT[h][:, :], reads=[f"qf{h}", f"kmT{h}"], writes=["psM"])
                    p.op("dve", lambda e: e.tensor_copy(gsb[:], psM[:, 0:32]), reads=["psM"], writes=["gsb"])
                    p.op("dve", lambda e, n=n: e.memset(gsb[:, n:32], -1e30), writes=["gsb"])
                    p.op("dve", lambda e: e.max(top8[:], gsb[:]), reads=["gsb"], writes=["top8"])
                    p.op("dve", lambda e: e.tensor_scalar(mb01[:], gsb[:], top8[:, 2:3], 1.0, ALU.is_ge, ALU.subtract),
                         reads=["gsb", "top8"], writes=["mb01"])
                    p.op("dve", lambda e, h=h, i=i: e.tensor_scalar(A[h][:, i, 0:32], mb01[:], -NEG, None, ALU.mult),
                         reads=["mb01"], writes=[f"A{h}"])
                    p.op("dve", lambda e, h=h, i=i, n=n: e.memset(A[h][:, i, n:n + 1], 0.0), writes=[f"A{h}"])

    kmx = small[:, 0:2]
    p.op("dve", lambda e: e.tensor_reduce(kmx, nrm[:, :, 2:4].rearrange("p i g -> p g i"), AX.X, ALU.max),
         reads=["nrm"], writes=["small"])
    p.tr(psM[0:2, 0:128], kmx, identf[:], reads=["small", "identf"], writes=["psM"])
    kmx2 = small[0:2, 8:9]
    p.op("dve", lambda e: e.tensor_reduce(kmx2, psM[0:2, 0:128], AX.X, ALU.max), reads=["psM"], writes=["small2"])
    kbr = small[0:2, 16:48]
    krow = p.sb([2, 128], F32)
    p.op("dve", lambda e: e.tensor_scalar(krow[:], ones_f[0:2, :], kmx2, None, ALU.mult),
         reads=["small2", "ones_f"], writes=["krow"])
    p.mm(psM[:, 0:2], krow[:], identf[0:2, 0:2], reads=["krow", "identf"], writes=["psM"])
    kbc = small[:, 4:6]
    p.op("dve", lambda e: e.tensor_copy(kbc, psM[:, 0:2]), reads=["psM"], writes=["kbc"])
    for h in range(2):
        p.op("dve", lambda e, h=h: e.tensor_scalar(sbnd[:, :, h], nrm[:, :, h], small[:, 4 + h:5 + h], None, ALU.mult),
             reads=["nrm", "kbc"], writes=["sbnd"])
    p.act(sbnd[:], sbnd[:], AF.Sqrt, reads=["sbnd"], writes=["sbnd"], scale=(0.125 * 1.02) ** 2)

    if fox:
        nfb = p.sb([128, 2], F32)
        e1 = p.sb([128, NT, 2], F32)
        g = p.sb([128, NT, 2], F32)
        incl = p.sb([128, NT, 2], F32)
        G = p.sb([128, NT, 2], F32)
        r1 = p.sb([128, NT, 2], F32)
        Gh = p.sb([128, NT, 2], BF16)
        Gm = p.sb([128, NT, 2], BF16)
        Gl = p.sb([128, NT, 2], BF16)
        p.op("dve", lambda e: e.tensor_scalar(nfb[:], fbb[:], -1.0, None, ALU.mult), reads=["fbb"], writes=["nfb"])
        for h in range(2):
            p.act(e1[:, :, h], f_sb[:, :, h], AF.Exp, reads=["f_sb", "nfb"], writes=["e1"], scale=-1.0, bias=nfb[:, h:h + 1])
        p.act(g[:], e1[:], AF.Ln, reads=["e1"], writes=["g"], bias=1.0)
        gf = g[:].rearrange("p i h -> p (i h)")
        p.mm(psM[:, 0:128], tri[:], gf, reads=["tri", "g"], writes=["psM"])
        p.mm(psM[:, 128:256], ones_f[:], gf, reads=["ones_f", "g"], writes=["psM"])
        W_v = psM[:, 0:128].rearrange("p (i h) -> p i h", h=2)
        B_v = psM[:, 128:256].rearrange("p (i h) -> p i h", h=2)
        for h in range(2):
            p.op("dve", lambda e, h=h: e.tensor_tensor_scan(incl[:, :, h], ones_f[:, 0:NT], B_v[:, :, h], 0.0, ALU.mult, ALU.add),
                 reads=["psM", "ones_f"], writes=["incl"])
        p.op("dve", lambda e: e.tensor_tensor(r1[:], incl[:], B_v, ALU.subtract), reads=["incl", "psM"], writes=["r1"])
        p.op("dve", lambda e: e.tensor_tensor(G[:], r1[:], W_v, ALU.add), reads=["r1", "psM"], writes=["G"])
        p.op("dve", lambda e: e.tensor_copy(Gh[:], G[:]), reads=["G"], writes=["Gh"])
        p.op("dve", lambda e: e.tensor_tensor(r1[:], G[:], Gh[:], ALU.subtract), reads=["G", "Gh"], writes=["r1"])
        p.op("dve", lambda e: e.tensor_copy(Gm[:], r1[:]), reads=["r1"], writes=["Gm"])
        p.op("dve", lambda e: e.tensor_tensor(r1[:], r1[:], Gm[:], ALU.subtract), reads=["r1", "Gm"], writes=["r1"])
        p.op("dve", lambda e: e.tensor_copy(Gl[:], r1[:]), reads=["r1"], writes=["Gl"])
        for h in range(2):
            Ah = A[h]
            for col in (0, 1, 2, 35, 36, 37, 38):
                p.op("pool", lambda e, Ah=Ah, col=col: e.memset(Ah[:, :, col:col + 1], 1.0), writes=[f"A{h}"])
            for k3, Gx in enumerate((Gh, Gm, Gl)):
                p.op("dve", lambda e, Ah=Ah, Gx=Gx, k3=k3, h=h: e.tensor_scalar(Ah[:, :, 3 + k3], Gx[:, :, h], -1.0, None, ALU.mult),
                     reads=["Gh", "Gm", "Gl"], writes=[f"A{h}"])
                p.op("dve", lambda e, Ah=Ah, Gx=Gx, k3=k3, h=h: e.tensor_copy(Ah[:, :, 32 + k3], Gx[:, :, h]),
                     reads=["Gh", "Gm", "Gl"], writes=[f"A{h}"])
            p.op("dve", lambda e, Ah=Ah, h=h: e.tensor_scalar(Ah[:, :, 6], sbnd[:, :, h], -1.0, None, ALU.mult),
                 reads=["sbnd"], writes=[f"A{h}"])
    elif swa:
        for h in range(2):
            p.op("dve", lambda e, h=h: e.tensor_scalar(A[h][:, :, 0], sbnd[:, :, h], -1.0, None, ALU.mult),
                 reads=["sbnd"], writes=[f"A{h}"])
            p.op("dve", lambda e, h=h: e.memset(A[h][:, :, 1:2], 1.0), writes=[f"A{h}"])
    else:
        for h in range(2):
            p.op("dve", lambda e, h=h: e.tensor_scalar(A[h][:, :, 32], sbnd[:, :, h], -1.0, None, ALU.mult),
                 reads=["sbnd"], writes=[f"A{h}"])

    for h in range(2):
        for g8 in range(NT // 8):
            for t in range(8):
                i = g8 * 8 + t
                p.tr(psB[0:NA, t * 128:(t + 1) * 128], A[h][:, i, :], ident[:], reads=[f"A{h}", "ident"], writes=["psB"])
            cs = slice(g8 * 1024, (g8 + 1) * 1024)
            if fox:
                p.op("dve", lambda e, h=h, cs=cs: e.tensor_copy(QTa[h][64:71, cs], psB[0:7, :]), reads=["psB"], writes=[f"QTa{h}"])
                p.op("act", lambda e, h=h, cs=cs: e.copy(KTa[h][64:71, cs], psB[32:39, :]), reads=["psB"], writes=[f"KTa{h}"])
            else:
                p.op("dve", lambda e, h=h, cs=cs: e.tensor_copy(QTa[h][64:64 + NA, cs], psB[0:NA, :]), reads=["psB"], writes=[f"QTa{h}"])

    KR = 71 if fox else (66 if swa else 97)
    it = 0
    oi = 0
    for h in range(2):
        for qc in range(NTC):
            qs = slice(qc * TCH, (qc + 1) * TCH)
            nk = 4 * qc + 4
            po = psO[oi % 2]
            pok = f"psO{oi % 2}"
            kt0 = max(0, 4 * qc - 1) if swa else 0
            for kt in range(kt0, nk):
                pa = psA[it % 2]
                pak = f"psA{it % 2}"
                pt = pT[it % 3]
                ptk = f"pT{it % 3}"
                ks = slice(kt * 128, (kt + 1) * 128)
                d0 = qc * TCH - kt * 128
                if fox:
                    extra = kt >= 4 * qc
                else:
                    extra = swa or d0 <= 896
                p.mm(pa[:, :], KTa[h][0:KR, ks], QTa[h][0:KR, qs], start=True, stop=not extra,
                     reads=[f"KTa{h}", f"QTa{h}"], writes=[pak])
                if extra:
                    if fox:
                        p.mm(pa[:, :], ident[:], maskT[:, kt - 4 * qc, :], start=False, stop=True,
                             reads=["ident", "maskT"], writes=[pak])
                    else:
                        off = d0 - MOBA_DMIN
                        p.mm(pa[:, :], jrev[:], U[h][:, off:off + TCH], start=False, stop=True,
                             reads=["jrev", f"U{h}"], writes=[pak])
                p.act(pt[:], pa[:, :], AF.Exp, reads=[pak], writes=[ptk])
                p.mm(po[:, :], V[:, kt, h, :], pt[:], start=(kt == kt0), stop=(kt == nk - 1 and not swa),
                     reads=["V", ptk], writes=[pok])
                it += 1
            if swa:
                p.mm(psM[0:2, :], ksink[:, 0:2], QTa[h][0:66, qs], reads=["ksink", f"QTa{h}"], writes=["psM"])
                p.act(psink[:], psM[0:2, :], AF.Exp, reads=["psM"], writes=["psink"])
                p.mm(po[:, :], vsink[:, h, :], psink[:], start=False, stop=True, reads=["vsink", "psink"], writes=[pok])
            r = rl[oi % 2]
            y = yst[oi % 2]
            if h == 0:
                num, den = slice(0, 64), slice(64, 128)
            else:
                num, den = slice(64, 128), slice(0, 64)
            p.op("dve", lambda e, r=r, po=po, den=den: e.reciprocal(r[den, :], po[den, :]), reads=[pok], writes=[f"rl{oi % 2}"])
            p.op("dve", lambda e, r=r, po=po, y=y, num=num, den=den: e.tensor_tensor(y[num, :], po[num, :], r[den, :], ALU.mult),
                 reads=[pok, f"rl{oi % 2}"], writes=[f"yst{oi % 2}"])
            p.dma("sp", yT[num, qs], y[num, :], reads=[f"yst{oi % 2}"], is_out=True)
            oi += 1


def attn_inputs(mode, hT_b, w_in_l, j, extra):
    off_moba = 512 + 1024 + 8
    off_fox = off_moba + 1536
    off_f = off_fox + 1536
    base = off_fox if mode == "fox" else off_moba
    hs = [2 * j, 2 * j + 1]
    if mode == "swa":
        off_q = off_f + 8
        off_kv = off_q + 512
        kv = j // 2
        q = [w_in_l[:, off_q + h * 64: off_q + (h + 1) * 64] for h in hs]
        k = [w_in_l[:, off_kv + kv * 64: off_kv + (kv + 1) * 64]] * 2
        v = [w_in_l[:, off_kv + 128 + kv * 64: off_kv + 128 + (kv + 1) * 64]] * 2
    else:
        q = [w_in_l[:, base + h * 64: base + (h + 1) * 64] for h in hs]
        k = [w_in_l[:, base + 512 + h * 64: base + 512 + (h + 1) * 64] for h in hs]
        v = [w_in_l[:, base + 1024 + h * 64: base + 1024 + (h + 1) * 64] for h in hs]
    m = dict(attn_consts(mode))
    m["hT"] = hT_b
    m["wqk"] = np.ascontiguousarray(np.concatenate(q + k, axis=1))
    if mode == "fox":
        f = [w_in_l[:, off_f + h: off_f + h + 1] for h in hs]
        m["wtok"] = np.ascontiguousarray(np.concatenate(v + q + k + f, axis=1))
        m["fb"] = np.ascontiguousarray(extra["forget_bias"][hs])
    else:
        m["wtok"] = np.ascontiguousarray(np.concatenate(v + q + k, axis=1))
        tab = extra["rel_bias"][:, hs] if mode == "moba" else extra["rel_bias"][:, [8 + hh for hh in hs]]
        if mode == "swa":
            ks = np.zeros((66, 2), np.float32)
            ks[64, :] = 1.0
            ks[65, :] = extra["sinks"][hs]
            m["ksink"] = ks
        m["tab"] = np.ascontiguousarray(np.concatenate([tab, np.ones((1, 2), np.float32)], axis=0))
    return m


S = 8192
D = 1024
NCH = 8
TCH = 512
NTC = S // TCH
NEG = -30000.0


def ssd_consts():
    c = {}
    c["ident"] = np.eye(128, dtype=NPBF)
    c["identf"] = np.eye(128, dtype=np.float32)
    c["tri"] = np.triu(np.ones((128, 128), np.float32))
    s = np.arange(128)[:, None]
    l = np.arange(256)[None, :]
    c["m01"] = np.where(l >= s, 1.0, 0.0).astype(np.float32)
    return c


def build_ssd(nchunks=NTC):
    p = Prog("ssd")
    ssd_phase(p, nchunks)
    return p.emit()


def ssd_phase(p, nchunks=NTC):
    hT = p.dram("hT", [D, S], BF16)
    wfm = p.dram("wfm", [D, 512], F32)
    wdt = p.dram("wdt", [D, 2], F32)
    cw_d = p.dram("cw", [128, 3, 4], F32)
    cb_d = p.dram("cb", [128, 3], F32)
    dtb_d = p.dram("dtb", [2], F32)
    alog_d = p.dram("alog", [2], F32)
    dcol_d = p.dram("dcol", [128, 1], F32)
    ident_d = p.dram("ident", [128, 128], BF16)
    identf_d = p.dram("identf", [128, 128], F32)
    tri_d = p.dram("tri", [128, 128], F32)
    m01_d = p.dram("m01", [128, 256], F32)
    yT = p.dram("yT", [128, S], F32, kind="ExternalOutput")
    hT_v = hT.rearrange("(c p) t -> p c t", p=128)

    hc = [p.sb([128, NCH, TCH], BF16) for _ in range(2)]
    stg = p.sb([128, NCH, 512], F32)
    wfm_sb = p.sb([128, NCH, 512], BF16)
    wdt_f = p.sb([128, NCH, 2], F32)
    wdt_sb = p.sb([128, NCH, 2], BF16)
    cw = p.sb([128, 3, 4], F32)
    cb = p.sb([128, 3], F32)
    dcol = p.sb([128, 1], F32)
    ident = p.sb([128, 128], BF16)
    identf = p.sb([128, 128], F32)
    tri = p.sb([128, 128], F32)
    ones_f = p.sb([128, 128], F32)
    dtb4 = p.sb([128, 4, 2], F32)
    negA4 = p.sb([128, 4, 2], F32)
    ub = [[p.sb([128, 3 + TCH], F32) for _ in range(2)] for _ in range(3)]
    acc = p.sb([128, TCH], F32)
    zs = p.sb([128, TCH], F32)
    xT = p.sb([128, TCH], F32)
    xTb = p.sb([128, TCH], BF16)
    BT = p.sb([128, TCH], BF16)
    CT = p.sb([128, TCH], BF16)
    Btok = p.sb([128, 4, 128], BF16)
    xdtp = [p.sb([128, 4, 128], BF16) for _ in range(2)]
    xdd = p.sb([128, 4, 128], BF16)
    zt = p.sb([128, 4, 2], F32)
    dt = p.sb([128, 4, 2], F32)
    a = p.sb([128, 4, 2], F32)
    acum = p.sb([128, 4, 2], F32)
    last = p.sb([128, 2, 2], F32)
    dte = p.sb([128, 4, 2], F32)
    dd = p.sb([128, 4, 2], F32)
    cd = p.sb([128, 2, 2], F32)
    nac = p.sb([128, 4, 2], F32)
    dg = [p.sb([128, 128], F32) for _ in range(2)]
    Dc = [p.sb([128, 256], F32) for _ in range(2)]
    Gm = [p.sb([128, 256], F32) for _ in range(2)]
    m01 = p.sb([128, 256], F32)
    LT = [p.sb([128, 256], F32) for _ in range(2)]
    WT = [p.sb([128, 256], BF16) for _ in range(2)]
    Eb = p.sb([128, 256], F32)
    CdT = [p.sb([128, 256], BF16) for _ in range(2)]
    hst = p.sb([128, 128], F32)
    hpad = [p.sb([128, 128], BF16) for _ in range(2)]
    t1 = p.sb([128, 256], F32)
    yst = [p.sb([128, 256], F32) for _ in range(2)]

    ps = [p.ps([128, 512], F32) for _ in range(7)]
    psB = p.ps([128, 1024], BF16)
    K = lambda i: f"ps{i}"

    for t, d_, k in ((ident, ident_d, "ident"), (identf, identf_d, "identf"), (tri, tri_d, "tri"), (m01, m01_d, "m01"),
                     (cw, cw_d, "cw"), (cb, cb_d, "cb"), (dcol, dcol_d, "dcol")):
        p.dma("sp", t[:], d_, writes=[k])
    p.dma("sp", stg[:], wfm.rearrange("(c p) n -> p c n", p=128), writes=["stg"])
    p.op("pool", lambda e: e.tensor_copy(wfm_sb[:], stg[:]), reads=["stg"], writes=["wfm"])
    p.dma("sp", wdt_f[:], wdt.rearrange("(c p) n -> p c n", p=128), writes=["wdt_f"])
    p.op("pool", lambda e: e.tensor_copy(wdt_sb[:], wdt_f[:]), reads=["wdt_f"], writes=["wdt"])
    p.op("pool", lambda e: e.memset(ones_f[:], 1.0), writes=["ones_f"])
    p.op("pool", lambda e: e.memset(hst[:], 0.0), writes=["hst"])
    for h in range(2):
        p.op("pool", lambda e, h=h: e.memset(hpad[h][:], 0.0), writes=[f"hpad{h}"])
        p.op("pool", lambda e, h=h: e.memset(xdtp[h][:], 0.0), writes=[f"xdtp{h}"])
    for g in range(3):
        p.op("pool", lambda e, g=g: e.memset(ub[g][1][:, TCH:TCH + 3], 0.0), writes=[f"ub{g}_1"])
    for j in range(4):
        p.dma("sp", dtb4[:, j, :], dtb_d.partition_broadcast(128), writes=["dtb4"])
        p.dma("sp", negA4[:, j, :], alog_d.partition_broadcast(128), writes=["negA4"])
    p.act(negA4[:], negA4[:], AF.Exp, reads=["negA4"], writes=["negA4"])
    p.op("dve", lambda e: e.tensor_scalar(negA4[:], negA4[:], -1.0, None, ALU.mult), reads=["negA4"], writes=["negA4"])

    for tc in range(nchunks):
        s = tc % 2
        p.dma("sp", hc[s][:], hT_v[:, :, tc * TCH:(tc + 1) * TCH], writes=[f"hc{s}"])
        for g in range(4):
            pp = ps[g % 2]
            for c in range(NCH):
                p.mm(pp[:, :], wfm_sb[:, c, g * 128:(g + 1) * 128], hc[s][:, c, :], start=(c == 0), stop=(c == NCH - 1),
                     reads=[f"hc{s}", "wfm"], writes=[K(g % 2)])
            if g == 0:
                p.act(zs[:], pp[:, :], AF.Silu, reads=[K(0)], writes=["zs"])
            else:
                gi = g - 1
                u = ub[gi][s]
                uo = ub[gi][1 - s]
                p.op("dve", lambda e, u=u, uo=uo: e.tensor_copy(u[:, 0:3], uo[:, TCH:TCH + 3]),
                     reads=[f"ub{gi}_{1 - s}"], writes=[f"ub{gi}_{s}"])
                p.op("dve", lambda e, u=u, pp=pp: e.tensor_copy(u[:, 3:3 + TCH], pp[:, :]),
                     reads=[K(g % 2)], writes=[f"ub{gi}_{s}"])
                p.op("dve", lambda e, u=u, gi=gi: e.tensor_scalar(acc[:], u[:, 0:TCH], cw[:, gi, 0:1], None, ALU.mult),
                     reads=[f"ub{gi}_{s}", "cw"], writes=["acc"])
                for i in range(1, 4):
                    p.op("dve", lambda e, u=u, gi=gi, i=i: e.scalar_tensor_tensor(acc[:], u[:, i:i + TCH], cw[:, gi, i:i + 1], acc[:], ALU.mult, ALU.add),
                         reads=[f"ub{gi}_{s}", "cw", "acc"], writes=["acc"])
                dst, dk = ((xT, "xT"), (BT, "BT"), (CT, "CT"))[gi]
                p.act(dst[:], acc[:], AF.Silu, reads=["acc", "cb"], writes=[dk], bias=cb[:, gi:gi + 1])
                if gi == 0:
                    p.op("pool", lambda e: e.tensor_copy(xTb[:], xT[:]), reads=["xT"], writes=["xTb"])
        for j in range(4):
            for c in range(NCH):
                p.mm(ps[6][:, 2 * j:2 * j + 2], hc[s][:, c, j * 128:(j + 1) * 128], wdt_sb[:, c, :], start=(c == 0), stop=(c == NCH - 1),
                     reads=[f"hc{s}", "wdt"], writes=[K(6)])
        p.op("dve", lambda e: e.tensor_tensor(zt[:], ps[6][:, 0:8].rearrange("p (j h) -> p j h", h=2), dtb4[:], ALU.add),
             reads=[K(6), "dtb4"], writes=["zt"])
        p.act(zt[:], zt[:], AF.Exp, reads=["zt"], writes=["zt"])
        p.act(dt[:], zt[:], AF.Ln, reads=["zt"], writes=["dt"], bias=1.0)
        p.op("dve", lambda e: e.tensor_tensor(a[:], dt[:], negA4[:], ALU.mult), reads=["dt", "negA4"], writes=["a"])
        af = a[:].rearrange("p j h -> p (j h)")
        p.mm(ps[6][:, 16:24], tri[:], af, reads=["tri", "a"], writes=[K(6)])
        p.mm(ps[6][:, 32:40], ones_f[:], af, reads=["ones_f", "a"], writes=[K(6)])
        Wv = ps[6][:, 16:24].rearrange("p (j h) -> p j h", h=2)
        Bv = ps[6][:, 32:40].rearrange("p (c a h) -> p c a h", a=2, h=2)
        p.op("dve", lambda e: e.tensor_copy(acum[:], Wv), reads=[K(6)], writes=["acum"])
        acv = acum[:].rearrange("p (c a) h -> p c a h", a=2)
        p.op("dve", lambda e: e.tensor_tensor(acv[:, :, 1, :], acv[:, :, 1, :], Bv[:, :, 0, :], ALU.add),
             reads=["acum", K(6)], writes=["acum"])
        p.op("dve", lambda e: e.tensor_copy(last[:], Bv[:, :, 0, :]), reads=[K(6)], writes=["last"])
        p.op("dve", lambda e: e.tensor_tensor(last[:], last[:], Bv[:, :, 1, :], ALU.add), reads=["last", K(6)], writes=["last"])
        for ch in range(2):
            for a2 in range(2):
                p.op("dve", lambda e, ch=ch, a2=a2: e.tensor_tensor(dd[:, 2 * ch + a2, :], last[:, ch, :], acum[:, 2 * ch + a2, :], ALU.subtract),
                     reads=["last", "acum"], writes=["dd"])
        p.act(dte[:], dd[:], AF.Exp, reads=["dd"], writes=["dte"])
        p.act(cd[:], last[:], AF.Exp, reads=["last"], writes=["cd"])
        p.op("dve", lambda e: e.tensor_tensor(dte[:], dte[:], dt[:], ALU.mult), reads=["dte", "dt"], writes=["dte"])
        p.op("dve", lambda e: e.tensor_scalar(nac[:], acum[:], -1.0, None, ALU.mult), reads=["acum"], writes=["nac"])
        for j in range(4):
            p.tr(psB[:, j * 128:(j + 1) * 128], xTb[:, j * 128:(j + 1) * 128], ident[:], reads=["xTb", "ident"], writes=["psB"])
            p.tr(psB[:, 512 + j * 128:512 + (j + 1) * 128], BT[:, j * 128:(j + 1) * 128], ident[:], reads=["BT", "ident"], writes=["psB"])
        p.op("act", lambda e: e.copy(Btok[:].rearrange("p j n -> p (j n)"), psB[:, 512:1024]), reads=["psB"], writes=["Btok"])
        for j in range(4):
            for h in range(2):
                hs = slice(h * 64, (h + 1) * 64)
                p.op("dve", lambda e, j=j, h=h, hs=hs: e.tensor_scalar(xdtp[h][:, j, hs], psB[:, j * 128 + h * 64:j * 128 + (h + 1) * 64],
                                                                       dt[:, j, h:h + 1], None, ALU.mult),
                     reads=["psB", "dt"], writes=[f"xdtp{h}"])
                p.op("dve", lambda e, j=j, h=h, hs=hs: e.tensor_scalar(xdd[:, j, hs], psB[:, j * 128 + h * 64:j * 128 + (h + 1) * 64],
                                                                       dte[:, j, h:h + 1], None, ALU.mult),
                     reads=["psB", "dte"], writes=["xdd"])
        for ch in range(2):
            c0 = ch * 256
            p.mm(ps[2][:, 0:256], BT[:, c0:c0 + 128], CT[:, c0:c0 + 256], reads=["BT", "CT"], writes=[K(2)])
            p.mm(ps[3][:, 0:128], BT[:, c0 + 128:c0 + 256], CT[:, c0 + 128:c0 + 256], reads=["BT", "CT"], writes=[K(3)])
            p.op("dve", lambda e: e.tensor_tensor(Gm[0][:, 0:256], ps[2][:, 0:256], m01[:, 0:256], ALU.mult),
                 reads=[K(2), "m01"], writes=["Gm0"])
            p.op("dve", lambda e: e.tensor_tensor(Gm[1][:, 0:128], ps[3][:, 0:128], m01[:, 0:128], ALU.mult),
                 reads=[K(3), "m01"], writes=["Gm1"])
            for h in range(2):
                for a2 in range(2):
                    j = 2 * ch + a2
                    p.op("dve", lambda e, a2=a2, j=j, h=h: e.tensor_scalar(dg[a2][:], identf[:], acum[:, j, h:h + 1], None, ALU.mult),
                         reads=["identf", "acum"], writes=[f"dg{a2}"])
                    p.mm(ps[5][:, a2 * 128:(a2 + 1) * 128], ones_f[:], dg[a2][:], reads=["ones_f", f"dg{a2}"], writes=[K(5)])
                j0, j1 = 2 * ch, 2 * ch + 1
                p.op("dve", lambda e, j0=j0, h=h: e.tensor_scalar(Dc[0][:, 0:256], ps[5][:, 0:256], nac[:, j0, h:h + 1], 0.0, ALU.add, ALU.min),
                     reads=[K(5), "nac"], writes=["Dc0"])
                p.op("dve", lambda e, j1=j1, h=h: e.tensor_scalar(Dc[1][:, 0:128], ps[5][:, 128:256], nac[:, j1, h:h + 1], 0.0, ALU.add, ALU.min),
                     reads=[K(5), "nac"], writes=["Dc1"])
                p.act(LT[0][:, 0:256], Dc[0][:, 0:256], AF.Exp, reads=["Dc0"], writes=["LT0"])
                p.act(LT[1][:, 0:128], Dc[1][:, 0:128], AF.Exp, reads=["Dc1"], writes=["LT1"])
                p.act(Eb[:], ps[5][:, 0:256], AF.Exp, reads=[K(5)], writes=["Eb"])
                p.op("dve", lambda e: e.tensor_tensor(WT[0][:, 0:256], Gm[0][:, 0:256], LT[0][:, 0:256], ALU.mult),
                     reads=["Gm0", "LT0"], writes=["WT0"])
                p.op("dve", lambda e: e.tensor_tensor(WT[1][:, 0:128], Gm[1][:, 0:128], LT[1][:, 0:128], ALU.mult),
                     reads=["Gm1", "LT1"], writes=["WT1"])
                p.op("dve", lambda e, h=h, c0=c0: e.tensor_tensor(CdT[h][:], CT[:, c0:c0 + 256], Eb[:], ALU.mult),
                     reads=["CT", "Eb"], writes=[f"CdT{h}"])
                p.mm(ps[4][:, 0:256], hpad[h][:], CdT[h][:], start=(h == 0), stop=False,
                     reads=[f"hpad{h}", f"CdT{h}"], writes=[K(4)])
                p.mm(ps[4][:, 0:256], xdtp[h][:, j0, :], WT[0][:, 0:256], start=False, stop=False,
                     reads=[f"xdtp{h}", "WT0"], writes=[K(4)])
                p.mm(ps[4][:, 128:256], xdtp[h][:, j1, :], WT[1][:, 0:128], start=False, stop=(h == 1),
                     reads=[f"xdtp{h}", "WT1"], writes=[K(4)])
            p.mm(ps[6][:, 128:256], Btok[:, j0, :], xdd[:, j0, :], start=True, stop=False, reads=["Btok", "xdd"], writes=[K(6)])
            p.mm(ps[6][:, 128:256], Btok[:, j1, :], xdd[:, j1, :], start=False, stop=True, reads=["Btok", "xdd"], writes=[K(6)])
            for h in range(2):
                hs = slice(h * 64, (h + 1) * 64)
                p.op("dve", lambda e, h=h, hs=hs, ch=ch: e.scalar_tensor_tensor(hst[:, hs], hst[:, hs], cd[:, ch, h:h + 1],
                                                                               ps[6][:, 128 + h * 64:128 + (h + 1) * 64], ALU.mult, ALU.add),
                     reads=["hst", "cd", K(6)], writes=["hst"])
                p.op("dve", lambda e, h=h, hs=hs: e.tensor_copy(hpad[h][:, hs], hst[:, hs]), reads=["hst"], writes=[f"hpad{h}"])
            o = yst[ch]
            p.op("dve", lambda e, c0=c0: e.scalar_tensor_tensor(t1[:], xT[:, c0:c0 + 256], dcol[:, 0:1], ps[4][:, 0:256], ALU.mult, ALU.add),
                 reads=["xT", "dcol", K(4)], writes=["t1"])
            p.op("dve", lambda e, o=o, c0=c0: e.tensor_tensor(o[:], t1[:], zs[:, c0:c0 + 256], ALU.mult),
                 reads=["t1", "zs"], writes=[f"yst{ch}"])
            p.dma("sp", yT[:, tc * TCH + c0:tc * TCH + c0 + 256], o[:], reads=[f"yst{ch}"], is_out=True)


def ssd_inputs(hT_b, inp, l, j):
    w_in_l = inp["w_in"][l]
    g = j // 2
    hs = [2 * j, 2 * j + 1]
    cz = slice(j * 128, (j + 1) * 128)
    cx = slice(512 + j * 128, 512 + (j + 1) * 128)
    cB = slice(512 + 512 + g * 128, 512 + 512 + (g + 1) * 128)
    cC = slice(512 + 768 + g * 128, 512 + 768 + (g + 1) * 128)
    m = dict(ssd_consts())
    m["hT"] = hT_b
    m["wfm"] = np.ascontiguousarray(np.concatenate([w_in_l[:, cz], w_in_l[:, cx], w_in_l[:, cB], w_in_l[:, cC]], axis=1))
    m["wdt"] = np.ascontiguousarray(w_in_l[:, 1536 + 2 * j:1536 + 2 * j + 2])
    chans = [slice(j * 128, (j + 1) * 128), slice(512 + g * 128, 512 + (g + 1) * 128), slice(768 + g * 128, 768 + (g + 1) * 128)]
    cwl = inp["conv_w"][l]
    cbl = inp["conv_b"][l]
    m["cw"] = np.ascontiguousarray(np.stack([cwl[:, c].T for c in chans], axis=1))
    m["cb"] = np.ascontiguousarray(np.stack([cbl[c] for c in chans], axis=1))
    m["dtb"] = np.ascontiguousarray(inp["dt_bias"][l][hs])
    m["alog"] = np.ascontiguousarray(inp["a_log"][l][hs])
    m["dcol"] = np.ascontiguousarray(np.repeat(inp["d_skip"][l][hs], 64)[:, None])
    return m


D = 1024
NTOK = 2048
NCH = 8
DFF = 2816
NFF = DFF // 128
EPS = 1e-6


def load_cast(p, dst, src, C, N, key, piece=None, stg=None, engines=("pool", "dve")):
    k = 0
    for c in range(C):
        for n0 in range(0, N, 2048):
            n1 = min(N, n0 + 2048)
            st = stg[k % len(stg)]
            sk = f"stg{k % len(stg)}"
            p.dma("sp", st[:, 0:n1 - n0], src[:, c, n0:n1], writes=[sk])
            eng = engines[k % len(engines)]
            p.op(eng, lambda e, st=st, c=c, n0=n0, n1=n1: e.tensor_copy(dst[:, c, n0:n1], st[:, 0:n1 - n0]),
                 reads=[sk], writes=[key])
            k += 1


def rms_tile(p, xt, xk, nwbc, nwk, hb, hbk, junk, ss, rs, tag):
    p.act(junk[:], xt[:], AF.Square, accum_out=ss[:], reads=[xk], writes=["junk", f"ss{tag}"])
    p.op("dve", lambda e: e.tensor_scalar(rs[:], ss[:], 1.0 / D, EPS, ALU.mult, ALU.add), reads=[f"ss{tag}"], writes=[f"rs{tag}"])
    p.act(rs[:], rs[:], AF.Sqrt, reads=[f"rs{tag}"], writes=[f"rs{tag}"])
    p.op("dve", lambda e: e.reciprocal(rs[:], rs[:]), reads=[f"rs{tag}"], writes=[f"rs{tag}"])
    p.op("dve", lambda e: e.scalar_tensor_tensor(hb[:], xt[:], rs[:, 0:1], nwbc[:], ALU.mult, ALU.mult),
         reads=[xk, f"rs{tag}", nwk], writes=[hbk])


def build_c1():
    p = Prog("c1")
    c1_phase(p)
    return p.emit()


def c1_phase(p):
    hT = p.dram("hT", [D, NTOK], BF16)
    x = p.dram("x", [NTOK, D], F32)
    yT = p.dram("yT", [2048, NTOK], F32)
    wg = p.dram("wg", [D, 4096], F32)
    wbr = p.dram("wbr", [2048, D], F32)
    wo = p.dram("wo", [D, D], F32)
    snw = p.dram("snw", [128, 4], F32)
    ident_d = p.dram("ident", [128, 128], BF16)
    x1 = p.dram("x1", [NTOK, D], F32, kind="ExternalOutput")

    wg_sb = p.sb([128, NCH, 4096], BF16)
    wbr_sb = p.sb([128, 16, D], BF16)
    wo_sb = p.sb([128, NCH, D], BF16)
    stg = [p.sb([128, 2048], F32) for _ in range(2)]
    ident = p.sb([128, 128], BF16)
    snw_sb = p.sb([128, 4], F32)
    ones_f = p.sb([128, 2], F32)
    hTt = [p.sb([128, NCH, 128], BF16) for _ in range(2)]
    yTt = [p.sb([128, 16, 128], F32) for _ in range(2)]
    yTb = p.sb([128, 16, 128], BF16)
    ysq = p.sb([128, 4, 128], F32)
    gates = p.sb([128, 4096], BF16)
    u = p.sb([128, D], F32)
    tmp = [p.sb([128, 512], F32) for _ in range(2)]
    ub = p.sb([128, D], BF16)
    uT = p.sb([128, NCH, 128], BF16)
    xt = [p.sb([128, D], F32) for _ in range(2)]
    rstd = p.sb([128, 2], F32)
    ps = [p.ps([128, 512], F32) for _ in range(6)]
    psS = p.ps([128, 512], F32)
    psB = p.ps([128, 1024], BF16)

    p.dma("sp", ident[:], ident_d, writes=["ident"])
    p.dma("sp", snw_sb[:], snw, writes=["snw"])
    p.op("pool", lambda e: e.memset(ones_f[:], 1.0), writes=["ones_f"])
    load_cast(p, wg_sb, wg.rearrange("(c p) n -> p c n", p=128), NCH, 4096, "wg", piece=128, stg=stg)
    load_cast(p, wbr_sb, wbr.rearrange("(c p) n -> p c n", p=128), 16, D, "wbr", piece=128, stg=stg)
    load_cast(p, wo_sb, wo.rearrange("(c p) n -> p c n", p=128), NCH, D, "wo", piece=128, stg=stg)

    hT_v = hT.rearrange("(c p) t -> p c t", p=128)
    yT_v = yT.rearrange("(c p) t -> p c t", p=128)
    pi = 0
    for t in range(NTOK // 128):
        s = t % 2
        ts_ = slice(t * 128, (t + 1) * 128)
        p.dma("sp", hTt[s][:], hT_v[:, :, ts_], writes=[f"hTt{s}"])
        p.dma("sp", yTt[s][:], yT_v[:, :, ts_], writes=[f"yTt{s}"])
        p.dma("sp", xt[s][:], x[ts_, :], writes=[f"xt{s}"])
        for fc in range(4):
            p.op("dve", lambda e, s=s, fc=fc: e.tensor_scalar(yTb[:, fc, :], yTt[s][:, fc, :], snw_sb[:, fc:fc + 1], None, ALU.mult),
                 reads=[f"yTt{s}", "snw"], writes=["yTb"])
        p.op("pool", lambda e, s=s: e.tensor_copy(yTb[:, 4:16, :], yTt[s][:, 4:16, :]), reads=[f"yTt{s}"], writes=["yTb"])
        p.act(ysq[:], yTt[s][:, 0:4, :], AF.Square, reads=[f"yTt{s}"], writes=["ysq"])
        for fc in range(4):
            p.mm(psS[:, 0:2], ysq[:, fc, :], ones_f[:], start=(fc == 0), stop=(fc == 3), reads=["ysq", "ones_f"], writes=["psS"])
        p.op("dve", lambda e: e.tensor_scalar(rstd[:], psS[:, 0:2], 1.0 / 512, EPS, ALU.mult, ALU.add), reads=["psS"], writes=["rstd"])
        p.act(rstd[:], rstd[:], AF.Sqrt, reads=["rstd"], writes=["rstd"])
        p.op("dve", lambda e: e.reciprocal(rstd[:], rstd[:]), reads=["rstd"], writes=["rstd"])
        for n in range(8):
            pp = ps[pi % 6]; pk = f"ps{pi % 6}"; pi += 1
            for c in range(NCH):
                p.mm(pp[:, :], hTt[s][:, c, :], wg_sb[:, c, n * 512:(n + 1) * 512], start=(c == 0), stop=(c == NCH - 1),
                     reads=[f"hTt{s}", "wg"], writes=[pk])
            p.act(gates[:, n * 512:(n + 1) * 512], pp[:, :], AF.Sigmoid, reads=[pk], writes=["gates"])
        for i in range(4):
            for hf in range(2):
                pp = ps[pi % 6]; pk = f"ps{pi % 6}"; pi += 1
                for fc in range(4):
                    p.mm(pp[:, :], yTb[:, i * 4 + fc, :], wbr_sb[:, i * 4 + fc, hf * 512:(hf + 1) * 512], start=(fc == 0), stop=(fc == 3),
                         reads=["yTb", "wbr"], writes=[pk])
                gsl = gates[:, i * 1024 + hf * 512:i * 1024 + (hf + 1) * 512]
                usl = u[:, hf * 512:(hf + 1) * 512]
                if i == 0:
                    p.op("dve", lambda e, pp=pp, gsl=gsl, usl=usl: e.scalar_tensor_tensor(usl, pp[:, :], rstd[:, 0:1], gsl, ALU.mult, ALU.mult),
                         reads=[pk, "rstd", "gates"], writes=["u"])
                else:
                    tm = tmp[hf]
                    p.op("dve", lambda e, pp=pp, gsl=gsl, tm=tm: e.tensor_tensor(tm[:], pp[:, :], gsl, ALU.mult),
                         reads=[pk, "gates"], writes=[f"tmp{hf}"])
                    p.op("pool", lambda e, tm=tm, usl=usl: e.tensor_tensor(usl, usl, tm[:], ALU.add),
                         reads=[f"tmp{hf}", "u"], writes=["u"])
        p.op("pool", lambda e: e.tensor_copy(ub[:], u[:]), reads=["u"], writes=["ub"])
        for c in range(NCH):
            p.tr(psB[:, c * 128:(c + 1) * 128], ub[:, c * 128:(c + 1) * 128], ident[:], reads=["ub", "ident"], writes=["psB"])
        p.op("act", lambda e: e.copy(uT[:].rearrange("p c t -> p (c t)"), psB[:, :]), reads=["psB"], writes=["uT"])
        for hf in range(2):
            pp = ps[pi % 6]; pk = f"ps{pi % 6}"; pi += 1
            for c in range(NCH):
                p.mm(pp[:, :], uT[:, c, :], wo_sb[:, c, hf * 512:(hf + 1) * 512], start=(c == 0), stop=(c == NCH - 1),
                     reads=["uT", "wo"], writes=[pk])
            p.op("dve", lambda e, pp=pp, s=s, hf=hf: e.tensor_tensor(xt[s][:, hf * 512:(hf + 1) * 512], pp[:, :], xt[s][:, hf * 512:(hf + 1) * 512], ALU.add),
                 reads=[pk, f"xt{s}"], writes=[f"xt{s}"])
        p.dma("sp", x1[ts_, :], xt[s][:], reads=[f"xt{s}"], is_out=True)


def build_c2():
    p = Prog("c2")
    c2_phase(p)
    return p.emit()


def c2_phase(p):
    x1 = p.dram("x1", [NTOK, D], F32)
    nw = p.dram("nw", [D], F32)
    nwf = p.dram("nwf", [D], F32)
    wgt = p.dram("wgt", [D, DFF], F32)
    wup = p.dram("wup", [D, DFF], F32)
    wdn = p.dram("wdn", [DFF, D], F32)
    ident_d = p.dram("ident", [128, 128], BF16)
    x2 = p.dram("x2", [NTOK, D], F32, kind="ExternalOutput")
    xn = p.dram("xn", [NTOK, D], F32, kind="ExternalOutput")

    wg_sb = p.sb([128, NCH, DFF], BF16)
    wu_sb = p.sb([128, NCH, DFF], BF16)
    wd_sb = p.sb([128, NFF, D], BF16)
    stg = [p.sb([128, 2048], F32) for _ in range(1)]
    ident = p.sb([128, 128], BF16)
    nwbc = p.sb([128, D], F32)
    nwfbc = p.sb([128, D], F32)
    xt = [p.sb([128, D], F32) for _ in range(3)]
    junk = p.sb([128, D], BF16)
    hb = p.sb([128, D], BF16)
    ss = p.sb([128, 1], F32)
    rs = p.sb([128, 1], F32)
    ss2 = p.sb([128, 1], F32)
    rs2 = p.sb([128, 1], F32)
    h2T = p.sb([128, NCH, 256], BF16)
    aT = p.sb([128, NFF, 256], BF16)
    sg = [p.sb([128, 256], F32) for _ in range(2)]
    xo = [p.sb([128, D], F32) for _ in range(1)]
    ps = [p.ps([128, 512], F32) for _ in range(6)]
    psB = p.ps([128, 1024], BF16)

    p.dma("sp", ident[:], ident_d, writes=["ident"])
    p.dma("sp", nwbc[:], nw.partition_broadcast(128), writes=["nwbc"])
    p.dma("sp", nwfbc[:], nwf.partition_broadcast(128), writes=["nwfbc"])
    load_cast(p, wg_sb, wgt.rearrange("(c p) n -> p c n", p=128), NCH, DFF, "wgt", piece=128, stg=stg)
    load_cast(p, wu_sb, wup.rearrange("(c p) n -> p c n", p=128), NCH, DFF, "wup", piece=128, stg=stg)
    load_cast(p, wd_sb, wdn.rearrange("(c p) n -> p c n", p=128), NFF, D, "wdn", piece=64, stg=stg)

    pi = 0
    oi = 0
    for g in range(NTOK // 256):
        for j in range(2):
            t = g * 2 + j
            xs = xt[t % 3]
            xk = f"xt{t % 3}"
            p.dma("sp", xs[:], x1[t * 128:(t + 1) * 128, :], writes=[xk])
            rms_tile(p, xs, xk, nwbc, "nwbc", hb, "hb", junk, ss, rs, "a")
            for c in range(NCH):
                p.tr(psB[:, c * 128:(c + 1) * 128], hb[:, c * 128:(c + 1) * 128], ident[:], reads=["hb", "ident"], writes=["psB"])
            p.op("act", lambda e, j=j: e.copy(h2T[:, :, j * 128:(j + 1) * 128], psB[:, :].rearrange("p (c t) -> p c t", c=NCH)),
                 reads=["psB"], writes=["h2T"])
        for f in range(NFF):
            pa = ps[pi % 6]; pak = f"ps{pi % 6}"; pi += 1
            pb = ps[pi % 6]; pbk = f"ps{pi % 6}"; pi += 1
            for c in range(NCH):
                p.mm(pa[:, 0:256], wg_sb[:, c, f * 128:(f + 1) * 128], h2T[:, c, :], start=(c == 0), stop=(c == NCH - 1),
                     reads=["wgt", "h2T"], writes=[pak])
            for c in range(NCH):
                p.mm(pb[:, 0:256], wu_sb[:, c, f * 128:(f + 1) * 128], h2T[:, c, :], start=(c == 0), stop=(c == NCH - 1),
                     reads=["wup", "h2T"], writes=[pbk])
            sgt = sg[f % 2]
            p.act(sgt[:], pa[:, 0:256], AF.Silu, reads=[pak], writes=[f"sg{f % 2}"])
            p.op("dve", lambda e, pb=pb, sgt=sgt, f=f: e.tensor_tensor(aT[:, f, :], pb[:, 0:256], sgt[:], ALU.mult),
                 reads=[pbk, f"sg{f % 2}"], writes=["aT"])
        for j in range(2):
            t = g * 2 + j
            xs = xt[t % 3]
            xk = f"xt{t % 3}"
            o = xo[0]; ok = "xo0"; oi += 1
            for hf in range(2):
                pp = ps[pi % 6]; pk = f"ps{pi % 6}"; pi += 1
                for f in range(NFF):
                    p.mm(pp[:, :], aT[:, f, j * 128:(j + 1) * 128], wd_sb[:, f, hf * 512:(hf + 1) * 512], start=(f == 0), stop=(f == NFF - 1),
                         reads=["aT", "wdn"], writes=[pk])
                p.op("dve", lambda e, pp=pp, xs=xs, hf=hf: e.tensor_tensor(xs[:, hf * 512:(hf + 1) * 512], pp[:, :], xs[:, hf * 512:(hf + 1) * 512], ALU.add),
                     reads=[pk, xk], writes=[xk])
            p.dma("sp", x2[t * 128:(t + 1) * 128, :], xs[:], reads=[xk], is_out=True)
            p.act(junk[:], xs[:], AF.Square, accum_out=ss2[:], reads=[xk], writes=["junk", "ss2"])
            p.op("dve", lambda e: e.tensor_scalar(rs2[:], ss2[:], 1.0 / D, EPS, ALU.mult, ALU.add), reads=["ss2"], writes=["rs2"])
            p.act(rs2[:], rs2[:], AF.Sqrt, reads=["rs2"], writes=["rs2"])
            p.op("dve", lambda e: e.reciprocal(rs2[:], rs2[:]), reads=["rs2"], writes=["rs2"])
            p.op("dve", lambda e, o=o, xs=xs: e.scalar_tensor_tensor(o[:], xs[:], rs2[:, 0:1], nwfbc[:], ALU.mult, ALU.mult),
                 reads=[xk, "rs2", "nwfbc"], writes=[ok])
            p.dma("sp", xn[t * 128:(t + 1) * 128, :], o[:], reads=[ok], is_out=True)


MIX = ((0, "ssd"), (1, "moba"), (2, "fox"), (3, "swa"))


def build_ab():
    p = Prog("ab")
    hT_i = p.nc.dram_tensor("hT_scr", [D, S], BF16).ap()
    yT_all = p.nc.dram_tensor("yT", [4, 128, S], F32, kind="ExternalOutput").ap()
    p.dp, p.dov = "n_", {"hT": hT_i}
    p.begin_phase("norm")
    norm_phase(p, S, D)
    p.end_phase()
    for bi, mode in MIX:
        p.dp, p.dov = mode + "_", {"hT": hT_i, "yT": yT_all[bi]}
        p.begin_phase(mode)
        if mode == "ssd":
            ssd_phase(p)
        else:
            attn_phase(p, mode)
        p.end_phase()
    return p.emit()


def ab_inputs(x_b, inp, l, j):
    m = {"n_x": x_b, "n_nw": inp["norm_mix"][l], "n_ident": np.eye(128).astype(NPBF)}
    extra = dict(forget_bias=inp["forget_bias"][l], rel_bias=inp["rel_bias"], sinks=inp["sinks"][l])
    for bi, mode in MIX:
        sub = ssd_inputs(None, inp, l, j) if mode == "ssd" else attn_inputs(mode, None, inp["w_in"][l], j, extra)
        for k, v in sub.items():
            if k != "hT":
                m[f"{mode}_{k}"] = v
    return m


def build_cd():
    p = Prog("cd")
    hT_i = p.nc.dram_tensor("hT_scr", [D, NTOK], BF16).ap()
    x1_i = p.nc.dram_tensor("x1_scr", [NTOK, D], F32).ap()
    x_in = p.dram("x", [NTOK, D], F32)
    p.dp, p.dov = "n_", {"hT": hT_i, "x": x_in}
    p.begin_phase("norm")
    norm_phase(p, NTOK, D)
    p.end_phase()
    p.dp, p.dov = "c1_", {"hT": hT_i, "x": x_in, "x1": x1_i}
    p.begin_phase("c1")
    c1_phase(p)
    p.end_phase()
    p.dp, p.dov = "c2_", {"x1": x1_i}
    p.begin_phase("c2")
    c2_phase(p)
    p.end_phase()
    return p.emit()


def cd_inputs(x_slab, yT_slab, inp, l):
    ident = np.eye(128).astype(NPBF)
    return {
        "x": x_slab, "n_nw": inp["norm_mix"][l], "n_ident": ident,
        "c1_yT": yT_slab, "c1_wg": np.ascontiguousarray(inp["w_in"][l][:, 5392:]),
        "c1_wbr": np.ascontiguousarray(inp["w_branch"][l].reshape(2048, D)), "c1_wo": inp["w_out"][l],
        "c1_snw": np.ascontiguousarray(inp["ssm_norm_w"][l].reshape(4, 128).T), "c1_ident": ident,
        "c2_nw": inp["norm_ffn"][l], "c2_nwf": inp["norm_final"], "c2_wgt": inp["w_ffn_gate"][l],
        "c2_wup": inp["w_ffn_up"][l], "c2_wdn": inp["w_ffn_down"][l], "c2_ident": ident,
    }


N_CORES = 8


def _run(nc, in_maps):
    return run_bass_kernel_spmd(nc, in_maps, core_ids=list(range(N_CORES))).results


def kernel(x, w_in, conv_w, conv_b, dt_bias, a_log, d_skip, ssm_norm_w, forget_bias, sinks, rel_bias,
           w_branch, w_out, norm_mix, norm_ffn, w_ffn_gate, w_ffn_up, w_ffn_down, norm_final):
    f32 = lambda a: np.ascontiguousarray(np.asarray(a, dtype=np.float32))
    inp = dict(x=f32(x), w_in=f32(w_in), conv_w=f32(conv_w), conv_b=f32(conv_b), dt_bias=f32(dt_bias), a_log=f32(a_log),
               d_skip=f32(d_skip), ssm_norm_w=f32(ssm_norm_w), forget_bias=f32(forget_bias), sinks=f32(sinks),
               rel_bias=f32(rel_bias), w_branch=f32(w_branch), w_out=f32(w_out), norm_mix=f32(norm_mix),
               norm_ffn=f32(norm_ffn), w_ffn_gate=f32(w_ffn_gate), w_ffn_up=f32(w_ffn_up), w_ffn_down=f32(w_ffn_down),
               norm_final=f32(norm_final))
    xcur = inp["x"]
    xn = None
    for l in range(2):
        res = _run(build_ab(), [ab_inputs(np.ascontiguousarray(xcur[c // 4]), inp, l, c % 4) for c in range(N_CORES)])
        yT_b = [np.concatenate([np.asarray(res[b * 4 + j]["yT"])[bi] for bi in range(4) for j in range(4)], axis=0)
                for b in range(2)]
        maps = []
        for c in range(N_CORES):
            b, sl = c // 4, slice((c % 4) * NTOK, (c % 4 + 1) * NTOK)
            maps.append(cd_inputs(np.ascontiguousarray(xcur[b, sl]), np.ascontiguousarray(yT_b[b][:, sl]), inp, l))
        res = _run(build_cd(), maps)
        xcur = np.stack([np.concatenate([np.asarray(res[b * 4 + q]["c2_x2"]) for q in range(4)], axis=0) for b in range(2)])
        xn = np.stack([np.concatenate([np.asarray(res[b * 4 + q]["c2_xn"]) for q in range(4)], axis=0) for b in range(2)])
    return np.ascontiguousarray(xn.astype(np.float32))
```

```python
import numpy as np
import ml_dtypes
from contextlib import ExitStack
import concourse.bass as bass
import concourse.mybir as mybir
from concourse.bass_utils import run_bass_kernel_spmd

F32 = mybir.dt.float32
BF16 = mybir.dt.bfloat16
AF = mybir.ActivationFunctionType
ALU = mybir.AluOpType
AX = mybir.AxisListType
NPBF = ml_dtypes.bfloat16

import os
DBG = bool(os.environ.get('FWDBG'))
ENGS = ("pe", "act", "dve", "pool", "sp")


class Prog:
    NDS = 24

    def __init__(self, name="k"):
        self.nc = bass.Bass("TRN2", target_bir_lowering=False)
        self.es = ExitStack()
        self.ops = {e: [] for e in ENGS}
        self.lastw = {}
        self.readers = {}
        self.dma_rr = 0
        self.dma_val = [0] * self.NDS
        self.out_tokens = []
        self._n = 0
        self.phase_es = None
        self.kp = ""
        self.dp = ""
        self.dov = {}

    def dram(self, name, shape, dt, kind="ExternalInput"):
        if name in self.dov:
            return self.dov[name]
        return self.nc.dram_tensor(self.dp + name, list(shape), dt, kind=kind).ap()

    def sb(self, shape, dt, name=None):
        self._n += 1
        es = self.phase_es if self.phase_es is not None else self.es
        return es.enter_context(self.nc.sbuf_tensor(name or f"sb{self._n}", list(shape), dt))

    def ps(self, shape, dt, name=None):
        self._n += 1
        es = self.phase_es if self.phase_es is not None else self.es
        return es.enter_context(self.nc.psum_tensor(name or f"ps{self._n}", list(shape), dt))

    def begin_phase(self, tag):
        assert self.phase_es is None
        self.phase_es = ExitStack()
        self.kp = tag + ":"

    def end_phase(self):
        self.barrier()
        self.phase_es.close()
        self.phase_es = None
        self.kp = ""

    def barrier(self):
        toks = set()
        for e in ENGS:
            n = 0
            last = None
            for i, o in enumerate(self.ops[e]):
                if o["fn"] is not None and o["dma"] is None:
                    last = i
            if last is not None:
                toks.add(("e", e, last))
        for k in range(self.NDS):
            if self.dma_val[k] > 0:
                toks.add(("d", k, self.dma_val[k]))
        for e in ENGS:
            deps = set(t for t in toks if not (t[0] == "e" and t[1] == e))
            self.ops[e].append(dict(fn=None, deps=deps, dma=None))

    def _deps(self, eng, reads, writes):
        raw, war = set(), set()
        for k in reads:
            t = self.lastw.get(k)
            if t is not None:
                raw.add(t)
            if k.split(":")[-1].startswith("ps"):
                rd = self.readers.get(k)
                if rd:
                    for e2, idx in rd[0].items():
                        if e2 != eng:
                            raw.add(("e", e2, idx))
        for k in writes:
            t = self.lastw.get(k)
            if t is not None:
                raw.add(t)
            rd = self.readers.get(k)
            if rd:
                for e2, idx in rd[0].items():
                    war.add(("e", e2, idx))
                for t in rd[1]:
                    war.add(t)
        deps = set()
        for t in raw:
            if t[0] == "e" and t[1] == eng and eng == "pe":
                continue
            deps.add(t)
        for t in war:
            if t[0] == "e" and t[1] == eng:
                continue
            deps.add(t)
        return deps

    def _commit(self, tok, reads, writes):
        for k in reads:
            rd = self.readers.setdefault(k, [{}, []])
            if tok[0] == "e":
                rd[0][tok[1]] = tok[2]
            else:
                rd[1].append(tok)
        for k in writes:
            self.lastw[k] = tok
            self.readers[k] = [{}, []]

    def _k(self, keys):
        return [k if k.startswith("g:") else self.kp + k for k in keys]

    def op(self, eng, fn, reads=(), writes=()):
        reads, writes = self._k(reads), self._k(writes)
        deps = self._deps(eng, reads, writes)
        idx = len(self.ops[eng])
        tok = ("e", eng, idx)
        self.ops[eng].append(dict(fn=fn, deps=deps, dma=None))
        self._commit(tok, reads, writes)
        return tok

    def dma(self, eng, out, in_, reads=(), writes=(), is_out=False, **kw):
        reads, writes = self._k(reads), self._k(writes)
        deps = self._deps(eng, reads, writes)
        k = self.dma_rr
        self.dma_rr = (self.dma_rr + 1) % self.NDS
        prev = self.dma_val[k]
        self.dma_val[k] = prev + 16
        tok = ("d", k, prev + 16)
        fn = lambda e, out=out, in_=in_, kw=kw: e.dma_start(out=out, in_=in_, **kw)
        self.ops[eng].append(dict(fn=fn, deps=deps, dma=(k, prev)))
        self._commit(tok, reads, writes)
        if is_out:
            self.out_tokens.append(tok)
        return tok

    def cc(self, kind, in_ap, out_ap, groups, reads=(), writes=()):
        deps = self._deps("pool", reads, writes)
        k = self.dma_rr
        self.dma_rr = (self.dma_rr + 1) % self.NDS
        prev = self.dma_val[k]
        self.dma_val[k] = prev + 16
        tok = ("d", k, prev + 16)
        fn = lambda e: e.collective_compute(kind, ALU.bypass, groups, [in_ap], [out_ap])
        self.ops["pool"].append(dict(fn=fn, deps=deps, dma=(k, prev)))
        self._commit(tok, reads, writes)
        return tok

    def emit(self):
        nc = self.nc
        self.ops["sp"].append(dict(fn=None, deps=set(self.out_tokens), dma=None))
        ms = {e: set() for e in ENGS}
        for e in ENGS:
            for o in self.ops[e]:
                for t in o["deps"]:
                    if t[0] == "e":
                        ms[t[1]].add(t[2])
        rank = {e: {idx: i + 1 for i, idx in enumerate(sorted(ms[e]))} for e in ENGS}
        esem = {e: self.es.enter_context(nc.semaphore(f"s_{e}")) for e in ENGS}
        dsem = [self.es.enter_context(nc.semaphore(f"d_{i}")) for i in range(self.NDS)]
        ops = self.ops

        def run(e, eng):
            seen = {}
            for idx, o in enumerate(ops[e]):
                need = {}
                for t in o["deps"]:
                    if t[0] == "e":
                        key = ("e", t[1])
                        c = rank[t[1]][t[2]]
                    else:
                        key = ("d", t[1])
                        c = t[2]
                    if c > need.get(key, 0):
                        need[key] = c
                if o["dma"] is not None:
                    k, prev = o["dma"]
                    if prev > need.get(("d", k), 0):
                        need[("d", k)] = prev
                for key, c in need.items():
                    if c > seen.get(key, 0):
                        s = esem[key[1]] if key[0] == "e" else dsem[key[1]]
                        eng.wait_ge(s, c)
                        seen[key] = c
                        if DBG: print(f"[{e}] #{idx} wait {key} >= {c}")
                if o["fn"] is None:
                    continue
                ins = o["fn"](eng)
                if DBG: print(f"[{e}] #{idx} issue dma={o['dma']} inc={idx in ms[e]} rank={rank[e].get(idx)}")
                if o["dma"] is not None:
                    ins.then_inc(dsem[o["dma"][0]], 16)
                elif idx in ms[e]:
                    ins.then_inc(esem[e], 1)

        with nc.Block() as block:
            @block.tensor
            def _(eng):
                run("pe", eng)

            @block.scalar
            def _(eng):
                run("act", eng)

            @block.vector
            def _(eng):
                run("dve", eng)

            @block.gpsimd
            def _(eng):
                run("pool", eng)

            @block.sync
            def _(eng):
                run("sp", eng)
        self.es.close()
        return nc

    def mm(self, out, lhsT, rhs, start=True, stop=True, reads=(), writes=()):
        return self.op("pe", lambda e: e.matmul(out, lhsT, rhs, start=start, stop=stop), reads, writes)

    def tr(self, out, in_, ident, reads=(), writes=()):
        return self.op("pe", lambda e: e.transpose(out, in_, ident), reads, writes)

    def act(self, out, in_, func, reads=(), writes=(), **kw):
        return self.op("act", lambda e: e.activation(out, in_, func, **kw), reads, writes)


EPS = 1e-6


def norm_rows(p, x_ap_fn, nwbc, ident, hT_sb, ntiles, D, tag, xt_keys=None):
    pass


def build_norm(NT=2048, D=1024):
    p = Prog("norm")
    norm_phase(p, NT, D)
    return p.emit()


def norm_phase(p, NT=2048, D=1024):
    x = p.dram("x", [NT, D], F32)
    nw = p.dram("nw", [D], F32)
    ident_d = p.dram("ident", [128, 128], BF16)
    hT = p.dram("hT", [D, NT], BF16, kind="ExternalOutput")
    nch = D // 128
    ntiles = NT // 128
    xt = [p.sb([128, D], F32) for _ in range(2)]
    junk = p.sb([128, D], BF16)
    hb = [p.sb([128, D], BF16) for _ in range(2)]
    nwbc = p.sb([128, D], F32)
    ident = p.sb([128, 128], BF16)
    ss = [p.sb([128, 1], F32) for _ in range(2)]
    rs = [p.sb([128, 1], F32) for _ in range(2)]
    hT_sb = p.sb([128, nch, NT], BF16)
    pst = [p.ps([128, nch * 128], BF16) for _ in range(2)]

    p.dma("sp", nwbc[:], nw.partition_broadcast(128), writes=["nwbc"])
    p.dma("sp", ident[:], ident_d, writes=["ident"])
    for i in range(ntiles):
        s = i % 2
        p.dma("sp", xt[s][:], x[i * 128:(i + 1) * 128, :], writes=[f"xt{s}"])
        p.act(junk[:], xt[s][:], AF.Square, accum_out=ss[s][:], reads=[f"xt{s}"], writes=["junk", f"ss{s}"])
        p.op("dve", lambda e, s=s: e.tensor_scalar(rs[s][:], ss[s][:], 1.0 / D, EPS, ALU.mult, ALU.add),
             reads=[f"ss{s}"], writes=[f"rs{s}"])
        p.act(rs[s][:], rs[s][:], AF.Sqrt, reads=[f"rs{s}"], writes=[f"rs{s}"])
        p.op("dve", lambda e, s=s: e.reciprocal(rs[s][:], rs[s][:]), reads=[f"rs{s}"], writes=[f"rs{s}"])
        p.op("dve", lambda e, s=s: e.scalar_tensor_tensor(hb[s][:], xt[s][:], rs[s][:, 0:1], nwbc[:], ALU.mult, ALU.mult),
             reads=[f"xt{s}", f"rs{s}", "nwbc"], writes=[f"hb{s}"])
        for c in range(nch):
            p.tr(pst[s][:, c * 128:(c + 1) * 128], hb[s][:, c * 128:(c + 1) * 128], ident[:],
                 reads=[f"hb{s}", "ident"], writes=[f"pst{s}"])
        p.op("act", lambda e, s=s, i=i: e.copy(hT_sb[:, :, i * 128:(i + 1) * 128],
                                                 pst[s][:].rearrange("p (c t) -> p c t", c=nch)),
             reads=[f"pst{s}"], writes=["hT_sb"])
    p.dma("sp", hT.rearrange("(c p) t -> p c t", p=128), hT_sb[:], reads=["hT_sb"], is_out=True)


import os
VAR = os.environ.get('VAR', '')

S = 8192
D = 1024
NCH = 8
TCH = 512
NTC = S // TCH
NT = S // 128
NEG = -30000.0
MOBA_W = 1792
MOBA_DMIN = -384
SWA_W = 1024


def t5_bucket(d):
    d = np.asarray(d)
    dd = np.maximum(d, 1).astype(np.float32)
    large = 16 + (np.log(dd / np.float32(16)) / np.float32(np.log(1024 / 16)) * np.float32(16)).astype(np.int32)
    large = np.minimum(large, 31)
    return np.where(d < 16, np.maximum(d, 0), large)


def attn_consts(mode):
    c = {}
    c["ident"] = np.eye(128, dtype=NPBF)
    c["identf"] = np.eye(128, dtype=np.float32)
    if mode == "fox":
        c["tri"] = np.triu(np.ones((128, 128), np.float32))
        k = np.arange(128)[:, None]
        q = np.arange(512)[None, :]
        m = np.stack([np.where(q >= k + 128 * d, 0.0, NEG) for d in range(4)], 1)
        c["maskT"] = m.astype(NPBF)
    elif mode == "swa":
        c["jrev"] = np.eye(128, dtype=NPBF)[::-1].copy()
        n = SWA_W + 127
        dist = np.arange(n) - 127 + MOBA_DMIN
        oh = np.zeros((33, n), np.float32)
        b = t5_bucket(dist)
        for i in range(n):
            if 0 <= dist[i] < 128:
                oh[b[i], i] = 1.0
            else:
                oh[32, i] = NEG
        c["oh"] = oh
        kind = np.zeros((2, S), np.float32)
        kind[0, :] = 1.0
        c["kind"] = kind.astype(NPBF)
        vs = np.zeros((2, 2, 128), np.float32)
        vs[0, 0, 64:128] = 1.0
        vs[1, 1, 0:64] = 1.0
        c["vsink"] = vs.astype(NPBF)
    else:
        c["jrev"] = np.eye(128, dtype=NPBF)[::-1].copy()
        n = MOBA_W + 127
        dist = np.arange(n) - 127 + MOBA_DMIN
        oh = np.zeros((33, n), np.float32)
        b = t5_bucket(dist)
        for i in range(n):
            if dist[i] >= 0:
                oh[b[i], i] += 1.0
                oh[31, i] -= 1.0
            else:
                oh[32, i] = NEG
        c["oh"] = oh
        ind = np.zeros((33, S), np.float32)
        for j in range(32):
            ind[j, j * 256:(j + 1) * 256] = 1.0
        ind[32, :] = 1.0
        c["kind"] = ind.astype(NPBF)
    return c


def build_attn(mode, stage=9):
    p = Prog(mode)
    attn_phase(p, mode)
    return p.emit()


def attn_phase(p, mode, stage=9):
    fox = mode == "fox"
    swa = mode == "swa"
    UW = SWA_W if swa else MOBA_W
    KIN = 2 if swa else 33
    NTOK = 386 if fox else 384
    hT = p.dram("hT", [D, S], BF16)
    wqk = p.dram("wqk", [D, 256], F32)
    wtok = p.dram("wtok", [D, NTOK], F32)
    ident_d = p.dram("ident", [128, 128], BF16)
    identf_d = p.dram("identf", [128, 128], F32)
    yT = p.dram("yT", [128, S], F32, kind="ExternalOutput")
    if fox:
        fb_d = p.dram("fb", [2], F32)
        tri_d = p.dram("tri", [128, 128], F32)
        maskT_d = p.dram("maskT", [128, 4, 512], BF16)
    else:
        jrev_d = p.dram("jrev", [128, 128], BF16)
        oh_d = p.dram("oh", [33, UW + 127], F32)
        tab_d = p.dram("tab", [33, 2], F32)
        kind_d = p.dram("kind", [KIN, S], BF16)
        vec_d = p.dram("vecscr", [2, UW + 127], F32, kind="Internal")
        if swa:
            ksink_d = p.dram("ksink", [66, 2], F32)

    hT_v = hT.rearrange("(c p) t -> p c t", p=128)
    hc = [p.sb([128, NCH, TCH], BF16) for _ in range(2)]
    wqk_sb = p.sb([128, NCH, 256], BF16)
    wtok_sb = p.sb([128, NCH, NTOK], BF16)
    ident = p.sb([128, 128], BF16)
    identf = p.sb([128, 128], F32)
    QTa = [p.sb([128, S], BF16) for _ in range(2)]
    KTa = [p.sb([128, S], BF16) for _ in range(2)]
    V = p.sb([128, NT, 2, 128], BF16)
    nrm = p.sb([128, NT, 4], F32)
    sq = [p.sb([128, 256], F32) for _ in range(2)]
    NA = 96 if fox else (2 if swa else 33)
    A = [p.sb([128, NT, NA], BF16) for _ in range(2)]
    pT = [p.sb([128, TCH], BF16) for _ in range(4)]
    rl = [p.sb([128, TCH], F32) for _ in range(2)]
    yst = [p.sb([128, TCH], F32) for _ in range(2)]
    ones_f = p.sb([128, 128], F32)
    sbnd = p.sb([128, NT, 2], F32)
    small = p.sb([128, 64], F32)
    psA = [p.ps([128, 512], F32) for _ in range(2)]
    psO = [p.ps([128, 512], F32) for _ in range(2)]
    psT = [p.ps([128, 512], F32) for _ in range(2)]
    psB = p.ps([128, 1024], BF16)
    psM = p.ps([128, 512], F32)

    p.dma("sp", ident[:], ident_d, writes=["ident"])
    p.dma("sp", identf[:], identf_d, writes=["identf"])
    stg = p.sb([128, NCH, NTOK], F32)
    p.dma("sp", stg[:, :, 0:256], wqk.rearrange("(c p) n -> p c n", p=128), writes=["stg"])
    p.op("pool", lambda e: e.tensor_copy(wqk_sb[:], stg[:, :, 0:256]), reads=["stg"], writes=["wqk"])
    p.dma("sp", stg[:], wtok.rearrange("(c p) n -> p c n", p=128), writes=["stg"])
    p.op("pool", lambda e: e.tensor_copy(wtok_sb[:], stg[:]), reads=["stg"], writes=["wtok"])
    p.op("pool", lambda e: e.memset(V[:], 1.0), writes=["V"])
    p.op("pool", lambda e: e.memset(ones_f[:], 1.0), writes=["ones_f"])
    for h in range(2):
        p.op("pool", lambda e, h=h: e.memset(A[h][:], 0.0), writes=[f"A{h}"])
    if fox:
        f_sb = p.sb([128, NT, 2], F32)
        tri = p.sb([128, 128], F32)
        maskT = p.sb([128, 4, 512], BF16)
        fbb = p.sb([128, 2], F32)
        p.dma("sp", tri[:], tri_d, writes=["tri"])
        p.dma("sp", maskT[:], maskT_d, writes=["maskT"])
        p.dma("sp", fbb[:], fb_d.partition_broadcast(128), writes=["fbb"])
    else:
        jrev = p.sb([128, 128], BF16)
        U = [p.sb([128, UW], BF16) for _ in range(2)]
        oh = p.sb([33, UW + 127], F32)
        tab = p.sb([33, 2], F32)
        vec_sb = p.sb([2, UW + 127], F32)
        if swa:
            ksf = p.sb([66, 2], F32)
            ksink = p.sb([66, 2], BF16)
            vsink = p.sb([2, 2, 128], BF16)
            psink = p.sb([2, TCH], BF16)
            vsink_d = p.dram("vsink", [2, 2, 128], BF16)
            p.dma("sp", vsink[:], vsink_d, writes=["vsink"])
            p.dma("sp", ksf[:], ksink_d, writes=["ksf"])
            p.op("dve", lambda e: e.tensor_copy(ksink[:], ksf[:]), reads=["ksf"], writes=["ksink"])

        kmT = [p.sb([64, 32], F32) for _ in range(2)]
        qf = [p.sb([64, TCH], F32) for _ in range(2)]
        gsb = p.sb([128, 32], F32)
        top8 = p.sb([128, 8], F32)
        mb01 = p.sb([128, 32], F32)
        p.dma("sp", jrev[:], jrev_d, writes=["jrev"])
        p.dma("sp", oh[:], oh_d, writes=["oh"])
        p.dma("sp", tab[:], tab_d, writes=["tab"])
        for h in range(2):
            p.dma("sp", KTa[h][64:64 + KIN, :], kind_d, writes=[f"KTa{h}"])
            p.op("pool", lambda e, h=h: e.memset(kmT[h][:], 0.0), writes=[f"kmT{h}"])
        W = UW + 127
        for c0 in range(0, W, 512):
            c1 = min(W, c0 + 512)
            p.mm(psM[0:2, 0:c1 - c0], tab[:, :], oh[:, c0:c1], reads=["tab", "oh"], writes=["psM"])
            p.op("dve", lambda e, c0=c0, c1=c1: e.tensor_copy(vec_sb[:, c0:c1], psM[0:2, 0:c1 - c0]),
                 reads=["psM"], writes=["vec_sb"])
        p.dma("sp", vec_d, vec_sb[:], reads=["vec_sb"], writes=["vec_d"])
        for h in range(2):
            src = bass.AP(vec_d.tensor, h * W, [[1, 128], [1, UW]])
            stgf = stg[:].rearrange("p c n -> p (c n)")
            p.dma("sp", stgf[:, 0:UW], src, reads=["vec_d"], writes=["stg"])
            p.op("pool", lambda e, h=h, stgf=stgf: e.tensor_copy(U[h][:], stgf[:, 0:UW]), reads=["stg"], writes=[f"U{h}"])

    for tc in range(NTC):
        s = tc % 2
        p.dma("sp", hc[s][:], hT_v[:, :, tc * TCH:(tc + 1) * TCH], writes=[f"hc{s}"])
        for g in range(4):
            if VAR in ('B', 'C', 'D'):
                break
            ps = psA[g % 2]
            for c in range(NCH):
                p.mm(ps[0:64, :], wqk_sb[:, c, g * 64:(g + 1) * 64], hc[s][:, c, :], start=(c == 0), stop=(c == NCH - 1),
                     reads=[f"hc{s}", "wqk"], writes=[f"psA{g % 2}"])
            h = g % 2
            cs = slice(tc * TCH, (tc + 1) * TCH)
            if g < 2:
                p.op("act", lambda e, ps=ps, h=h, cs=cs: e.mul(QTa[h][0:64, cs], ps[0:64, :], 0.125),
                     reads=[f"psA{g % 2}"], writes=[f"QTa{h}"])
                if not fox and not swa:
                    p.op("dve", lambda e, ps=ps, h=h: e.tensor_copy(qf[h][:], ps[0:64, :]),
                         reads=[f"psA{g % 2}"], writes=[f"qf{h}"])
            else:
                p.op("dve", lambda e, ps=ps, h=h, cs=cs: e.tensor_copy(KTa[h][0:64, cs], ps[0:64, :]),
                     reads=[f"psA{g % 2}"], writes=[f"KTa{h}"])
                if not fox and not swa:
                    p.op("dve", lambda e, ps=ps, h=h, tc=tc: e.tensor_reduce(
                        kmT[h][:, 2 * tc:2 * tc + 2], ps[0:64, :].rearrange("p (b t) -> p b t", b=2), AX.X, ALU.add),
                        reads=[f"psA{g % 2}"], writes=[f"kmT{h}"])
        for j in range(4):
            if VAR == 'A':
                break
            i = tc * 4 + j
            ps = psT[j % 2]
            for c in range(NCH):
                p.mm(ps[:, 0:NTOK], hc[s][:, c, j * 128:(j + 1) * 128], wtok_sb[:, c, :], start=(c == 0), stop=(c == NCH - 1),
                     reads=[f"hc{s}", "wtok"], writes=[f"psT{j % 2}"])
            if os.environ.get("EXP") == "8":
                p.act(sq[j % 2][:], ps[:, 128:384], AF.Square, reads=[f"psT{j % 2}"], writes=[f"sq{j % 2}"])
                continue
            p.op("dve", lambda e, ps=ps, i=i: e.tensor_copy(V[:, i, 0, 0:64], ps[:, 0:64]),
                 reads=[f"psT{j % 2}"], writes=["V"])
            p.op("dve", lambda e, ps=ps, i=i: e.tensor_copy(V[:, i, 1, 64:128], ps[:, 64:128]),
                 reads=[f"psT{j % 2}"], writes=["V"])
            if VAR == 'B':
                continue
            EXP = os.environ.get("EXP", "")
            if EXP == "1":
                p.act(sq[j % 2][:], ps[:, 0:256], AF.Square, reads=[f"psT{j % 2}"], writes=[f"sq{j % 2}"])
            elif EXP == "3":
                p.act(sq[j % 2][:], ps[:, 128:384], AF.Square, reads=[f"psT{j % 2}", "V"], writes=[f"sq{j % 2}"])
            elif EXP == "5":
                p.act(sq[j % 2][:], ps[:, 128:384], AF.Square, reads=[f"psT{j % 2}"], writes=[f"sqx{i}"])
            elif EXP == "6":
                p.op("act", lambda e, ps=ps, j=j: e.mul(sq[j % 2][:], ps[:, 128:384], 1.0), reads=[f"psT{j % 2}"], writes=[f"sq{j % 2}"])
            elif EXP == "7":
                p.op("act", lambda e, ps=ps, j=j: e.mul(sq[j % 2][0:64, :], ps[0:64, 128:384], 1.0), reads=[f"psT{j % 2}"], writes=[f"sq{j % 2}"])
            elif EXP == "4":
                p.op("dve", lambda e, ps=ps, j=j: e.tensor_copy(sq[j % 2][:], ps[:, 128:384]), reads=[f"psT{j % 2}"], writes=[f"sq{j % 2}"])
            else:
                p.act(sq[j % 2][:], ps[:, 128:384], AF.Square, reads=[f"psT{j % 2}"], writes=[f"sq{j % 2}"])
            if VAR == 'C':
                continue
            p.op("dve", lambda e, i=i, j=j: e.tensor_reduce(nrm[:, i, :], sq[j % 2][:].rearrange("p (g d) -> p g d", d=64), AX.X, ALU.add),
                 reads=[f"sq{j % 2}"], writes=["nrm"])
            if fox:
                p.op("dve", lambda e, ps=ps, i=i: e.tensor_copy(f_sb[:, i, :], ps[:, 384:386]),
                     reads=[f"psT{j % 2}"], writes=["f_sb"])
        if not fox and not swa:
            for h in range(2):
                for j in range(4):
                    i = tc * 4 + j
                    n = i // 2
                    p.mm(psM[:, 0:32], qf[h][:, j * 128:(j + 1) * 128], kmT[h][:, :], reads=[f"qf{h}", f"kmT{h}"], writes=["psM"])
                    p.op("dve", lambda e: e.tensor_copy(gsb[:], psM[:, 0:32]), reads=["psM"], writes=["gsb"])
                    p.op("dve", lambda e, n=n: e.memset(gsb[:, n:32], -1e30), writes=["gsb"])
                    p.op("dve", lambda e: e.max(top8[:], gsb[:]), reads=["gsb"], writes=["top8"])
                    p.op("dve", lambda e: e.tensor_scalar(mb01[:], gsb[:], top8[:, 2:3], 1.0, ALU.is_ge, ALU.subtract),
                         reads=["gsb", "top8"], writes=["mb01"])
                    p.op("dve", lambda e, h=h, i=i: e.tensor_scalar(A[h][:, i, 0:32], mb01[:], -NEG, None, ALU.mult),
                         reads=["mb01"], writes=[f"A{h}"])
                    p.op("dve", lambda e, h=h, i=i, n=n: e.memset(A[h][:, i, n:n + 1], 0.0), writes=[f"A{h}"])

    kmx = small[:, 0:2]
    p.op("dve", lambda e: e.tensor_reduce(kmx, nrm[:, :, 2:4].rearrange("p i g -> p g i"), AX.X, ALU.max),
         reads=["nrm"], writes=["small"])
    p.tr(psM[0:2, 0:128], kmx, identf[:], reads=["small", "identf"], writes=["psM"])
    kmx2 = small[0:2, 8:9]
    p.op("dve", lambda e: e.tensor_reduce(kmx2, psM[0:2, 0:128], AX.X, ALU.max), reads=["psM"], writes=["small2"])
    kbr = small[0:2, 16:48]
    krow = p.sb([2, 128], F32)
    p.op("dve", lambda e: e.tensor_scalar(krow[:], ones_f[0:2, :], kmx2, None, ALU.mult),
         reads=["small2", "ones_f"], writes=["krow"])
    p.mm(psM[:, 0:2], krow[:], identf[0:2, 0:2], reads=["krow", "identf"], writes=["psM"])
    kbc = small[:, 4:6]
    p.op("dve", lambda e: e.tensor_copy(kbc, psM[:, 0:2]), reads=["psM"], writes=["kbc"])
    for h in range(2):
        p.op("dve", lambda e, h=h: e.tensor_scalar(sbnd[:, :, h], nrm[:, :, h], small[:, 4 + h:5 + h], None, ALU.mult),
             reads=["nrm", "kbc"], writes=["sbnd"])
    p.act(sbnd[:], sbnd[:], AF.Sqrt, reads=["sbnd"], writes=["sbnd"], scale=(0.125 * 1.02) ** 2)

    if fox:
        nfb = p.sb([128, 2], F32)
        e1 = p.sb([128, NT, 2], F32)
        g = p.sb([128, NT, 2], F32)
        incl = p.sb([128, NT, 2], F32)
        G = p.sb([128, NT, 2], F32)
        r1 = p.sb([128, NT, 2], F32)
        Gh = p.sb([128, NT, 2], BF16)
        Gm = p.sb([128, NT, 2], BF16)
        Gl = p.sb([128, NT, 2], BF16)
        p.op("dve", lambda e: e.tensor_scalar(nfb[:], fbb[:], -1.0, None, ALU.mult), reads=["fbb"], writes=["nfb"])
        for h in range(2):
            p.act(e1[:, :, h], f_sb[:, :, h], AF.Exp, reads=["f_sb", "nfb"], writes=["e1"], scale=-1.0, bias=nfb[:, h:h + 1])
        p.act(g[:], e1[:], AF.Ln, reads=["e1"], writes=["g"], bias=1.0)
        gf = g[:].rearrange("p i h -> p (i h)")
        p.mm(psM[:, 0:128], tri[:], gf, reads=["tri", "g"], writes=["psM"])
        p.mm(psM[:, 128:256], ones_f[:], gf, reads=["ones_f", "g"], writes=["psM"])
        W_v = psM[:, 0:128].rearrange("p (i h) -> p i h", h=2)
        B_v = psM[:, 128:256].rearrange("p (i h) -> p i h", h=2)
        for h in range(2):
            p.op("dve", lambda e, h=h: e.tensor_tensor_scan(incl[:, :, h], ones_f[:, 0:NT], B_v[:, :, h], 0.0, ALU.mult, ALU.add),
                 reads=["psM", "ones_f"], writes=["incl"])
        p.op("dve", lambda e: e.tensor_tensor(r1[:], incl[:], B_v, ALU.subtract), reads=["incl", "psM"], writes=["r1"])
        p.op("dve", lambda e: e.tensor_tensor(G[:], r1[:], W_v, ALU.add), reads=["r1", "psM"], writes=["G"])
        p.op("dve", lambda e: e.tensor_copy(Gh[:], G[:]), reads=["G"], writes=["Gh"])
        p.op("dve", lambda e: e.tensor_tensor(r1[:], G[:], Gh[:], ALU.subtract), reads=["G", "Gh"], writes=["r1"])
        p.op("dve", lambda e: e.tensor_copy(Gm[:], r1[:]), reads=["r1"], writes=["Gm"])
        p.op("dve", lambda e: e.tensor_tensor(r1[:], r1[:], Gm[:], ALU.subtract), reads=["r1", "Gm"], writes=["r1"])
        p.op("dve", lambda e: e.tensor_copy(Gl[:], r1[:]), reads=["r1"], writes=["Gl"])
        for h in range(2):
            Ah = A[h]
            for col in (0, 1, 2, 35, 36, 37, 38):
                p.op("pool", lambda e, Ah=Ah, col=col: e.memset(Ah[:, :, col:col + 1], 1.0), writes=[f"A{h}"])
            for k3, Gx in enumerate((Gh, Gm, Gl)):
                p.op("dve", lambda e, Ah=Ah, Gx=Gx, k3=k3, h=h: e.tensor_scalar(Ah[:, :, 3 + k3], Gx[:, :, h], -1.0, None, ALU.mult),
                     reads=["Gh", "Gm", "Gl"], writes=[f"A{h}"])
                p.op("dve", lambda e, Ah=Ah, Gx=Gx, k3=k3, h=h: e.tensor_copy(Ah[:, :, 32 + k3], Gx[:, :, h]),
                     reads=["Gh", "Gm", "Gl"], writes=[f"A{h}"])
            p.op("dve", lambda e, Ah=Ah, h=h: e.tensor_scalar(Ah[:, :, 6], sbnd[:, :, h], -1.0, None, ALU.mult),
                 reads=["sbnd"], writes=[f"A{h}"])
    elif swa:
        for h in range(2):
            p.op("dve", lambda e, h=h: e.tensor_scalar(A[h][:, :, 0], sbnd[:, :, h], -1.0, None, ALU.mult),
                 reads=["sbnd"], writes=[f"A{h}"])
            p.op("dve", lambda e, h=h: e.memset(A[h][:, :, 1:2], 1.0), writes=[f"A{h}"])
    else:
        for h in range(2):
            p.op("dve", lambda e, h=h: e.tensor_scalar(A[h][:, :, 32], sbnd[:, :, h], -1.0, None, ALU.mult),
                 reads=["sbnd"], writes=[f"A{h}"])

    for h in range(2):
        for g8 in range(NT // 8):
            for t in range(8):
                i = g8 * 8 + t
                p.tr(psB[0:NA, t * 128:(t + 1) * 128], A[h][:, i, :], ident[:], reads=[f"A{h}", "ident"], writes=["psB"])
            cs = slice(g8 * 1024, (g8 + 1) * 1024)
            if fox:
                p.op("dve", lambda e, h=h, cs=cs: e.tensor_copy(QTa[h][64:71, cs], psB[0:7, :]), reads=["psB"], writes=[f"QTa{h}"])
                p.op("act", lambda e, h=h, cs=cs: e.copy(KTa[h][64:71, cs], psB[32:39, :]), reads=["psB"], writes=[f"KTa{h}"])
            else:
                p.op("dve", lambda e, h=h, cs=cs: e.tensor_copy(QTa[h][64:64 + NA, cs], psB[0:NA, :]), reads=["psB"], writes=[f"QTa{h}"])

    KR = 71 if fox else (66 if swa else 97)
    SB = [(psA[0], "psA0"), (psA[1], "psA1"), (psT[0], "psT0"), (psT[1], "psT1")]
    NB = len(SB)
    LOOK = 2
    jobs = []
    oi = 0
    for h in range(2):
        for qc in range(NTC):
            nk = 4 * qc + 4
            kt0 = max(0, 4 * qc - 1) if swa else 0
            for kt in range(kt0, nk):
                jobs.append((h, qc, kt, kt == kt0, kt == nk - 1, oi))
            oi += 1

    def s_stage(i):
        h, qc, kt, first, last, oi = jobs[i]
        pa, pak = SB[i % NB]
        qs = slice(qc * TCH, (qc + 1) * TCH)
        ks = slice(kt * 128, (kt + 1) * 128)
        d0 = qc * TCH - kt * 128
        extra = (kt >= 4 * qc) if fox else (swa or d0 <= 896)
        p.mm(pa[:, :], KTa[h][0:KR, ks], QTa[h][0:KR, qs], start=True, stop=not extra,
             reads=[f"KTa{h}", f"QTa{h}"], writes=[pak])
        if extra:
            if fox:
                p.mm(pa[:, :], ident[:], maskT[:, kt - 4 * qc, :], start=False, stop=True,
                     reads=["ident", "maskT"], writes=[pak])
            else:
                off = d0 - MOBA_DMIN
                p.mm(pa[:, :], jrev[:], U[h][:, off:off + TCH], start=False, stop=True,
                     reads=["jrev", f"U{h}"], writes=[pak])

    def pv_stage(i):
        h, qc, kt, first, last, oi = jobs[i]
        pa, pak = SB[i % NB]
        pt = pT[i % len(pT)]
        ptk = f"pT{i % len(pT)}"
        po = psO[oi % 2]
        pok = f"psO{oi % 2}"
        qs = slice(qc * TCH, (qc + 1) * TCH)
        p.act(pt[:], pa[:, :], AF.Exp, reads=[pak], writes=[ptk])
        p.mm(po[:, :], V[:, kt, h, :], pt[:], start=first, stop=(last and not swa),
             reads=["V", ptk], writes=[pok])
        if not last:
            return
        if swa:
            p.mm(psM[0:2, :], ksink[:, 0:2], QTa[h][0:66, qs], reads=["ksink", f"QTa{h}"], writes=["psM"])
            p.act(psink[:], psM[0:2, :], AF.Exp, reads=["psM"], writes=["psink"])
            p.mm(po[:, :], vsink[:, h, :], psink[:], start=False, stop=True, reads=["vsink", "psink"], writes=[pok])
        r = rl[oi % 2]
        y = yst[oi % 2]
        if h == 0:
            num, den = slice(0, 64), slice(64, 128)
        else:
            num, den = slice(64, 128), slice(0, 64)
        p.op("dve", lambda e: e.reciprocal(r[den, :], po[den, :]), reads=[pok], writes=[f"rl{oi % 2}"])
        p.op("dve", lambda e: e.tensor_tensor(y[num, :], po[num, :], r[den, :], ALU.mult),
             reads=[pok, f"rl{oi % 2}"], writes=[f"yst{oi % 2}"])
        p.dma("sp", yT[num, qs], y[num, :], reads=[f"yst{oi % 2}"], is_out=True)

    for i in range(len(jobs) + LOOK):
        if i < len(jobs):
            s_stage(i)
        if i - LOOK >= 0:
            pv_stage(i - LOOK)


def attn_inputs(mode, hT_b, w_in_l, j, extra):
    off_moba = 512 + 1024 + 8
    off_fox = off_moba + 1536
    off_f = off_fox + 1536
    base = off_fox if mode == "fox" else off_moba
    hs = [2 * j, 2 * j + 1]
    if mode == "swa":
        off_q = off_f + 8
        off_kv = off_q + 512
        kv = j // 2
        q = [w_in_l[:, off_q + h * 64: off_q + (h + 1) * 64] for h in hs]
        k = [w_in_l[:, off_kv + kv * 64: off_kv + (kv + 1) * 64]] * 2
        v = [w_in_l[:, off_kv + 128 + kv * 64: off_kv + 128 + (kv + 1) * 64]] * 2
    else:
        q = [w_in_l[:, base + h * 64: base + (h + 1) * 64] for h in hs]
        k = [w_in_l[:, base + 512 + h * 64: base + 512 + (h + 1) * 64] for h in hs]
        v = [w_in_l[:, base + 1024 + h * 64: base + 1024 + (h + 1) * 64] for h in hs]
    m = dict(attn_consts(mode))
    m["hT"] = hT_b
    m["wqk"] = np.ascontiguousarray(np.concatenate(q + k, axis=1))
    if mode == "fox":
        f = [w_in_l[:, off_f + h: off_f + h + 1] for h in hs]
        m["wtok"] = np.ascontiguousarray(np.concatenate(v + q + k + f, axis=1))
        m["fb"] = np.ascontiguousarray(extra["forget_bias"][hs])
    else:
        m["wtok"] = np.ascontiguousarray(np.concatenate(v + q + k, axis=1))
        tab = extra["rel_bias"][:, hs] if mode == "moba" else extra["rel_bias"][:, [8 + hh for hh in hs]]
        if mode == "swa":
            ks = np.zeros((66, 2), np.float32)
            ks[64, :] = 1.0
            ks[65, :] = extra["sinks"][hs]
            m["ksink"] = ks
        m["tab"] = np.ascontiguousarray(np.concatenate([tab, np.ones((1, 2), np.float32)], axis=0))
    return m


S = 8192
D = 1024
NCH = 8
TCH = 512
NTC = S // TCH
NEG = -30000.0


def ssd_consts():
    c = {}
    c["ident"] = np.eye(128, dtype=NPBF)
    c["identf"] = np.eye(128, dtype=np.float32)
    c["tri"] = np.triu(np.ones((128, 128), np.float32))
    s = np.arange(128)[:, None]
    l = np.arange(256)[None, :]
    c["m01"] = np.where(l >= s, 1.0, 0.0).astype(np.float32)
    return c


def build_ssd(nchunks=NTC):
    p = Prog("ssd")
    ssd_phase(p, nchunks)
    return p.emit()


def ssd_phase(p, nchunks=NTC):
    hT = p.dram("hT", [D, S], BF16)
    wfm = p.dram("wfm", [D, 512], F32)
    wdt = p.dram("wdt", [D, 2], F32)
    cw_d = p.dram("cw", [128, 3, 4], F32)
    cb_d = p.dram("cb", [128, 3], F32)
    dtb_d = p.dram("dtb", [2], F32)
    alog_d = p.dram("alog", [2], F32)
    dcol_d = p.dram("dcol", [128, 1], F32)
    ident_d = p.dram("ident", [128, 128], BF16)
    identf_d = p.dram("identf", [128, 128], F32)
    tri_d = p.dram("tri", [128, 128], F32)
    m01_d = p.dram("m01", [128, 256], F32)
    yT = p.dram("yT", [128, S], F32, kind="ExternalOutput")
    hT_v = hT.rearrange("(c p) t -> p c t", p=128)

    hc = [p.sb([128, NCH, TCH], BF16) for _ in range(2)]
    stg = p.sb([128, NCH, 512], F32)
    wfm_sb = p.sb([128, NCH, 512], BF16)
    wdt_f = p.sb([128, NCH, 2], F32)
    wdt_sb = p.sb([128, NCH, 2], BF16)
    cw = p.sb([128, 3, 4], F32)
    cb = p.sb([128, 3], F32)
    dcol = p.sb([128, 1], F32)
    ident = p.sb([128, 128], BF16)
    identf = p.sb([128, 128], F32)
    tri = p.sb([128, 128], F32)
    ones_f = p.sb([128, 128], F32)
    dtb4 = p.sb([128, 4, 2], F32)
    negA4 = p.sb([128, 4, 2], F32)
    ub = [[p.sb([128, 3 + TCH], F32) for _ in range(2)] for _ in range(3)]
    acc = p.sb([128, TCH], F32)
    zs = p.sb([128, TCH], F32)
    xT = p.sb([128, TCH], F32)
    xTb = p.sb([128, TCH], BF16)
    BT = p.sb([128, TCH], BF16)
    CT = p.sb([128, TCH], BF16)
    Btok = p.sb([128, 4, 128], BF16)
    xdtp = [p.sb([128, 4, 128], BF16) for _ in range(2)]
    xdd = p.sb([128, 4, 128], BF16)
    zt = p.sb([128, 4, 2], F32)
    dt = p.sb([128, 4, 2], F32)
    a = p.sb([128, 4, 2], F32)
    acum = p.sb([128, 4, 2], F32)
    last = p.sb([128, 2, 2], F32)
    dte = p.sb([128, 4, 2], F32)
    dd = p.sb([128, 4, 2], F32)
    cd = p.sb([128, 2, 2], F32)
    nac = p.sb([128, 4, 2], F32)
    dg = [p.sb([128, 128], F32) for _ in range(2)]
    Dc = [p.sb([128, 256], F32) for _ in range(2)]
    Gm = [p.sb([128, 256], F32) for _ in range(2)]
    m01 = p.sb([128, 256], F32)
    LT = [p.sb([128, 256], F32) for _ in range(2)]
    WT = [p.sb([128, 256], BF16) for _ in range(2)]
    Eb = p.sb([128, 256], F32)
    CdT = [p.sb([128, 256], BF16) for _ in range(2)]
    hst = p.sb([128, 128], F32)
    hpad = [p.sb([128, 128], BF16) for _ in range(2)]
    t1 = p.sb([128, 256], F32)
    yst = [p.sb([128, 256], F32) for _ in range(2)]

    ps = [p.ps([128, 512], F32) for _ in range(7)]
    psB = p.ps([128, 1024], BF16)
    K = lambda i: f"ps{i}"

    for t, d_, k in ((ident, ident_d, "ident"), (identf, identf_d, "identf"), (tri, tri_d, "tri"), (m01, m01_d, "m01"),
                     (cw, cw_d, "cw"), (cb, cb_d, "cb"), (dcol, dcol_d, "dcol")):
        p.dma("sp", t[:], d_, writes=[k])
    p.dma("sp", stg[:], wfm.rearrange("(c p) n -> p c n", p=128), writes=["stg"])
    p.op("pool", lambda e: e.tensor_copy(wfm_sb[:], stg[:]), reads=["stg"], writes=["wfm"])
    p.dma("sp", wdt_f[:], wdt.rearrange("(c p) n -> p c n", p=128), writes=["wdt_f"])
    p.op("pool", lambda e: e.tensor_copy(wdt_sb[:], wdt_f[:]), reads=["wdt_f"], writes=["wdt"])
    p.op("pool", lambda e: e.memset(ones_f[:], 1.0), writes=["ones_f"])
    p.op("pool", lambda e: e.memset(hst[:], 0.0), writes=["hst"])
    for h in range(2):
        p.op("pool", lambda e, h=h: e.memset(hpad[h][:], 0.0), writes=[f"hpad{h}"])
        p.op("pool", lambda e, h=h: e.memset(xdtp[h][:], 0.0), writes=[f"xdtp{h}"])
    for g in range(3):
        p.op("pool", lambda e, g=g: e.memset(ub[g][1][:, TCH:TCH + 3], 0.0), writes=[f"ub{g}_1"])
    for j in range(4):
        p.dma("sp", dtb4[:, j, :], dtb_d.partition_broadcast(128), writes=["dtb4"])
        p.dma("sp", negA4[:, j, :], alog_d.partition_broadcast(128), writes=["negA4"])
    p.act(negA4[:], negA4[:], AF.Exp, reads=["negA4"], writes=["negA4"])
    p.op("dve", lambda e: e.tensor_scalar(negA4[:], negA4[:], -1.0, None, ALU.mult), reads=["negA4"], writes=["negA4"])

    for tc in range(nchunks):
        s = tc % 2
        p.dma("sp", hc[s][:], hT_v[:, :, tc * TCH:(tc + 1) * TCH], writes=[f"hc{s}"])
        for g in range(4):
            pp = ps[g % 2]
            for c in range(NCH):
                p.mm(pp[:, :], wfm_sb[:, c, g * 128:(g + 1) * 128], hc[s][:, c, :], start=(c == 0), stop=(c == NCH - 1),
                     reads=[f"hc{s}", "wfm"], writes=[K(g % 2)])
            if g == 0:
                p.act(zs[:], pp[:, :], AF.Silu, reads=[K(0)], writes=["zs"])
            else:
                gi = g - 1
                u = ub[gi][s]
                uo = ub[gi][1 - s]
                p.op("dve", lambda e, u=u, uo=uo: e.tensor_copy(u[:, 0:3], uo[:, TCH:TCH + 3]),
                     reads=[f"ub{gi}_{1 - s}"], writes=[f"ub{gi}_{s}"])
                p.op("dve", lambda e, u=u, pp=pp: e.tensor_copy(u[:, 3:3 + TCH], pp[:, :]),
                     reads=[K(g % 2)], writes=[f"ub{gi}_{s}"])
                p.op("dve", lambda e, u=u, gi=gi: e.tensor_scalar(acc[:], u[:, 0:TCH], cw[:, gi, 0:1], None, ALU.mult),
                     reads=[f"ub{gi}_{s}", "cw"], writes=["acc"])
                for i in range(1, 4):
                    p.op("dve", lambda e, u=u, gi=gi, i=i: e.scalar_tensor_tensor(acc[:], u[:, i:i + TCH], cw[:, gi, i:i + 1], acc[:], ALU.mult, ALU.add),
                         reads=[f"ub{gi}_{s}", "cw", "acc"], writes=["acc"])
                dst, dk = ((xT, "xT"), (BT, "BT"), (CT, "CT"))[gi]
                p.act(dst[:], acc[:], AF.Silu, reads=["acc", "cb"], writes=[dk], bias=cb[:, gi:gi + 1])
                if gi == 0:
                    p.op("pool", lambda e: e.tensor_copy(xTb[:], xT[:]), reads=["xT"], writes=["xTb"])
        for j in range(4):
            for c in range(NCH):
                p.mm(ps[6][:, 2 * j:2 * j + 2], hc[s][:, c, j * 128:(j + 1) * 128], wdt_sb[:, c, :], start=(c == 0), stop=(c == NCH - 1),
                     reads=[f"hc{s}", "wdt"], writes=[K(6)])
        p.op("dve", lambda e: e.tensor_tensor(zt[:], ps[6][:, 0:8].rearrange("p (j h) -> p j h", h=2), dtb4[:], ALU.add),
             reads=[K(6), "dtb4"], writes=["zt"])
        p.act(zt[:], zt[:], AF.Exp, reads=["zt"], writes=["zt"])
        p.act(dt[:], zt[:], AF.Ln, reads=["zt"], writes=["dt"], bias=1.0)
        p.op("dve", lambda e: e.tensor_tensor(a[:], dt[:], negA4[:], ALU.mult), reads=["dt", "negA4"], writes=["a"])
        af = a[:].rearrange("p j h -> p (j h)")
        p.mm(ps[6][:, 16:24], tri[:], af, reads=["tri", "a"], writes=[K(6)])
        p.mm(ps[6][:, 32:40], ones_f[:], af, reads=["ones_f", "a"], writes=[K(6)])
        Wv = ps[6][:, 16:24].rearrange("p (j h) -> p j h", h=2)
        Bv = ps[6][:, 32:40].rearrange("p (c a h) -> p c a h", a=2, h=2)
        p.op("dve", lambda e: e.tensor_copy(acum[:], Wv), reads=[K(6)], writes=["acum"])
        acv = acum[:].rearrange("p (c a) h -> p c a h", a=2)
        p.op("dve", lambda e: e.tensor_tensor(acv[:, :, 1, :], acv[:, :, 1, :], Bv[:, :, 0, :], ALU.add),
             reads=["acum", K(6)], writes=["acum"])
        p.op("dve", lambda e: e.tensor_copy(last[:], Bv[:, :, 0, :]), reads=[K(6)], writes=["last"])
        p.op("dve", lambda e: e.tensor_tensor(last[:], last[:], Bv[:, :, 1, :], ALU.add), reads=["last", K(6)], writes=["last"])
        for ch in range(2):
            for a2 in range(2):
                p.op("dve", lambda e, ch=ch, a2=a2: e.tensor_tensor(dd[:, 2 * ch + a2, :], last[:, ch, :], acum[:, 2 * ch + a2, :], ALU.subtract),
                     reads=["last", "acum"], writes=["dd"])
        p.act(dte[:], dd[:], AF.Exp, reads=["dd"], writes=["dte"])
        p.act(cd[:], last[:], AF.Exp, reads=["last"], writes=["cd"])
        p.op("dve", lambda e: e.tensor_tensor(dte[:], dte[:], dt[:], ALU.mult), reads=["dte", "dt"], writes=["dte"])
        p.op("dve", lambda e: e.tensor_scalar(nac[:], acum[:], -1.0, None, ALU.mult), reads=["acum"], writes=["nac"])
        for j in range(4):
            p.tr(psB[:, j * 128:(j + 1) * 128], xTb[:, j * 128:(j + 1) * 128], ident[:], reads=["xTb", "ident"], writes=["psB"])
            p.tr(psB[:, 512 + j * 128:512 + (j + 1) * 128], BT[:, j * 128:(j + 1) * 128], ident[:], reads=["BT", "ident"], writes=["psB"])
        p.op("act", lambda e: e.copy(Btok[:].rearrange("p j n -> p (j n)"), psB[:, 512:1024]), reads=["psB"], writes=["Btok"])
        for j in range(4):
            for h in range(2):
                hs = slice(h * 64, (h + 1) * 64)
                p.op("dve", lambda e, j=j, h=h, hs=hs: e.tensor_scalar(xdtp[h][:, j, hs], psB[:, j * 128 + h * 64:j * 128 + (h + 1) * 64],
                                                                       dt[:, j, h:h + 1], None, ALU.mult),
                     reads=["psB", "dt"], writes=[f"xdtp{h}"])
                p.op("dve", lambda e, j=j, h=h, hs=hs: e.tensor_scalar(xdd[:, j, hs], psB[:, j * 128 + h * 64:j * 128 + (h + 1) * 64],
                                                                       dte[:, j, h:h + 1], None, ALU.mult),
                     reads=["psB", "dte"], writes=["xdd"])
        for ch in range(2):
            c0 = ch * 256
            p.mm(ps[2][:, 0:256], BT[:, c0:c0 + 128], CT[:, c0:c0 + 256], reads=["BT", "CT"], writes=[K(2)])
            p.mm(ps[3][:, 0:128], BT[:, c0 + 128:c0 + 256], CT[:, c0 + 128:c0 + 256], reads=["BT", "CT"], writes=[K(3)])
            p.op("dve", lambda e: e.tensor_tensor(Gm[0][:, 0:256], ps[2][:, 0:256], m01[:, 0:256], ALU.mult),
                 reads=[K(2), "m01"], writes=["Gm0"])
            p.op("dve", lambda e: e.tensor_tensor(Gm[1][:, 0:128], ps[3][:, 0:128], m01[:, 0:128], ALU.mult),
                 reads=[K(3), "m01"], writes=["Gm1"])
            for h in range(2):
                for a2 in range(2):
                    j = 2 * ch + a2
                    p.op("dve", lambda e, a2=a2, j=j, h=h: e.tensor_scalar(dg[a2][:], identf[:], acum[:, j, h:h + 1], None, ALU.mult),
                         reads=["identf", "acum"], writes=[f"dg{a2}"])
                    p.mm(ps[5][:, a2 * 128:(a2 + 1) * 128], ones_f[:], dg[a2][:], reads=["ones_f", f"dg{a2}"], writes=[K(5)])
                j0, j1 = 2 * ch, 2 * ch + 1
                p.op("dve", lambda e, j0=j0, h=h: e.tensor_scalar(Dc[0][:, 0:256], ps[5][:, 0:256], nac[:, j0, h:h + 1], 0.0, ALU.add, ALU.min),
                     reads=[K(5), "nac"], writes=["Dc0"])
                p.op("dve", lambda e, j1=j1, h=h: e.tensor_scalar(Dc[1][:, 0:128], ps[5][:, 128:256], nac[:, j1, h:h + 1], 0.0, ALU.add, ALU.min),
                     reads=[K(5), "nac"], writes=["Dc1"])
                p.act(LT[0][:, 0:256], Dc[0][:, 0:256], AF.Exp, reads=["Dc0"], writes=["LT0"])
                p.act(LT[1][:, 0:128], Dc[1][:, 0:128], AF.Exp, reads=["Dc1"], writes=["LT1"])
                p.act(Eb[:], ps[5][:, 0:256], AF.Exp, reads=[K(5)], writes=["Eb"])
                p.op("dve", lambda e: e.tensor_tensor(WT[0][:, 0:256], Gm[0][:, 0:256], LT[0][:, 0:256], ALU.mult),
                     reads=["Gm0", "LT0"], writes=["WT0"])
                p.op("dve", lambda e: e.tensor_tensor(WT[1][:, 0:128], Gm[1][:, 0:128], LT[1][:, 0:128], ALU.mult),
                     reads=["Gm1", "LT1"], writes=["WT1"])
                p.op("dve", lambda e, h=h, c0=c0: e.tensor_tensor(CdT[h][:], CT[:, c0:c0 + 256], Eb[:], ALU.mult),
                     reads=["CT", "Eb"], writes=[f"CdT{h}"])
                p.mm(ps[4][:, 0:256], hpad[h][:], CdT[h][:], start=(h == 0), stop=False,
                     reads=[f"hpad{h}", f"CdT{h}"], writes=[K(4)])
                p.mm(ps[4][:, 0:256], xdtp[h][:, j0, :], WT[0][:, 0:256], start=False, stop=False,
                     reads=[f"xdtp{h}", "WT0"], writes=[K(4)])
                p.mm(ps[4][:, 128:256], xdtp[h][:, j1, :], WT[1][:, 0:128], start=False, stop=(h == 1),
                     reads=[f"xdtp{h}", "WT1"], writes=[K(4)])
            p.mm(ps[6][:, 128:256], Btok[:, j0, :], xdd[:, j0, :], start=True, stop=False, reads=["Btok", "xdd"], writes=[K(6)])
            p.mm(ps[6][:, 128:256], Btok[:, j1, :], xdd[:, j1, :], start=False, stop=True, reads=["Btok", "xdd"], writes=[K(6)])
            for h in range(2):
                hs = slice(h * 64, (h + 1) * 64)
                p.op("dve", lambda e, h=h, hs=hs, ch=ch: e.scalar_tensor_tensor(hst[:, hs], hst[:, hs], cd[:, ch, h:h + 1],
                                                                               ps[6][:, 128 + h * 64:128 + (h + 1) * 64], ALU.mult, ALU.add),
                     reads=["hst", "cd", K(6)], writes=["hst"])
                p.op("dve", lambda e, h=h, hs=hs: e.tensor_copy(hpad[h][:, hs], hst[:, hs]), reads=["hst"], writes=[f"hpad{h}"])
            o = yst[ch]
            p.op("dve", lambda e, c0=c0: e.scalar_tensor_tensor(t1[:], xT[:, c0:c0 + 256], dcol[:, 0:1], ps[4][:, 0:256], ALU.mult, ALU.add),
                 reads=["xT", "dcol", K(4)], writes=["t1"])
            p.op("dve", lambda e, o=o, c0=c0: e.tensor_tensor(o[:], t1[:], zs[:, c0:c0 + 256], ALU.mult),
                 reads=["t1", "zs"], writes=[f"yst{ch}"])
            p.dma("sp", yT[:, tc * TCH + c0:tc * TCH + c0 + 256], o[:], reads=[f"yst{ch}"], is_out=True)


def ssd_inputs(hT_b, inp, l, j):
    w_in_l = inp["w_in"][l]
    g = j // 2
    hs = [2 * j, 2 * j + 1]
    cz = slice(j * 128, (j + 1) * 128)
    cx = slice(512 + j * 128, 512 + (j + 1) * 128)
    cB = slice(512 + 512 + g * 128, 512 + 512 + (g + 1) * 128)
    cC = slice(512 + 768 + g * 128, 512 + 768 + (g + 1) * 128)
    m = dict(ssd_consts())
    m["hT"] = hT_b
    m["wfm"] = np.ascontiguousarray(np.concatenate([w_in_l[:, cz], w_in_l[:, cx], w_in_l[:, cB], w_in_l[:, cC]], axis=1))
    m["wdt"] = np.ascontiguousarray(w_in_l[:, 1536 + 2 * j:1536 + 2 * j + 2])
    chans = [slice(j * 128, (j + 1) * 128), slice(512 + g * 128, 512 + (g + 1) * 128), slice(768 + g * 128, 768 + (g + 1) * 128)]
    cwl = inp["conv_w"][l]
    cbl = inp["conv_b"][l]
    m["cw"] = np.ascontiguousarray(np.stack([cwl[:, c].T for c in chans], axis=1))
    m["cb"] = np.ascontiguousarray(np.stack([cbl[c] for c in chans], axis=1))
    m["dtb"] = np.ascontiguousarray(inp["dt_bias"][l][hs])
    m["alog"] = np.ascontiguousarray(inp["a_log"][l][hs])
    m["dcol"] = np.ascontiguousarray(np.repeat(inp["d_skip"][l][hs], 64)[:, None])
    return m


D = 1024
NTOK = 2048
NCH = 8
DFF = 2816
NFF = DFF // 128
EPS = 1e-6


def load_cast(p, dst, src, C, N, key, piece=None, stg=None, engines=("pool", "dve")):
    k = 0
    for c in range(C):
        for n0 in range(0, N, 2048):
            n1 = min(N, n0 + 2048)
            st = stg[k % len(stg)]
            sk = f"stg{k % len(stg)}"
            p.dma("sp", st[:, 0:n1 - n0], src[:, c, n0:n1], writes=[sk])
            eng = engines[k % len(engines)]
            p.op(eng, lambda e, st=st, c=c, n0=n0, n1=n1: e.tensor_copy(dst[:, c, n0:n1], st[:, 0:n1 - n0]),
                 reads=[sk], writes=[key])
            k += 1


def rms_tile(p, xt, xk, nwbc, nwk, hb, hbk, junk, ss, rs, tag):
    p.act(junk[:], xt[:], AF.Square, accum_out=ss[:], reads=[xk], writes=["junk", f"ss{tag}"])
    p.op("dve", lambda e: e.tensor_scalar(rs[:], ss[:], 1.0 / D, EPS, ALU.mult, ALU.add), reads=[f"ss{tag}"], writes=[f"rs{tag}"])
    p.act(rs[:], rs[:], AF.Sqrt, reads=[f"rs{tag}"], writes=[f"rs{tag}"])
    p.op("dve", lambda e: e.reciprocal(rs[:], rs[:]), reads=[f"rs{tag}"], writes=[f"rs{tag}"])
    p.op("dve", lambda e: e.scalar_tensor_tensor(hb[:], xt[:], rs[:, 0:1], nwbc[:], ALU.mult, ALU.mult),
         reads=[xk, f"rs{tag}", nwk], writes=[hbk])


def build_c1():
    p = Prog("c1")
    c1_phase(p)
    return p.emit()


def c1_phase(p):
    hT = p.dram("hT", [D, NTOK], BF16)
    x = p.dram("x", [NTOK, D], F32)
    yT = p.dram("yT", [2048, NTOK], F32)
    wg = p.dram("wg", [D, 4096], F32)
    wbr = p.dram("wbr", [2048, D], F32)
    wo = p.dram("wo", [D, D], F32)
    snw = p.dram("snw", [128, 4], F32)
    ident_d = p.dram("ident", [128, 128], BF16)
    x1 = p.dram("x1", [NTOK, D], F32, kind="ExternalOutput")

    wg_sb = p.sb([128, NCH, 4096], BF16)
    wbr_sb = p.sb([128, 16, D], BF16)
    wo_sb = p.sb([128, NCH, D], BF16)
    stg = [p.sb([128, 2048], F32) for _ in range(2)]
    ident = p.sb([128, 128], BF16)
    snw_sb = p.sb([128, 4], F32)
    ones_f = p.sb([128, 2], F32)
    hTt = [p.sb([128, NCH, 128], BF16) for _ in range(2)]
    yTt = [p.sb([128, 16, 128], F32) for _ in range(2)]
    yTb = p.sb([128, 16, 128], BF16)
    ysq = p.sb([128, 4, 128], F32)
    gates = p.sb([128, 4096], BF16)
    u = p.sb([128, D], F32)
    tmp = [p.sb([128, 512], F32) for _ in range(2)]
    ub = p.sb([128, D], BF16)
    uT = p.sb([128, NCH, 128], BF16)
    xt = [p.sb([128, D], F32) for _ in range(2)]
    rstd = p.sb([128, 2], F32)
    ps = [p.ps([128, 512], F32) for _ in range(6)]
    psS = p.ps([128, 512], F32)
    psB = p.ps([128, 1024], BF16)

    p.dma("sp", ident[:], ident_d, writes=["ident"])
    p.dma("sp", snw_sb[:], snw, writes=["snw"])
    p.op("pool", lambda e: e.memset(ones_f[:], 1.0), writes=["ones_f"])
    load_cast(p, wg_sb, wg.rearrange("(c p) n -> p c n", p=128), NCH, 4096, "wg", piece=128, stg=stg)
    load_cast(p, wbr_sb, wbr.rearrange("(c p) n -> p c n", p=128), 16, D, "wbr", piece=128, stg=stg)
    load_cast(p, wo_sb, wo.rearrange("(c p) n -> p c n", p=128), NCH, D, "wo", piece=128, stg=stg)

    hT_v = hT.rearrange("(c p) t -> p c t", p=128)
    yT_v = yT.rearrange("(c p) t -> p c t", p=128)
    pi = 0
    for t in range(NTOK // 128):
        s = t % 2
        ts_ = slice(t * 128, (t + 1) * 128)
        p.dma("sp", hTt[s][:], hT_v[:, :, ts_], writes=[f"hTt{s}"])
        p.dma("sp", yTt[s][:], yT_v[:, :, ts_], writes=[f"yTt{s}"])
        p.dma("sp", xt[s][:], x[ts_, :], writes=[f"xt{s}"])
        for fc in range(4):
            p.op("dve", lambda e, s=s, fc=fc: e.tensor_scalar(yTb[:, fc, :], yTt[s][:, fc, :], snw_sb[:, fc:fc + 1], None, ALU.mult),
                 reads=[f"yTt{s}", "snw"], writes=["yTb"])
        p.op("pool", lambda e, s=s: e.tensor_copy(yTb[:, 4:16, :], yTt[s][:, 4:16, :]), reads=[f"yTt{s}"], writes=["yTb"])
        p.act(ysq[:], yTt[s][:, 0:4, :], AF.Square, reads=[f"yTt{s}"], writes=["ysq"])
        for fc in range(4):
            p.mm(psS[:, 0:2], ysq[:, fc, :], ones_f[:], start=(fc == 0), stop=(fc == 3), reads=["ysq", "ones_f"], writes=["psS"])
        p.op("dve", lambda e: e.tensor_scalar(rstd[:], psS[:, 0:2], 1.0 / 512, EPS, ALU.mult, ALU.add), reads=["psS"], writes=["rstd"])
        p.act(rstd[:], rstd[:], AF.Sqrt, reads=["rstd"], writes=["rstd"])
        p.op("dve", lambda e: e.reciprocal(rstd[:], rstd[:]), reads=["rstd"], writes=["rstd"])
        for n in range(8):
            pp = ps[pi % 6]; pk = f"ps{pi % 6}"; pi += 1
            for c in range(NCH):
                p.mm(pp[:, :], hTt[s][:, c, :], wg_sb[:, c, n * 512:(n + 1) * 512], start=(c == 0), stop=(c == NCH - 1),
                     reads=[f"hTt{s}", "wg"], writes=[pk])
            p.act(gates[:, n * 512:(n + 1) * 512], pp[:, :], AF.Sigmoid, reads=[pk], writes=["gates"])
        for i in range(4):
            for hf in range(2):
                pp = ps[pi % 6]; pk = f"ps{pi % 6}"; pi += 1
                for fc in range(4):
                    p.mm(pp[:, :], yTb[:, i * 4 + fc, :], wbr_sb[:, i * 4 + fc, hf * 512:(hf + 1) * 512], start=(fc == 0), stop=(fc == 3),
                         reads=["yTb", "wbr"], writes=[pk])
                gsl = gates[:, i * 1024 + hf * 512:i * 1024 + (hf + 1) * 512]
                usl = u[:, hf * 512:(hf + 1) * 512]
                if i == 0:
                    p.op("dve", lambda e, pp=pp, gsl=gsl, usl=usl: e.scalar_tensor_tensor(usl, pp[:, :], rstd[:, 0:1], gsl, ALU.mult, ALU.mult),
                         reads=[pk, "rstd", "gates"], writes=["u"])
                else:
                    tm = tmp[hf]
                    p.op("dve", lambda e, pp=pp, gsl=gsl, tm=tm: e.tensor_tensor(tm[:], pp[:, :], gsl, ALU.mult),
                         reads=[pk, "gates"], writes=[f"tmp{hf}"])
                    p.op("pool", lambda e, tm=tm, usl=usl: e.tensor_tensor(usl, usl, tm[:], ALU.add),
                         reads=[f"tmp{hf}", "u"], writes=["u"])
        p.op("pool", lambda e: e.tensor_copy(ub[:], u[:]), reads=["u"], writes=["ub"])
        for c in range(NCH):
            p.tr(psB[:, c * 128:(c + 1) * 128], ub[:, c * 128:(c + 1) * 128], ident[:], reads=["ub", "ident"], writes=["psB"])
        p.op("act", lambda e: e.copy(uT[:].rearrange("p c t -> p (c t)"), psB[:, :]), reads=["psB"], writes=["uT"])
        for hf in range(2):
            pp = ps[pi % 6]; pk = f"ps{pi % 6}"; pi += 1
            for c in range(NCH):
                p.mm(pp[:, :], uT[:, c, :], wo_sb[:, c, hf * 512:(hf + 1) * 512], start=(c == 0), stop=(c == NCH - 1),
                     reads=["uT", "wo"], writes=[pk])
            p.op("dve", lambda e, pp=pp, s=s, hf=hf: e.tensor_tensor(xt[s][:, hf * 512:(hf + 1) * 512], pp[:, :], xt[s][:, hf * 512:(hf + 1) * 512], ALU.add),
                 reads=[pk, f"xt{s}"], writes=[f"xt{s}"])
        p.dma("sp", x1[ts_, :], xt[s][:], reads=[f"xt{s}"], is_out=True)


def build_c2():
    p = Prog("c2")
    c2_phase(p)
    return p.emit()


def c2_phase(p):
    x1 = p.dram("x1", [NTOK, D], F32)
    nw = p.dram("nw", [D], F32)
    nwf = p.dram("nwf", [D], F32)
    wgt = p.dram("wgt", [D, DFF], F32)
    wup = p.dram("wup", [D, DFF], F32)
    wdn = p.dram("wdn", [DFF, D], F32)
    ident_d = p.dram("ident", [128, 128], BF16)
    x2 = p.dram("x2", [NTOK, D], F32, kind="ExternalOutput")
    xn = p.dram("xn", [NTOK, D], F32, kind="ExternalOutput")

    wg_sb = p.sb([128, NCH, DFF], BF16)
    wu_sb = p.sb([128, NCH, DFF], BF16)
    wd_sb = p.sb([128, NFF, D], BF16)
    stg = [p.sb([128, 2048], F32) for _ in range(1)]
    ident = p.sb([128, 128], BF16)
    nwbc = p.sb([128, D], F32)
    nwfbc = p.sb([128, D], F32)
    xt = [p.sb([128, D], F32) for _ in range(3)]
    junk = p.sb([128, D], BF16)
    hb = p.sb([128, D], BF16)
    ss = p.sb([128, 1], F32)
    rs = p.sb([128, 1], F32)
    ss2 = p.sb([128, 1], F32)
    rs2 = p.sb([128, 1], F32)
    h2T = p.sb([128, NCH, 256], BF16)
    aT = p.sb([128, NFF, 256], BF16)
    sg = [p.sb([128, 256], F32) for _ in range(2)]
    xo = [p.sb([128, D], F32) for _ in range(1)]
    ps = [p.ps([128, 512], F32) for _ in range(6)]
    psB = p.ps([128, 1024], BF16)

    p.dma("sp", ident[:], ident_d, writes=["ident"])
    p.dma("sp", nwbc[:], nw.partition_broadcast(128), writes=["nwbc"])
    p.dma("sp", nwfbc[:], nwf.partition_broadcast(128), writes=["nwfbc"])
    load_cast(p, wg_sb, wgt.rearrange("(c p) n -> p c n", p=128), NCH, DFF, "wgt", piece=128, stg=stg)
    load_cast(p, wu_sb, wup.rearrange("(c p) n -> p c n", p=128), NCH, DFF, "wup", piece=128, stg=stg)
    load_cast(p, wd_sb, wdn.rearrange("(c p) n -> p c n", p=128), NFF, D, "wdn", piece=64, stg=stg)

    pi = 0
    oi = 0
    for g in range(NTOK // 256):
        for j in range(2):
            t = g * 2 + j
            xs = xt[t % 3]
            xk = f"xt{t % 3}"
            p.dma("sp", xs[:], x1[t * 128:(t + 1) * 128, :], writes=[xk])
            rms_tile(p, xs, xk, nwbc, "nwbc", hb, "hb", junk, ss, rs, "a")
            for c in range(NCH):
                p.tr(psB[:, c * 128:(c + 1) * 128], hb[:, c * 128:(c + 1) * 128], ident[:], reads=["hb", "ident"], writes=["psB"])
            p.op("act", lambda e, j=j: e.copy(h2T[:, :, j * 128:(j + 1) * 128], psB[:, :].rearrange("p (c t) -> p c t", c=NCH)),
                 reads=["psB"], writes=["h2T"])
        for f in range(NFF):
            pa = ps[pi % 6]; pak = f"ps{pi % 6}"; pi += 1
            pb = ps[pi % 6]; pbk = f"ps{pi % 6}"; pi += 1
            for c in range(NCH):
                p.mm(pa[:, 0:256], wg_sb[:, c, f * 128:(f + 1) * 128], h2T[:, c, :], start=(c == 0), stop=(c == NCH - 1),
                     reads=["wgt", "h2T"], writes=[pak])
            for c in range(NCH):
                p.mm(pb[:, 0:256], wu_sb[:, c, f * 128:(f + 1) * 128], h2T[:, c, :], start=(c == 0), stop=(c == NCH - 1),
                     reads=["wup", "h2T"], writes=[pbk])
            sgt = sg[f % 2]
            p.act(sgt[:], pa[:, 0:256], AF.Silu, reads=[pak], writes=[f"sg{f % 2}"])
            p.op("dve", lambda e, pb=pb, sgt=sgt, f=f: e.tensor_tensor(aT[:, f, :], pb[:, 0:256], sgt[:], ALU.mult),
                 reads=[pbk, f"sg{f % 2}"], writes=["aT"])
        for j in range(2):
            t = g * 2 + j
            xs = xt[t % 3]
            xk = f"xt{t % 3}"
            o = xo[0]; ok = "xo0"; oi += 1
            for hf in range(2):
                pp = ps[pi % 6]; pk = f"ps{pi % 6}"; pi += 1
                for f in range(NFF):
                    p.mm(pp[:, :], aT[:, f, j * 128:(j + 1) * 128], wd_sb[:, f, hf * 512:(hf + 1) * 512], start=(f == 0), stop=(f == NFF - 1),
                         reads=["aT", "wdn"], writes=[pk])
                p.op("dve", lambda e, pp=pp, xs=xs, hf=hf: e.tensor_tensor(xs[:, hf * 512:(hf + 1) * 512], pp[:, :], xs[:, hf * 512:(hf + 1) * 512], ALU.add),
                     reads=[pk, xk], writes=[xk])
            p.dma("sp", x2[t * 128:(t + 1) * 128, :], xs[:], reads=[xk], is_out=True)
            p.act(junk[:], xs[:], AF.Square, accum_out=ss2[:], reads=[xk], writes=["junk", "ss2"])
            p.op("dve", lambda e: e.tensor_scalar(rs2[:], ss2[:], 1.0 / D, EPS, ALU.mult, ALU.add), reads=["ss2"], writes=["rs2"])
            p.act(rs2[:], rs2[:], AF.Sqrt, reads=["rs2"], writes=["rs2"])
            p.op("dve", lambda e: e.reciprocal(rs2[:], rs2[:]), reads=["rs2"], writes=["rs2"])
            p.op("dve", lambda e, o=o, xs=xs: e.scalar_tensor_tensor(o[:], xs[:], rs2[:, 0:1], nwfbc[:], ALU.mult, ALU.mult),
                 reads=[xk, "rs2", "nwfbc"], writes=[ok])
            p.dma("sp", xn[t * 128:(t + 1) * 128, :], o[:], reads=[ok], is_out=True)


MIX = ((0, "ssd"), (1, "moba"), (2, "fox"), (3, "swa"))


def build_ab():
    p = Prog("ab")
    hT_i = p.nc.dram_tensor("hT_scr", [D, S], BF16).ap()
    yT_all = p.nc.dram_tensor("yT", [4, 128, S], F32, kind="ExternalOutput").ap()
    p.dp, p.dov = "n_", {"hT": hT_i}
    p.begin_phase("norm")
    norm_phase(p, S, D)
    p.end_phase()
    for bi, mode in MIX:
        p.dp, p.dov = mode + "_", {"hT": hT_i, "yT": yT_all[bi]}
        p.begin_phase(mode)
        if mode == "ssd":
            ssd_phase(p)
        else:
            attn_phase(p, mode)
        p.end_phase()
    return p.emit()


def ab_inputs(x_b, inp, l, j):
    m = {"n_x": x_b, "n_nw": inp["norm_mix"][l], "n_ident": np.eye(128).astype(NPBF)}
    extra = dict(forget_bias=inp["forget_bias"][l], rel_bias=inp["rel_bias"], sinks=inp["sinks"][l])
    for bi, mode in MIX:
        sub = ssd_inputs(None, inp, l, j) if mode == "ssd" else attn_inputs(mode, None, inp["w_in"][l], j, extra)
        for k, v in sub.items():
            if k != "hT":
                m[f"{mode}_{k}"] = v
    return m


def build_cd():
    p = Prog("cd")
    hT_i = p.nc.dram_tensor("hT_scr", [D, NTOK], BF16).ap()
    x1_i = p.nc.dram_tensor("x1_scr", [NTOK, D], F32).ap()
    x_in = p.dram("x", [NTOK, D], F32)
    p.dp, p.dov = "n_", {"hT": hT_i, "x": x_in}
    p.begin_phase("norm")
    norm_phase(p, NTOK, D)
    p.end_phase()
    p.dp, p.dov = "c1_", {"hT": hT_i, "x": x_in, "x1": x1_i}
    p.begin_phase("c1")
    c1_phase(p)
    p.end_phase()
    p.dp, p.dov = "c2_", {"x1": x1_i}
    p.begin_phase("c2")
    c2_phase(p)
    p.end_phase()
    return p.emit()


def cd_inputs(x_slab, yT_slab, inp, l):
    ident = np.eye(128).astype(NPBF)
    return {
        "x": x_slab, "n_nw": inp["norm_mix"][l], "n_ident": ident,
        "c1_yT": yT_slab, "c1_wg": np.ascontiguousarray(inp["w_in"][l][:, 5392:]),
        "c1_wbr": np.ascontiguousarray(inp["w_branch"][l].reshape(2048, D)), "c1_wo": inp["w_out"][l],
        "c1_snw": np.ascontiguousarray(inp["ssm_norm_w"][l].reshape(4, 128).T), "c1_ident": ident,
        "c2_nw": inp["norm_ffn"][l], "c2_nwf": inp["norm_final"], "c2_wgt": inp["w_ffn_gate"][l],
        "c2_wup": inp["w_ffn_up"][l], "c2_wdn": inp["w_ffn_down"][l], "c2_ident": ident,
    }


N_CORES = 8


def _run(nc, in_maps):
    return run_bass_kernel_spmd(nc, in_maps, core_ids=list(range(N_CORES))).results


def kernel(x, w_in, conv_w, conv_b, dt_bias, a_log, d_skip, ssm_norm_w, forget_bias, sinks, rel_bias,
           w_branch, w_out, norm_mix, norm_ffn, w_ffn_gate, w_ffn_up, w_ffn_down, norm_final):
    f32 = lambda a: np.ascontiguousarray(np.asarray(a, dtype=np.float32))
    inp = dict(x=f32(x), w_in=f32(w_in), conv_w=f32(conv_w), conv_b=f32(conv_b), dt_bias=f32(dt_bias), a_log=f32(a_log),
               d_skip=f32(d_skip), ssm_norm_w=f32(ssm_norm_w), forget_bias=f32(forget_bias), sinks=f32(sinks),
               rel_bias=f32(rel_bias), w_branch=f32(w_branch), w_out=f32(w_out), norm_mix=f32(norm_mix),
               norm_ffn=f32(norm_ffn), w_ffn_gate=f32(w_ffn_gate), w_ffn_up=f32(w_ffn_up), w_ffn_down=f32(w_ffn_down),
               norm_final=f32(norm_final))
    xcur = inp["x"]
    xn = None
    for l in range(2):
        res = _run(build_ab(), [ab_inputs(np.ascontiguousarray(xcur[c // 4]), inp, l, c % 4) for c in range(N_CORES)])
        yT_b = [np.concatenate([np.asarray(res[b * 4 + j]["yT"])[bi] for bi in range(4) for j in range(4)], axis=0)
                for b in range(2)]
        maps = []
        for c in range(N_CORES):
            b, sl = c // 4, slice((c % 4) * NTOK, (c % 4 + 1) * NTOK)
            maps.append(cd_inputs(np.ascontiguousarray(xcur[b, sl]), np.ascontiguousarray(yT_b[b][:, sl]), inp, l))
        res = _run(build_cd(), maps)
        xcur = np.stack([np.concatenate([np.asarray(res[b * 4 + q]["c2_x2"]) for q in range(4)], axis=0) for b in range(2)])
        xn = np.stack([np.concatenate([np.asarray(res[b * 4 + q]["c2_xn"]) for q in range(4)], axis=0) for b in range(2)])
    return np.ascontiguousarray(xn.astype(np.float32))
```

```python
import numpy as np
import ml_dtypes
from contextlib import ExitStack
import concourse.bass as bass
import concourse.mybir as mybir
from concourse.bass_utils import run_bass_kernel_spmd

F32 = mybir.dt.float32
BF16 = mybir.dt.bfloat16
AF = mybir.ActivationFunctionType
ALU = mybir.AluOpType
AX = mybir.AxisListType
NPBF = ml_dtypes.bfloat16

import os
DBG = bool(os.environ.get('FWDBG'))
ENGS = ("pe", "act", "dve", "pool", "sp")


class Prog:
    NDS = 24

    def __init__(self, name="k"):
        self.nc = bass.Bass("TRN2", target_bir_lowering=False)
        self.es = ExitStack()
        self.ops = {e: [] for e in ENGS}
        self.lastw = {}
        self.readers = {}
        self.dma_rr = 0
        self.dma_val = [0] * self.NDS
        self.out_tokens = []
        self._n = 0
        self.phase_es = None
        self.kp = ""
        self.dp = ""
        self.dov = {}

    def dram(self, name, shape, dt, kind="ExternalInput"):
        if name in self.dov:
            return self.dov[name]
        return self.nc.dram_tensor(self.dp + name, list(shape), dt, kind=kind).ap()

    def sb(self, shape, dt, name=None):
        self._n += 1
        es = self.phase_es if self.phase_es is not None else self.es
        return es.enter_context(self.nc.sbuf_tensor(name or f"sb{self._n}", list(shape), dt))

    def ps(self, shape, dt, name=None):
        self._n += 1
        es = self.phase_es if self.phase_es is not None else self.es
        return es.enter_context(self.nc.psum_tensor(name or f"ps{self._n}", list(shape), dt))

    def begin_phase(self, tag):
        assert self.phase_es is None
        self.phase_es = ExitStack()
        self.kp = tag + ":"

    def end_phase(self):
        self.barrier()
        self.phase_es.close()
        self.phase_es = None
        self.kp = ""

    def barrier(self):
        toks = set()
        for e in ENGS:
            n = 0
            last = None
            for i, o in enumerate(self.ops[e]):
                if o["fn"] is not None and o["dma"] is None:
                    last = i
            if last is not None:
                toks.add(("e", e, last))
        for k in range(self.NDS):
            if self.dma_val[k] > 0:
                toks.add(("d", k, self.dma_val[k]))
        for e in ENGS:
            deps = set(t for t in toks if not (t[0] == "e" and t[1] == e))
            self.ops[e].append(dict(fn=None, deps=deps, dma=None))

    def _deps(self, eng, reads, writes):
        raw, war = set(), set()
        for k in reads:
            t = self.lastw.get(k)
            if t is not None:
                raw.add(t)
            if k.split(":")[-1].startswith("ps"):
                rd = self.readers.get(k)
                if rd:
                    for e2, idx in rd[0].items():
                        if e2 != eng:
                            raw.add(("e", e2, idx))
        for k in writes:
            t = self.lastw.get(k)
            if t is not None:
                raw.add(t)
            rd = self.readers.get(k)
            if rd:
                for e2, idx in rd[0].items():
                    war.add(("e", e2, idx))
                for t in rd[1]:
                    war.add(t)
        deps = set()
        for t in raw:
            if t[0] == "e" and t[1] == eng and eng == "pe":
                continue
            deps.add(t)
        for t in war:
            if t[0] == "e" and t[1] == eng:
                continue
            deps.add(t)
        return deps

    def _commit(self, tok, reads, writes):
        for k in reads:
            rd = self.readers.setdefault(k, [{}, []])
            if tok[0] == "e":
                rd[0][tok[1]] = tok[2]
            else:
                rd[1].append(tok)
        for k in writes:
            self.lastw[k] = tok
            self.readers[k] = [{}, []]

    def _k(self, keys):
        return [k if k.startswith("g:") else self.kp + k for k in keys]

    def op(self, eng, fn, reads=(), writes=()):
        reads, writes = self._k(reads), self._k(writes)
        deps = self._deps(eng, reads, writes)
        idx = len(self.ops[eng])
        tok = ("e", eng, idx)
        self.ops[eng].append(dict(fn=fn, deps=deps, dma=None))
        self._commit(tok, reads, writes)
        return tok

    def dma(self, eng, out, in_, reads=(), writes=(), is_out=False, **kw):
        reads, writes = self._k(reads), self._k(writes)
        deps = self._deps(eng, reads, writes)
        k = self.dma_rr
        self.dma_rr = (self.dma_rr + 1) % self.NDS
        prev = self.dma_val[k]
        self.dma_val[k] = prev + 16
        tok = ("d", k, prev + 16)
        fn = lambda e, out=out, in_=in_, kw=kw: e.dma_start(out=out, in_=in_, **kw)
        self.ops[eng].append(dict(fn=fn, deps=deps, dma=(k, prev)))
        self._commit(tok, reads, writes)
        if is_out:
            self.out_tokens.append(tok)
        return tok

    def cc(self, kind, in_ap, out_ap, groups, reads=(), writes=()):
        deps = self._deps("pool", reads, writes)
        k = self.dma_rr
        self.dma_rr = (self.dma_rr + 1) % self.NDS
        prev = self.dma_val[k]
        self.dma_val[k] = prev + 16
        tok = ("d", k, prev + 16)
        fn = lambda e: e.collective_compute(kind, ALU.bypass, groups, [in_ap], [out_ap])
        self.ops["pool"].append(dict(fn=fn, deps=deps, dma=(k, prev)))
        self._commit(tok, reads, writes)
        return tok

    def emit(self):
        nc = self.nc
        self.ops["sp"].append(dict(fn=None, deps=set(self.out_tokens), dma=None))
        ms = {e: set() for e in ENGS}
        for e in ENGS:
            for o in self.ops[e]:
                for t in o["deps"]:
                    if t[0] == "e":
                        ms[t[1]].add(t[2])
        rank = {e: {idx: i + 1 for i, idx in enumerate(sorted(ms[e]))} for e in ENGS}
        esem = {e: self.es.enter_context(nc.semaphore(f"s_{e}")) for e in ENGS}
        dsem = [self.es.enter_context(nc.semaphore(f"d_{i}")) for i in range(self.NDS)]
        ops = self.ops

        def run(e, eng):
            seen = {}
            for idx, o in enumerate(ops[e]):
                need = {}
                for t in o["deps"]:
                    if t[0] == "e":
                        key = ("e", t[1])
                        c = rank[t[1]][t[2]]
                    else:
                        key = ("d", t[1])
                        c = t[2]
                    if c > need.get(key, 0):
                        need[key] = c
                if o["dma"] is not None:
                    k, prev = o["dma"]
                    if prev > need.get(("d", k), 0):
                        need[("d", k)] = prev
                for key, c in need.items():
                    if c > seen.get(key, 0):
                        s = esem[key[1]] if key[0] == "e" else dsem[key[1]]
                        eng.wait_ge(s, c)
                        seen[key] = c
                        if DBG: print(f"[{e}] #{idx} wait {key} >= {c}")
                if o["fn"] is None:
                    continue
                ins = o["fn"](eng)
                if DBG: print(f"[{e}] #{idx} issue dma={o['dma']} inc={idx in ms[e]} rank={rank[e].get(idx)}")
                if o["dma"] is not None:
                    ins.then_inc(dsem[o["dma"][0]], 16)
                elif idx in ms[e]:
                    ins.then_inc(esem[e], 1)

        with nc.Block() as block:
            @block.tensor
            def _(eng):
                run("pe", eng)

            @block.scalar
            def _(eng):
                run("act", eng)

            @block.vector
            def _(eng):
                run("dve", eng)

            @block.gpsimd
            def _(eng):
                run("pool", eng)

            @block.sync
            def _(eng):
                run("sp", eng)
        self.es.close()
        return nc

    def mm(self, out, lhsT, rhs, start=True, stop=True, reads=(), writes=()):
        return self.op("pe", lambda e: e.matmul(out, lhsT, rhs, start=start, stop=stop), reads, writes)

    def tr(self, out, in_, ident, reads=(), writes=()):
        return self.op("pe", lambda e: e.transpose(out, in_, ident), reads, writes)

    def act(self, out, in_, func, reads=(), writes=(), **kw):
        return self.op("act", lambda e: e.activation(out, in_, func, **kw), reads, writes)


EPS = 1e-6


def norm_rows(p, x_ap_fn, nwbc, ident, hT_sb, ntiles, D, tag, xt_keys=None):
    pass


def build_norm(NT=2048, D=1024):
    p = Prog("norm")
    norm_phase(p, NT, D)
    return p.emit()


def norm_phase(p, NT=2048, D=1024):
    x = p.dram("x", [NT, D], F32)
    nw = p.dram("nw", [D], F32)
    ident_d = p.dram("ident", [128, 128], BF16)
    hT = p.dram("hT", [D, NT], BF16, kind="ExternalOutput")
    nch = D // 128
    ntiles = NT // 128
    xt = [p.sb([128, D], F32) for _ in range(2)]
    junk = p.sb([128, D], BF16)
    hb = [p.sb([128, D], BF16) for _ in range(2)]
    nwbc = p.sb([128, D], F32)
    ident = p.sb([128, 128], BF16)
    ss = [p.sb([128, 1], F32) for _ in range(2)]
    rs = [p.sb([128, 1], F32) for _ in range(2)]
    hT_sb = p.sb([128, nch, NT], BF16)
    pst = [p.ps([128, nch * 128], BF16) for _ in range(2)]

    p.dma("sp", nwbc[:], nw.partition_broadcast(128), writes=["nwbc"])
    p.dma("sp", ident[:], ident_d, writes=["ident"])
    for i in range(ntiles):
        s = i % 2
        p.dma("sp", xt[s][:], x[i * 128:(i + 1) * 128, :], writes=[f"xt{s}"])
        p.act(junk[:], xt[s][:], AF.Square, accum_out=ss[s][:], reads=[f"xt{s}"], writes=["junk", f"ss{s}"])
        p.op("dve", lambda e, s=s: e.tensor_scalar(rs[s][:], ss[s][:], 1.0 / D, EPS, ALU.mult, ALU.add),
             reads=[f"ss{s}"], writes=[f"rs{s}"])
        p.act(rs[s][:], rs[s][:], AF.Sqrt, reads=[f"rs{s}"], writes=[f"rs{s}"])
        p.op("dve", lambda e, s=s: e.reciprocal(rs[s][:], rs[s][:]), reads=[f"rs{s}"], writes=[f"rs{s}"])
        p.op("dve", lambda e, s=s: e.scalar_tensor_tensor(hb[s][:], xt[s][:], rs[s][:, 0:1], nwbc[:], ALU.mult, ALU.mult),
             reads=[f"xt{s}", f"rs{s}", "nwbc"], writes=[f"hb{s}"])
        for c in range(nch):
            p.tr(pst[s][:, c * 128:(c + 1) * 128], hb[s][:, c * 128:(c + 1) * 128], ident[:],
                 reads=[f"hb{s}", "ident"], writes=[f"pst{s}"])
        p.op("act", lambda e, s=s, i=i: e.copy(hT_sb[:, :, i * 128:(i + 1) * 128],
                                                 pst[s][:].rearrange("p (c t) -> p c t", c=nch)),
             reads=[f"pst{s}"], writes=["hT_sb"])
    p.dma("sp", hT.rearrange("(c p) t -> p c t", p=128), hT_sb[:], reads=["hT_sb"], is_out=True)


import os
VAR = os.environ.get('VAR', '')

S = 8192
D = 1024
NCH = 8
TCH = 512
NTC = S // TCH
NT = S // 128
NEG = -30000.0
MOBA_W = 1792
MOBA_DMIN = -384
SWA_W = 1024


def t5_bucket(d):
    d = np.asarray(d)
    dd = np.maximum(d, 1).astype(np.float32)
    large = 16 + (np.log(dd / np.float32(16)) / np.float32(np.log(1024 / 16)) * np.float32(16)).astype(np.int32)
    large = np.minimum(large, 31)
    return np.where(d < 16, np.maximum(d, 0), large)


def attn_consts(mode):
    c = {}
    c["ident"] = np.eye(128, dtype=NPBF)
    c["identf"] = np.eye(128, dtype=np.float32)
    if mode == "fox":
        c["tri"] = np.triu(np.ones((128, 128), np.float32))
        k = np.arange(128)[:, None]
        q = np.arange(512)[None, :]
        m = np.stack([np.where(q >= k + 128 * d, 0.0, NEG) for d in range(4)], 1)
        c["maskT"] = m.astype(NPBF)
    elif mode == "swa":
        c["jrev"] = np.eye(128, dtype=NPBF)[::-1].copy()
        n = SWA_W + 127
        dist = np.arange(n) - 127 + MOBA_DMIN
        oh = np.zeros((33, n), np.float32)
        b = t5_bucket(dist)
        for i in range(n):
            if 0 <= dist[i] < 128:
                oh[b[i], i] = 1.0
            else:
                oh[32, i] = NEG
        c["oh"] = oh
        kind = np.zeros((2, S), np.float32)
        kind[0, :] = 1.0
        c["kind"] = kind.astype(NPBF)
        vs = np.zeros((2, 2, 128), np.float32)
        vs[0, 0, 64:128] = 1.0
        vs[1, 1, 0:64] = 1.0
        c["vsink"] = vs.astype(NPBF)
    else:
        c["jrev"] = np.eye(128, dtype=NPBF)[::-1].copy()
        n = MOBA_W + 127
        dist = np.arange(n) - 127 + MOBA_DMIN
        oh = np.zeros((33, n), np.float32)
        b = t5_bucket(dist)
        for i in range(n):
            if dist[i] >= 0:
                oh[b[i], i] += 1.0
                oh[31, i] -= 1.0
            else:
                oh[32, i] = NEG
        c["oh"] = oh
        ind = np.zeros((33, S), np.float32)
        for j in range(32):
            ind[j, j * 256:(j + 1) * 256] = 1.0
        ind[32, :] = 1.0
        c["kind"] = ind.astype(NPBF)
    return c


def build_attn(mode, stage=9):
    p = Prog(mode)
    attn_phase(p, mode)
    return p.emit()


def attn_phase(p, mode, stage=9):
    fox = mode == "fox"
    swa = mode == "swa"
    UW = SWA_W if swa else MOBA_W
    KIN = 2 if swa else 33
    NTOK = 386 if fox else 384
    hT = p.dram("hT", [D, S], BF16)
    wqk = p.dram("wqk", [D, 256], F32)
    wtok = p.dram("wtok", [D, NTOK], F32)
    ident_d = p.dram("ident", [128, 128], BF16)
    identf_d = p.dram("identf", [128, 128], F32)
    yT = p.dram("yT", [128, S], F32, kind="ExternalOutput")
    if fox:
        fb_d = p.dram("fb", [2], F32)
        tri_d = p.dram("tri", [128, 128], F32)
        maskT_d = p.dram("maskT", [128, 4, 512], BF16)
    else:
        jrev_d = p.dram("jrev", [128, 128], BF16)
        oh_d = p.dram("oh", [33, UW + 127], F32)
        tab_d = p.dram("tab", [33, 2], F32)
        kind_d = p.dram("kind", [KIN, S], BF16)
        vec_d = p.dram("vecscr", [2, UW + 127], F32, kind="Internal")
        if swa:
            ksink_d = p.dram("ksink", [66, 2], F32)

    hT_v = hT.rearrange("(c p) t -> p c t", p=128)
    hc = [p.sb([128, NCH, TCH], BF16) for _ in range(2)]
    wqk_sb = p.sb([128, NCH, 256], BF16)
    wtok_sb = p.sb([128, NCH, NTOK], BF16)
    ident = p.sb([128, 128], BF16)
    identf = p.sb([128, 128], F32)
    QTa = [p.sb([128, S], BF16) for _ in range(2)]
    KTa = [p.sb([128, S], BF16) for _ in range(2)]
    V = p.sb([128, NT, 2, 128], BF16)
    nrm = p.sb([128, NT, 4], F32)
    sq = [p.sb([128, 256], F32) for _ in range(2)]
    NA = 96 if fox else (2 if swa else 33)
    A = [p.sb([128, NT, NA], BF16) for _ in range(2)]
    pT = [p.sb([128, TCH], BF16) for _ in range(4)]
    rl = [p.sb([128, TCH], F32) for _ in range(2)]
    yst = [p.sb([128, TCH], F32) for _ in range(2)]
    ones_f = p.sb([128, 128], F32)
    sbnd = p.sb([128, NT, 2], F32)
    small = p.sb([128, 64], F32)
    psA = [p.ps([128, 512], F32) for _ in range(2)]
    psO = [p.ps([128, 512], F32) for _ in range(2)]
    psT = [p.ps([128, 512], F32) for _ in range(2)]
    psB = p.ps([128, 1024], BF16)
    psM = p.ps([128, 512], F32)

    p.dma("sp", ident[:], ident_d, writes=["ident"])
    p.dma("sp", identf[:], identf_d, writes=["identf"])
    stg = p.sb([128, NCH, NTOK], F32)
    p.dma("sp", stg[:, :, 0:256], wqk.rearrange("(c p) n -> p c n", p=128), writes=["stg"])
    p.op("pool", lambda e: e.tensor_copy(wqk_sb[:], stg[:, :, 0:256]), reads=["stg"], writes=["wqk"])
    p.dma("sp", stg[:], wtok.rearrange("(c p) n -> p c n", p=128), writes=["stg"])
    p.op("pool", lambda e: e.tensor_copy(wtok_sb[:], stg[:]), reads=["stg"], writes=["wtok"])
    p.op("pool", lambda e: e.memset(V[:], 1.0), writes=["V"])
    p.op("pool", lambda e: e.memset(ones_f[:], 1.0), writes=["ones_f"])
    for h in range(2):
        p.op("pool", lambda e, h=h: e.memset(A[h][:], 0.0), writes=[f"A{h}"])
    if fox:
        f_sb = p.sb([128, NT, 2], F32)
        tri = p.sb([128, 128], F32)
        maskT = p.sb([128, 4, 512], BF16)
        fbb = p.sb([128, 2], F32)
        p.dma("sp", tri[:], tri_d, writes=["tri"])
        p.dma("sp", maskT[:], maskT_d, writes=["maskT"])
        p.dma("sp", fbb[:], fb_d.partition_broadcast(128), writes=["fbb"])
    else:
        jrev = p.sb([128, 128], BF16)
        U = [p.sb([128, UW], BF16) for _ in range(2)]
        oh = p.sb([33, UW + 127], F32)
        tab = p.sb([33, 2], F32)
        vec_sb = p.sb([2, UW + 127], F32)
        if swa:
            ksf = p.sb([66, 2], F32)
            ksink = p.sb([66, 2], BF16)
            vsink = p.sb([2, 2, 128], BF16)
            psink = p.sb([2, TCH], BF16)
            vsink_d = p.dram("vsink", [2, 2, 128], BF16)
            p.dma("sp", vsink[:], vsink_d, writes=["vsink"])
            p.dma("sp", ksf[:], ksink_d, writes=["ksf"])
            p.op("dve", lambda e: e.tensor_copy(ksink[:], ksf[:]), reads=["ksf"], writes=["ksink"])

        kmT = [p.sb([64, 32], F32) for _ in range(2)]
        qf = [p.sb([64, TCH], F32) for _ in range(2)]
        gsb = p.sb([128, 32], F32)
        top8 = p.sb([128, 8], F32)
        mb01 = p.sb([128, 32], F32)
        p.dma("sp", jrev[:], jrev_d, writes=["jrev"])
        p.dma("sp", oh[:], oh_d, writes=["oh"])
        p.dma("sp", tab[:], tab_d, writes=["tab"])
        for h in range(2):
            p.dma("sp", KTa[h][64:64 + KIN, :], kind_d, writes=[f"KTa{h}"])
            p.op("pool", lambda e, h=h: e.memset(kmT[h][:], 0.0), writes=[f"kmT{h}"])
        W = UW + 127
        for c0 in range(0, W, 512):
            c1 = min(W, c0 + 512)
            p.mm(psM[0:2, 0:c1 - c0], tab[:, :], oh[:, c0:c1], reads=["tab", "oh"], writes=["psM"])
            p.op("dve", lambda e, c0=c0, c1=c1: e.tensor_copy(vec_sb[:, c0:c1], psM[0:2, 0:c1 - c0]),
                 reads=["psM"], writes=["vec_sb"])
        p.dma("sp", vec_d, vec_sb[:], reads=["vec_sb"], writes=["vec_d"])
        for h in range(2):
            src = bass.AP(vec_d.tensor, h * W, [[1, 128], [1, UW]])
            stgf = stg[:].rearrange("p c n -> p (c n)")
            p.dma("sp", stgf[:, 0:UW], src, reads=["vec_d"], writes=["stg"])
            p.op("pool", lambda e, h=h, stgf=stgf: e.tensor_copy(U[h][:], stgf[:, 0:UW]), reads=["stg"], writes=[f"U{h}"])

    for tc in range(NTC):
        s = tc % 2
        p.dma("sp", hc[s][:], hT_v[:, :, tc * TCH:(tc + 1) * TCH], writes=[f"hc{s}"])
        for g in range(4):
            if VAR in ('B', 'C', 'D'):
                break
            ps = psA[g % 2]
            for c in range(NCH):
                p.mm(ps[0:64, :], wqk_sb[:, c, g * 64:(g + 1) * 64], hc[s][:, c, :], start=(c == 0), stop=(c == NCH - 1),
                     reads=[f"hc{s}", "wqk"], writes=[f"psA{g % 2}"])
            h = g % 2
            cs = slice(tc * TCH, (tc + 1) * TCH)
            if g < 2:
                p.op("act", lambda e, ps=ps, h=h, cs=cs: e.mul(QTa[h][0:64, cs], ps[0:64, :], 0.125),
                     reads=[f"psA{g % 2}"], writes=[f"QTa{h}"])
                if not fox and not swa:
                    p.op("dve", lambda e, ps=ps, h=h: e.tensor_copy(qf[h][:], ps[0:64, :]),
                         reads=[f"psA{g % 2}"], writes=[f"qf{h}"])
            else:
                p.op("dve", lambda e, ps=ps, h=h, cs=cs: e.tensor_copy(KTa[h][0:64, cs], ps[0:64, :]),
                     reads=[f"psA{g % 2}"], writes=[f"KTa{h}"])
                if not fox and not swa:
                    p.op("dve", lambda e, ps=ps, h=h, tc=tc: e.tensor_reduce(
                        kmT[h][:, 2 * tc:2 * tc + 2], ps[0:64, :].rearrange("p (b t) -> p b t", b=2), AX.X, ALU.add),
                        reads=[f"psA{g % 2}"], writes=[f"kmT{h}"])
        for j in range(4):
            if VAR == 'A':
                break
            i = tc * 4 + j
            ps = psT[j % 2]
            for c in range(NCH):
                p.mm(ps[:, 0:NTOK], hc[s][:, c, j * 128:(j + 1) * 128], wtok_sb[:, c, :], start=(c == 0), stop=(c == NCH - 1),
                     reads=[f"hc{s}", "wtok"], writes=[f"psT{j % 2}"])
            if os.environ.get("EXP") == "8":
                p.act(sq[j % 2][:], ps[:, 128:384], AF.Square, reads=[f"psT{j % 2}"], writes=[f"sq{j % 2}"])
                continue
            p.op("dve", lambda e, ps=ps, i=i: e.tensor_copy(V[:, i, 0, 0:64], ps[:, 0:64]),
                 reads=[f"psT{j % 2}"], writes=["V"])
            p.op("dve", lambda e, ps=ps, i=i: e.tensor_copy(V[:, i, 1, 64:128], ps[:, 64:128]),
                 reads=[f"psT{j % 2}"], writes=["V"])
            if VAR == 'B':
                continue
            EXP = os.environ.get("EXP", "")
            if EXP == "1":
                p.act(sq[j % 2][:], ps[:, 0:256], AF.Square, reads=[f"psT{j % 2}"], writes=[f"sq{j % 2}"])
            elif EXP == "3":
                p.act(sq[j % 2][:], ps[:, 128:384], AF.Square, reads=[f"psT{j % 2}", "V"], writes=[f"sq{j % 2}"])
            elif EXP == "5":
                p.act(sq[j % 2][:], ps[:, 128:384], AF.Square, reads=[f"psT{j % 2}"], writes=[f"sqx{i}"])
            elif EXP == "6":
                p.op("act", lambda e, ps=ps, j=j: e.mul(sq[j % 2][:], ps[:, 128:384], 1.0), reads=[f"psT{j % 2}"], writes=[f"sq{j % 2}"])
            elif EXP == "7":
                p.op("act", lambda e, ps=ps, j=j: e.mul(sq[j % 2][0:64, :], ps[0:64, 128:384], 1.0), reads=[f"psT{j % 2}"], writes=[f"sq{j % 2}"])
            elif EXP == "4":
                p.op("dve", lambda e, ps=ps, j=j: e.tensor_copy(sq[j % 2][:], ps[:, 128:384]), reads=[f"psT{j % 2}"], writes=[f"sq{j % 2}"])
            else:
                p.act(sq[j % 2][:], ps[:, 128:384], AF.Square, reads=[f"psT{j % 2}"], writes=[f"sq{j % 2}"])
            if VAR == 'C':
                continue
            p.op("dve", lambda e, i=i, j=j: e.tensor_reduce(nrm[:, i, :], sq[j % 2][:].rearrange("p (g d) -> p g d", d=64), AX.X, ALU.add),
                 reads=[f"sq{j % 2}"], writes=["nrm"])
            if fox:
                p.op("dve", lambda e, ps=ps, i=i: e.tensor_copy(f_sb[:, i, :], ps[:, 384:386]),
                     reads=[f"psT{j % 2}"], writes=["f_sb"])
        if not fox and not swa:
            for h in range(2):
                for j in range(4):
                    i = tc * 4 + j
                    n = i // 2
                    p.mm(psM[:, 0:32], qf[h][:, j * 128:(j + 1) * 128], kmT[h][:, :], reads=[f"qf{h}", f"kmT{h}"], writes=["psM"])
                    p.op("dve", lambda e: e.tensor_copy(gsb[:], psM[:, 0:32]), reads=["psM"], writes=["gsb"])
                    p.op("dve", lambda e, n=n: e.memset(gsb[:, n:32], -1e30), writes=["gsb"])
                    p.op("dve", lambda e: e.max(top8[:], gsb[:]), reads=["gsb"], writes=["top8"])
                    p.op("dve", lambda e: e.tensor_scalar(mb01[:], gsb[:], top8[:, 2:3], 1.0, ALU.is_ge, ALU.subtract),
                         reads=["gsb", "top8"], writes=["mb01"])
                    p.op("dve", lambda e, h=h, i=i: e.tensor_scalar(A[h][:, i, 0:32], mb01[:], -NEG, None, ALU.mult),
                         reads=["mb01"], writes=[f"A{h}"])
                    p.op("dve", lambda e, h=h, i=i, n=n: e.memset(A[h][:, i, n:n + 1], 0.0), writes=[f"A{h}"])

    kmx = small[:, 0:2]
    p.op("dve", lambda e: e.tensor_reduce(kmx, nrm[:, :, 2:4].rearrange("p i g -> p g i"), AX.X, ALU.max),
         reads=["nrm"], writes=["small"])
    p.tr(psM[0:2, 0:128], kmx, identf[:], reads=["small", "identf"], writes=["psM"])
    kmx2 = small[0:2, 8:9]
    p.op("dve", lambda e: e.tensor_reduce(kmx2, psM[0:2, 0:128], AX.X, ALU.max), reads=["psM"], writes=["small2"])
    kbr = small[0:2, 16:48]
    krow = p.sb([2, 128], F32)
    p.op("dve", lambda e: e.tensor_scalar(krow[:], ones_f[0:2, :], kmx2, None, ALU.mult),
         reads=["small2", "ones_f"], writes=["krow"])
    p.mm(psM[:, 0:2], krow[:], identf[0:2, 0:2], reads=["krow", "identf"], writes=["psM"])
    kbc = small[:, 4:6]
    p.op("dve", lambda e: e.tensor_copy(kbc, psM[:, 0:2]), reads=["psM"], writes=["kbc"])
    for h in range(2):
        p.op("dve", lambda e, h=h: e.tensor_scalar(sbnd[:, :, h], nrm[:, :, h], small[:, 4 + h:5 + h], None, ALU.mult),
             reads=["nrm", "kbc"], writes=["sbnd"])
    p.act(sbnd[:], sbnd[:], AF.Sqrt, reads=["sbnd"], writes=["sbnd"], scale=(0.125 * 1.02) ** 2)

    if fox:
        nfb = p.sb([128, 2], F32)
        e1 = p.sb([128, NT, 2], F32)
        g = p.sb([128, NT, 2], F32)
        incl = p.sb([128, NT, 2], F32)
        G = p.sb([128, NT, 2], F32)
        r1 = p.sb([128, NT, 2], F32)
        Gh = p.sb([128, NT, 2], BF16)
        Gm = p.sb([128, NT, 2], BF16)
        Gl = p.sb([128, NT, 2], BF16)
        p.op("dve", lambda e: e.tensor_scalar(nfb[:], fbb[:], -1.0, None, ALU.mult), reads=["fbb"], writes=["nfb"])
        for h in range(2):
            p.act(e1[:, :, h], f_sb[:, :, h], AF.Exp, reads=["f_sb", "nfb"], writes=["e1"], scale=-1.0, bias=nfb[:, h:h + 1])
        p.act(g[:], e1[:], AF.Ln, reads=["e1"], writes=["g"], bias=1.0)
        gf = g[:].rearrange("p i h -> p (i h)")
        p.mm(psM[:, 0:128], tri[:], gf, reads=["tri", "g"], writes=["psM"])
        p.mm(psM[:, 128:256], ones_f[:], gf, reads=["ones_f", "g"], writes=["psM"])
        W_v = psM[:, 0:128].rearrange("p (i h) -> p i h", h=2)
        B_v = psM[:, 128:256].rearrange("p (i h) -> p i h", h=2)
        for h in range(2):
            p.op("dve", lambda e, h=h: e.tensor_tensor_scan(incl[:, :, h], ones_f[:, 0:NT], B_v[:, :, h], 0.0, ALU.mult, ALU.add),
                 reads=["psM", "ones_f"], writes=["incl"])
        p.op("dve", lambda e: e.tensor_tensor(r1[:], incl[:], B_v, ALU.subtract), reads=["incl", "psM"], writes=["r1"])
        p.op("dve", lambda e: e.tensor_tensor(G[:], r1[:], W_v, ALU.add), reads=["r1", "psM"], writes=["G"])
        p.op("dve", lambda e: e.tensor_copy(Gh[:], G[:]), reads=["G"], writes=["Gh"])
        p.op("dve", lambda e: e.tensor_tensor(r1[:], G[:], Gh[:], ALU.subtract), reads=["G", "Gh"], writes=["r1"])
        p.op("dve", lambda e: e.tensor_copy(Gm[:], r1[:]), reads=["r1"], writes=["Gm"])
        p.op("dve", lambda e: e.tensor_tensor(r1[:], r1[:], Gm[:], ALU.subtract), reads=["r1", "Gm"], writes=["r1"])
        p.op("dve", lambda e: e.tensor_copy(Gl[:], r1[:]), reads=["r1"], writes=["Gl"])
        for h in range(2):
            Ah = A[h]
            for col in (0, 1, 2, 35, 36, 37, 38):
                p.op("pool", lambda e, Ah=Ah, col=col: e.memset(Ah[:, :, col:col + 1], 1.0), writes=[f"A{h}"])
            for k3, Gx in enumerate((Gh, Gm, Gl)):
                p.op("dve", lambda e, Ah=Ah, Gx=Gx, k3=k3, h=h: e.tensor_scalar(Ah[:, :, 3 + k3], Gx[:, :, h], -1.0, None, ALU.mult),
                     reads=["Gh", "Gm", "Gl"], writes=[f"A{h}"])
                p.op("dve", lambda e, Ah=Ah, Gx=Gx, k3=k3, h=h: e.tensor_copy(Ah[:, :, 32 + k3], Gx[:, :, h]),
                     reads=["Gh", "Gm", "Gl"], writes=[f"A{h}"])
            p.op("dve", lambda e, Ah=Ah, h=h: e.tensor_scalar(Ah[:, :, 6], sbnd[:, :, h], -1.0, None, ALU.mult),
                 reads=["sbnd"], writes=[f"A{h}"])
    elif swa:
        for h in range(2):
            p.op("dve", lambda e, h=h: e.tensor_scalar(A[h][:, :, 0], sbnd[:, :, h], -1.0, None, ALU.mult),
                 reads=["sbnd"], writes=[f"A{h}"])
            p.op("dve", lambda e, h=h: e.memset(A[h][:, :, 1:2], 1.0), writes=[f"A{h}"])
    else:
        for h in range(2):
            p.op("dve", lambda e, h=h: e.tensor_scalar(A[h][:, :, 32], sbnd[:, :, h], -1.0, None, ALU.mult),
                 reads=["sbnd"], writes=[f"A{h}"])

    for h in range(2):
        for g8 in range(NT // 8):
            for t in range(8):
                i = g8 * 8 + t
                p.tr(psB[0:NA, t * 128:(t + 1) * 128], A[h][:, i, :], ident[:], reads=[f"A{h}", "ident"], writes=["psB"])
            cs = slice(g8 * 1024, (g8 + 1) * 1024)
            if fox:
                p.op("dve", lambda e, h=h, cs=cs: e.tensor_copy(QTa[h][64:71, cs], psB[0:7, :]), reads=["psB"], writes=[f"QTa{h}"])
                p.op("act", lambda e, h=h, cs=cs: e.copy(KTa[h][64:71, cs], psB[32:39, :]), reads=["psB"], writes=[f"KTa{h}"])
            else:
                p.op("dve", lambda e, h=h, cs=cs: e.tensor_copy(QTa[h][64:64 + NA, cs], psB[0:NA, :]), reads=["psB"], writes=[f"QTa{h}"])

    KR = 71 if fox else (66 if swa else 97)
    SB = [(psA[0], "psA0"), (psA[1], "psA1"), (psT[0], "psT0"), (psT[1], "psT1")]
    NB = len(SB)
    LOOK = 2
    jobs = []
    oi = 0
    for h in range(2):
        for qc in range(NTC):
            nk = 4 * qc + 4
            kt0 = max(0, 4 * qc - 1) if swa else 0
            for kt in range(kt0, nk):
                jobs.append((h, qc, kt, kt == kt0, kt == nk - 1, oi))
            oi += 1

    def s_stage(i):
        h, qc, kt, first, last, oi = jobs[i]
        pa, pak = SB[i % NB]
        qs = slice(qc * TCH, (qc + 1) * TCH)
        ks = slice(kt * 128, (kt + 1) * 128)
        d0 = qc * TCH - kt * 128
        extra = (kt >= 4 * qc) if fox else (swa or d0 <= 896)
        p.mm(pa[:, :], KTa[h][0:KR, ks], QTa[h][0:KR, qs], start=True, stop=not extra,
             reads=[f"KTa{h}", f"QTa{h}"], writes=[pak])
        if extra:
            if fox:
                p.mm(pa[:, :], ident[:], maskT[:, kt - 4 * qc, :], start=False, stop=True,
                     reads=["ident", "maskT"], writes=[pak])
            else:
                off = d0 - MOBA_DMIN
                p.mm(pa[:, :], jrev[:], U[h][:, off:off + TCH], start=False, stop=True,
                     reads=["jrev", f"U{h}"], writes=[pak])

    def pv_stage(i):
        h, qc, kt, first, last, oi = jobs[i]
        pa, pak = SB[i % NB]
        pt = pT[i % len(pT)]
        ptk = f"pT{i % len(pT)}"
        po = psO[oi % 2]
        pok = f"psO{oi % 2}"
        qs = slice(qc * TCH, (qc + 1) * TCH)
        p.act(pt[:], pa[:, :], AF.Exp, reads=[pak], writes=[ptk])
        p.mm(po[:, :], V[:, kt, h, :], pt[:], start=first, stop=(last and not swa),
             reads=["V", ptk], writes=[pok])
        if not last:
            return
        if swa:
            p.mm(psM[0:2, :], ksink[:, 0:2], QTa[h][0:66, qs], reads=["ksink", f"QTa{h}"], writes=["psM"])
            p.act(psink[:], psM[0:2, :], AF.Exp, reads=["psM"], writes=["psink"])
            p.mm(po[:, :], vsink[:, h, :], psink[:], start=False, stop=True, reads=["vsink", "psink"], writes=[pok])
        r = rl[oi % 2]
        y = yst[oi % 2]
        if h == 0:
            num, den = slice(0, 64), slice(64, 128)
        else:
            num, den = slice(64, 128), slice(0, 64)
        p.op("dve", lambda e: e.reciprocal(r[den, :], po[den, :]), reads=[pok], writes=[f"rl{oi % 2}"])
        p.op("dve", lambda e: e.tensor_tensor(y[num, :], po[num, :], r[den, :], ALU.mult),
             reads=[pok, f"rl{oi % 2}"], writes=[f"yst{oi % 2}"])
        p.dma("sp", yT[num, qs], y[num, :], reads=[f"yst{oi % 2}"], is_out=True)

    for i in range(len(jobs) + LOOK):
        if i < len(jobs):
            s_stage(i)
        if i - LOOK >= 0:
            pv_stage(i - LOOK)


def attn_inputs(mode, hT_b, w_in_l, j, extra):
    off_moba = 512 + 1024 + 8
    off_fox = off_moba + 1536
    off_f = off_fox + 1536
    base = off_fox if mode == "fox" else off_moba
    hs = [2 * j, 2 * j + 1]
    if mode == "swa":
        off_q = off_f + 8
        off_kv = off_q + 512
        kv = j // 2
        q = [w_in_l[:, off_q + h * 64: off_q + (h + 1) * 64] for h in hs]
        k = [w_in_l[:, off_kv + kv * 64: off_kv + (kv + 1) * 64]] * 2
        v = [w_in_l[:, off_kv + 128 + kv * 64: off_kv + 128 + (kv + 1) * 64]] * 2
    else:
        q = [w_in_l[:, base + h * 64: base + (h + 1) * 64] for h in hs]
        k = [w_in_l[:, base + 512 + h * 64: base + 512 + (h + 1) * 64] for h in hs]
        v = [w_in_l[:, base + 1024 + h * 64: base + 1024 + (h + 1) * 64] for h in hs]
    m = dict(attn_consts(mode))
    m["hT"] = hT_b
    m["wqk"] = np.ascontiguousarray(np.concatenate(q + k, axis=1))
    if mode == "fox":
        f = [w_in_l[:, off_f + h: off_f + h + 1] for h in hs]
        m["wtok"] = np.ascontiguousarray(np.concatenate(v + q + k + f, axis=1))
        m["fb"] = np.ascontiguousarray(extra["forget_bias"][hs])
    else:
        m["wtok"] = np.ascontiguousarray(np.concatenate(v + q + k, axis=1))
        tab = extra["rel_bias"][:, hs] if mode == "moba" else extra["rel_bias"][:, [8 + hh for hh in hs]]
        if mode == "swa":
            ks = np.zeros((66, 2), np.float32)
            ks[64, :] = 1.0
            ks[65, :] = extra["sinks"][hs]
            m["ksink"] = ks
        m["tab"] = np.ascontiguousarray(np.concatenate([tab, np.ones((1, 2), np.float32)], axis=0))
    return m


S = 8192
D = 1024
NCH = 8
TCH = 512
NTC = S // TCH
NEG = -30000.0


def ssd_consts():
    c = {}
    c["ident"] = np.eye(128, dtype=NPBF)
    c["identf"] = np.eye(128, dtype=np.float32)
    c["tri"] = np.triu(np.ones((128, 128), np.float32))
    s = np.arange(128)[:, None]
    l = np.arange(256)[None, :]
    c["m01"] = np.where(l >= s, 1.0, 0.0).astype(np.float32)
    return c


def build_ssd(nchunks=NTC):
    p = Prog("ssd")
    ssd_phase(p, nchunks)
    return p.emit()


def ssd_phase(p, nchunks=NTC):
    hT = p.dram("hT", [D, S], BF16)
    wfm = p.dram("wfm", [D, 512], F32)
    wdt = p.dram("wdt", [D, 2], F32)
    cw_d = p.dram("cw", [128, 3, 4], F32)
    cb_d = p.dram("cb", [128, 3], F32)
    dtb_d = p.dram("dtb", [2], F32)
    alog_d = p.dram("alog", [2], F32)
    dcol_d = p.dram("dcol", [128, 1], F32)
    ident_d = p.dram("ident", [128, 128], BF16)
    identf_d = p.dram("identf", [128, 128], F32)
    tri_d = p.dram("tri", [128, 128], F32)
    m01_d = p.dram("m01", [128, 256], F32)
    yT = p.dram("yT", [128, S], F32, kind="ExternalOutput")
    hT_v = hT.rearrange("(c p) t -> p c t", p=128)

    hc = [p.sb([128, NCH, TCH], BF16) for _ in range(2)]
    stg = p.sb([128, NCH, 512], F32)
    wfm_sb = p.sb([128, NCH, 512], BF16)
    wdt_f = p.sb([128, NCH, 2], F32)
    wdt_sb = p.sb([128, NCH, 2], BF16)
    cw = p.sb([128, 3, 4], F32)
    cb = p.sb([128, 3], F32)
    dcol = p.sb([128, 1], F32)
    ident = p.sb([128, 128], BF16)
    identf = p.sb([128, 128], F32)
    tri = p.sb([128, 128], F32)
    ones_f = p.sb([128, 128], F32)
    dtb4 = p.sb([128, 4, 2], F32)
    negA4 = p.sb([128, 4, 2], F32)
    ub = [[p.sb([128, 3 + TCH], F32) for _ in range(2)] for _ in range(3)]
    acc = p.sb([128, TCH], F32)
    zs = p.sb([128, TCH], F32)
    xT = p.sb([128, TCH], F32)
    xTb = p.sb([128, TCH], BF16)
    BT = p.sb([128, TCH], BF16)
    CT = p.sb([128, TCH], BF16)
    Btok = p.sb([128, 4, 128], BF16)
    xdtp = [p.sb([128, 4, 128], BF16) for _ in range(2)]
    xdd = p.sb([128, 4, 128], BF16)
    zt = p.sb([128, 4, 2], F32)
    dt = p.sb([128, 4, 2], F32)
    a = p.sb([128, 4, 2], F32)
    acum = p.sb([128, 4, 2], F32)
    last = p.sb([128, 2, 2], F32)
    dte = p.sb([128, 4, 2], F32)
    dd = p.sb([128, 4, 2], F32)
    cd = p.sb([128, 2, 2], F32)
    nac = p.sb([128, 4, 2], F32)
    dg = [p.sb([128, 128], F32) for _ in range(2)]
    Dc = [p.sb([128, 256], F32) for _ in range(2)]
    Gm = [p.sb([128, 256], F32) for _ in range(2)]
    m01 = p.sb([128, 256], F32)
    LT = [p.sb([128, 256], F32) for _ in range(2)]
    WT = [p.sb([128, 256], BF16) for _ in range(2)]
    Eb = p.sb([128, 256], F32)
    CdT = [p.sb([128, 256], BF16) for _ in range(2)]
    hst = p.sb([128, 128], F32)
    hpad = [p.sb([128, 128], BF16) for _ in range(2)]
    t1 = p.sb([128, 256], F32)
    yst = [p.sb([128, 256], F32) for _ in range(2)]

    ps = [p.ps([128, 512], F32) for _ in range(7)]
    psB = p.ps([128, 1024], BF16)
    K = lambda i: f"ps{i}"

    for t, d_, k in ((ident, ident_d, "ident"), (identf, identf_d, "identf"), (tri, tri_d, "tri"), (m01, m01_d, "m01"),
                     (cw, cw_d, "cw"), (cb, cb_d, "cb"), (dcol, dcol_d, "dcol")):
        p.dma("sp", t[:], d_, writes=[k])
    p.dma("sp", stg[:], wfm.rearrange("(c p) n -> p c n", p=128), writes=["stg"])
    p.op("pool", lambda e: e.tensor_copy(wfm_sb[:], stg[:]), reads=["stg"], writes=["wfm"])
    p.dma("sp", wdt_f[:], wdt.rearrange("(c p) n -> p c n", p=128), writes=["wdt_f"])
    p.op("pool", lambda e: e.tensor_copy(wdt_sb[:], wdt_f[:]), reads=["wdt_f"], writes=["wdt"])
    p.op("pool", lambda e: e.memset(ones_f[:], 1.0), writes=["ones_f"])
    p.op("pool", lambda e: e.memset(hst[:], 0.0), writes=["hst"])
    for h in range(2):
        p.op("pool", lambda e, h=h: e.memset(hpad[h][:], 0.0), writes=[f"hpad{h}"])
        p.op("pool", lambda e, h=h: e.memset(xdtp[h][:], 0.0), writes=[f"xdtp{h}"])
    for g in range(3):
        p.op("pool", lambda e, g=g: e.memset(ub[g][1][:, TCH:TCH + 3], 0.0), writes=[f"ub{g}_1"])
    for j in range(4):
        p.dma("sp", dtb4[:, j, :], dtb_d.partition_broadcast(128), writes=["dtb4"])
        p.dma("sp", negA4[:, j, :], alog_d.partition_broadcast(128), writes=["negA4"])
    p.act(negA4[:], negA4[:], AF.Exp, reads=["negA4"], writes=["negA4"])
    p.op("dve", lambda e: e.tensor_scalar(negA4[:], negA4[:], -1.0, None, ALU.mult), reads=["negA4"], writes=["negA4"])

    for tc in range(nchunks):
        s = tc % 2
        p.dma("sp", hc[s][:], hT_v[:, :, tc * TCH:(tc + 1) * TCH], writes=[f"hc{s}"])
        for g in range(4):
            pp = ps[g % 2]
            for c in range(NCH):
                p.mm(pp[:, :], wfm_sb[:, c, g * 128:(g + 1) * 128], hc[s][:, c, :], start=(c == 0), stop=(c == NCH - 1),
                     reads=[f"hc{s}", "wfm"], writes=[K(g % 2)])
            if g == 0:
                p.act(zs[:], pp[:, :], AF.Silu, reads=[K(0)], writes=["zs"])
            else:
                gi = g - 1
                u = ub[gi][s]
                uo = ub[gi][1 - s]
                p.op("dve", lambda e, u=u, uo=uo: e.tensor_copy(u[:, 0:3], uo[:, TCH:TCH + 3]),
                     reads=[f"ub{gi}_{1 - s}"], writes=[f"ub{gi}_{s}"])
                p.op("dve", lambda e, u=u, pp=pp: e.tensor_copy(u[:, 3:3 + TCH], pp[:, :]),
                     reads=[K(g % 2)], writes=[f"ub{gi}_{s}"])
                p.op("dve", lambda e, u=u, gi=gi: e.tensor_scalar(acc[:], u[:, 0:TCH], cw[:, gi, 0:1], None, ALU.mult),
                     reads=[f"ub{gi}_{s}", "cw"], writes=["acc"])
                for i in range(1, 4):
                    p.op("dve", lambda e, u=u, gi=gi, i=i: e.scalar_tensor_tensor(acc[:], u[:, i:i + TCH], cw[:, gi, i:i + 1], acc[:], ALU.mult, ALU.add),
                         reads=[f"ub{gi}_{s}", "cw", "acc"], writes=["acc"])
                dst, dk = ((xT, "xT"), (BT, "BT"), (CT, "CT"))[gi]
                p.act(dst[:], acc[:], AF.Silu, reads=["acc", "cb"], writes=[dk], bias=cb[:, gi:gi + 1])
                if gi == 0:
                    p.op("pool", lambda e: e.tensor_copy(xTb[:], xT[:]), reads=["xT"], writes=["xTb"])
        for j in range(4):
            for c in range(NCH):
                p.mm(ps[6][:, 2 * j:2 * j + 2], hc[s][:, c, j * 128:(j + 1) * 128], wdt_sb[:, c, :], start=(c == 0), stop=(c == NCH - 1),
                     reads=[f"hc{s}", "wdt"], writes=[K(6)])
        p.op("dve", lambda e: e.tensor_tensor(zt[:], ps[6][:, 0:8].rearrange("p (j h) -> p j h", h=2), dtb4[:], ALU.add),
             reads=[K(6), "dtb4"], writes=["zt"])
        p.act(zt[:], zt[:], AF.Exp, reads=["zt"], writes=["zt"])
        p.act(dt[:], zt[:], AF.Ln, reads=["zt"], writes=["dt"], bias=1.0)
        p.op("dve", lambda e: e.tensor_tensor(a[:], dt[:], negA4[:], ALU.mult), reads=["dt", "negA4"], writes=["a"])
        af = a[:].rearrange("p j h -> p (j h)")
        p.mm(ps[6][:, 16:24], tri[:], af, reads=["tri", "a"], writes=[K(6)])
        p.mm(ps[6][:, 32:40], ones_f[:], af, reads=["ones_f", "a"], writes=[K(6)])
        Wv = ps[6][:, 16:24].rearrange("p (j h) -> p j h", h=2)
        Bv = ps[6][:, 32:40].rearrange("p (c a h) -> p c a h", a=2, h=2)
        p.op("dve", lambda e: e.tensor_copy(acum[:], Wv), reads=[K(6)], writes=["acum"])
        acv = acum[:].rearrange("p (c a) h -> p c a h", a=2)
        p.op("dve", lambda e: e.tensor_tensor(acv[:, :, 1, :], acv[:, :, 1, :], Bv[:, :, 0, :], ALU.add),
             reads=["acum", K(6)], writes=["acum"])
        p.op("dve", lambda e: e.tensor_copy(last[:], Bv[:, :, 0, :]), reads=[K(6)], writes=["last"])
        p.op("dve", lambda e: e.tensor_tensor(last[:], last[:], Bv[:, :, 1, :], ALU.add), reads=["last", K(6)], writes=["last"])
        for ch in range(2):
            for a2 in range(2):
                p.op("dve", lambda e, ch=ch, a2=a2: e.tensor_tensor(dd[:, 2 * ch + a2, :], last[:, ch, :], acum[:, 2 * ch + a2, :], ALU.subtract),
                     reads=["last", "acum"], writes=["dd"])
        p.act(dte[:], dd[:], AF.Exp, reads=["dd"], writes=["dte"])
        p.act(cd[:], last[:], AF.Exp, reads=["last"], writes=["cd"])
        p.op("dve", lambda e: e.tensor_tensor(dte[:], dte[:], dt[:], ALU.mult), reads=["dte", "dt"], writes=["dte"])
        p.op("dve", lambda e: e.tensor_scalar(nac[:], acum[:], -1.0, None, ALU.mult), reads=["acum"], writes=["nac"])
        for j in range(4):
            p.tr(psB[:, j * 128:(j + 1) * 128], xTb[:, j * 128:(j + 1) * 128], ident[:], reads=["xTb", "ident"], writes=["psB"])
            p.tr(psB[:, 512 + j * 128:512 + (j + 1) * 128], BT[:, j * 128:(j + 1) * 128], ident[:], reads=["BT", "ident"], writes=["psB"])
        p.op("act", lambda e: e.copy(Btok[:].rearrange("p j n -> p (j n)"), psB[:, 512:1024]), reads=["psB"], writes=["Btok"])
        for j in range(4):
            for h in range(2):
                hs = slice(h * 64, (h + 1) * 64)
                p.op("dve", lambda e, j=j, h=h, hs=hs: e.tensor_scalar(xdtp[h][:, j, hs], psB[:, j * 128 + h * 64:j * 128 + (h + 1) * 64],
                                                                       dt[:, j, h:h + 1], None, ALU.mult),
                     reads=["psB", "dt"], writes=[f"xdtp{h}"])
                p.op("dve", lambda e, j=j, h=h, hs=hs: e.tensor_scalar(xdd[:, j, hs], psB[:, j * 128 + h * 64:j * 128 + (h + 1) * 64],
                                                                       dte[:, j, h:h + 1], None, ALU.mult),
                     reads=["psB", "dte"], writes=["xdd"])
        for ch in range(2):
            c0 = ch * 256
            p.mm(ps[2][:, 0:256], BT[:, c0:c0 + 128], CT[:, c0:c0 + 256], reads=["BT", "CT"], writes=[K(2)])
            p.mm(ps[3][:, 0:128], BT[:, c0 + 128:c0 + 256], CT[:, c0 + 128:c0 + 256], reads=["BT", "CT"], writes=[K(3)])
            p.op("dve", lambda e: e.tensor_tensor(Gm[0][:, 0:256], ps[2][:, 0:256], m01[:, 0:256], ALU.mult),
                 reads=[K(2), "m01"], writes=["Gm0"])
            p.op("dve", lambda e: e.tensor_tensor(Gm[1][:, 0:128], ps[3][:, 0:128], m01[:, 0:128], ALU.mult),
                 reads=[K(3), "m01"], writes=["Gm1"])
            for h in range(2):
                for a2 in range(2):
                    j = 2 * ch + a2
                    p.op("dve", lambda e, a2=a2, j=j, h=h: e.tensor_scalar(dg[a2][:], identf[:], acum[:, j, h:h + 1], None, ALU.mult),
                         reads=["identf", "acum"], writes=[f"dg{a2}"])
                    p.mm(ps[5][:, a2 * 128:(a2 + 1) * 128], ones_f[:], dg[a2][:], reads=["ones_f", f"dg{a2}"], writes=[K(5)])
                j0, j1 = 2 * ch, 2 * ch + 1
                p.op("dve", lambda e, j0=j0, h=h: e.tensor_scalar(Dc[0][:, 0:256], ps[5][:, 0:256], nac[:, j0, h:h + 1], 0.0, ALU.add, ALU.min),
                     reads=[K(5), "nac"], writes=["Dc0"])
                p.op("dve", lambda e, j1=j1, h=h: e.tensor_scalar(Dc[1][:, 0:128], ps[5][:, 128:256], nac[:, j1, h:h + 1], 0.0, ALU.add, ALU.min),
                     reads=[K(5), "nac"], writes=["Dc1"])
                p.act(LT[0][:, 0:256], Dc[0][:, 0:256], AF.Exp, reads=["Dc0"], writes=["LT0"])
                p.act(LT[1][:, 0:128], Dc[1][:, 0:128], AF.Exp, reads=["Dc1"], writes=["LT1"])
                p.act(Eb[:], ps[5][:, 0:256], AF.Exp, reads=[K(5)], writes=["Eb"])
                p.op("dve", lambda e: e.tensor_tensor(WT[0][:, 0:256], Gm[0][:, 0:256], LT[0][:, 0:256], ALU.mult),
                     reads=["Gm0", "LT0"], writes=["WT0"])
                p.op("dve", lambda e: e.tensor_tensor(WT[1][:, 0:128], Gm[1][:, 0:128], LT[1][:, 0:128], ALU.mult),
                     reads=["Gm1", "LT1"], writes=["WT1"])
                p.op("dve", lambda e, h=h, c0=c0: e.tensor_tensor(CdT[h][:], CT[:, c0:c0 + 256], Eb[:], ALU.mult),
                     reads=["CT", "Eb"], writes=[f"CdT{h}"])
                p.mm(ps[4][:, 0:256], hpad[h][:], CdT[h][:], start=(h == 0), stop=False,
                     reads=[f"hpad{h}", f"CdT{h}"], writes=[K(4)])
                p.mm(ps[4][:, 0:256], xdtp[h][:, j0, :], WT[0][:, 0:256], start=False, stop=False,
                     reads=[f"xdtp{h}", "WT0"], writes=[K(4)])
                p.mm(ps[4][:, 128:256], xdtp[h][:, j1, :], WT[1][:, 0:128], start=False, stop=(h == 1),
                     reads=[f"xdtp{h}", "WT1"], writes=[K(4)])
            p.mm(ps[6][:, 128:256], Btok[:, j0, :], xdd[:, j0, :], start=True, stop=False, reads=["Btok", "xdd"], writes=[K(6)])
            p.mm(ps[6][:, 128:256], Btok[:, j1, :], xdd[:, j1, :], start=False, stop=True, reads=["Btok", "xdd"], writes=[K(6)])
            for h in range(2):
                hs = slice(h * 64, (h + 1) * 64)
                p.op("dve", lambda e, h=h, hs=hs, ch=ch: e.scalar_tensor_tensor(hst[:, hs], hst[:, hs], cd[:, ch, h:h + 1],
                                                                               ps[6][:, 128 + h * 64:128 + (h + 1) * 64], ALU.mult, ALU.add),
                     reads=["hst", "cd", K(6)], writes=["hst"])
                p.op("dve", lambda e, h=h, hs=hs: e.tensor_copy(hpad[h][:, hs], hst[:, hs]), reads=["hst"], writes=[f"hpad{h}"])
            o = yst[ch]
            p.op("dve", lambda e, c0=c0: e.scalar_tensor_tensor(t1[:], xT[:, c0:c0 + 256], dcol[:, 0:1], ps[4][:, 0:256], ALU.mult, ALU.add),
                 reads=["xT", "dcol", K(4)], writes=["t1"])
            p.op("dve", lambda e, o=o, c0=c0: e.tensor_tensor(o[:], t1[:], zs[:, c0:c0 + 256], ALU.mult),
                 reads=["t1", "zs"], writes=[f"yst{ch}"])
            p.dma("sp", yT[:, tc * TCH + c0:tc * TCH + c0 + 256], o[:], reads=[f"yst{ch}"], is_out=True)


def ssd_inputs(hT_b, inp, l, j):
    w_in_l = inp["w_in"][l]
    g = j // 2
    hs = [2 * j, 2 * j + 1]
    cz = slice(j * 128, (j + 1) * 128)
    cx = slice(512 + j * 128, 512 + (j + 1) * 128)
    cB = slice(512 + 512 + g * 128, 512 + 512 + (g + 1) * 128)
    cC = slice(512 + 768 + g * 128, 512 + 768 + (g + 1) * 128)
    m = dict(ssd_consts())
    m["hT"] = hT_b
    m["wfm"] = np.ascontiguousarray(np.concatenate([w_in_l[:, cz], w_in_l[:, cx], w_in_l[:, cB], w_in_l[:, cC]], axis=1))
    m["wdt"] = np.ascontiguousarray(w_in_l[:, 1536 + 2 * j:1536 + 2 * j + 2])
    chans = [slice(j * 128, (j + 1) * 128), slice(512 + g * 128, 512 + (g + 1) * 128), slice(768 + g * 128, 768 + (g + 1) * 128)]
    cwl = inp["conv_w"][l]
    cbl = inp["conv_b"][l]
    m["cw"] = np.ascontiguousarray(np.stack([cwl[:, c].T for c in chans], axis=1))
    m["cb"] = np.ascontiguousarray(np.stack([cbl[c] for c in chans], axis=1))
    m["dtb"] = np.ascontiguousarray(inp["dt_bias"][l][hs])
    m["alog"] = np.ascontiguousarray(inp["a_log"][l][hs])
    m["dcol"] = np.ascontiguousarray(np.repeat(inp["d_skip"][l][hs], 64)[:, None])
    return m


D = 1024
NTOK = 2048
NCH = 8
DFF = 2816
NFF = DFF // 128
EPS = 1e-6


def load_rows(p, dst, src, C, N, key, stg, k0=0, engines=("pool", "dve"), cs=None):
    k = k0
    for c in (cs if cs is not None else range(C)):
        for n0 in range(0, N, 2048):
            n1 = min(N, n0 + 2048)
            st = stg[k % len(stg)]
            sk = f"stg{k % len(stg)}"
            p.dma("sp", st[:, 0:n1 - n0], src[:, c, n0:n1], writes=[sk])
            eng = engines[k % len(engines)]
            p.op(eng, lambda e, st=st, c=c, n0=n0, n1=n1: e.tensor_copy(dst[:, c, n0:n1], st[:, 0:n1 - n0]),
                 reads=[sk], writes=[f"{key}_r{c}"])
            k += 1
    return k


def load_block(p, dst, src, C, n0, n1, key, stg, k, engines=("pool", "dve")):
    w = n1 - n0
    st = stg[k % len(stg)]
    sk = f"stg{k % len(stg)}"
    stv = st[:, 0:C * w].rearrange("p (c n) -> p c n", c=C)
    p.dma("sp", stv, src[:, :, n0:n1], writes=[sk])
    eng = engines[k % len(engines)]
    p.op(eng, lambda e: e.tensor_copy(dst[:, :, n0:n1], stv), reads=[sk], writes=[key])
    return k + 1


def rms_tile(p, xt, xk, nwbc, nwk, hb, hbk, junk, ss, rs, tag):
    p.act(junk[:], xt[:], AF.Square, accum_out=ss[:], reads=[xk], writes=["junk", f"ss{tag}"])
    p.op("dve", lambda e: e.tensor_scalar(rs[:], ss[:], 1.0 / D, EPS, ALU.mult, ALU.add), reads=[f"ss{tag}"], writes=[f"rs{tag}"])
    p.act(rs[:], rs[:], AF.Sqrt, reads=[f"rs{tag}"], writes=[f"rs{tag}"])
    p.op("dve", lambda e: e.reciprocal(rs[:], rs[:]), reads=[f"rs{tag}"], writes=[f"rs{tag}"])
    p.op("dve", lambda e: e.scalar_tensor_tensor(hb[:], xt[:], rs[:, 0:1], nwbc[:], ALU.mult, ALU.mult),
         reads=[xk, f"rs{tag}", nwk], writes=[hbk])


def build_c1():
    p = Prog("c1")
    c1_phase(p)
    return p.emit()


def c1_phase(p):
    hT = p.dram("hT", [D, NTOK], BF16)
    x = p.dram("x", [NTOK, D], F32)
    yT = p.dram("yT", [2048, NTOK], F32)
    wg = p.dram("wg", [D, 4096], F32)
    wbr = p.dram("wbr", [2048, D], F32)
    wo = p.dram("wo", [D, D], F32)
    snw = p.dram("snw", [128, 4], F32)
    ident_d = p.dram("ident", [128, 128], BF16)
    x1 = p.dram("x1", [NTOK, D], F32, kind="ExternalOutput")

    wg_sb = p.sb([128, NCH, 4096], BF16)
    wbr_sb = p.sb([128, 16, D], BF16)
    wo_sb = p.sb([128, NCH, D], BF16)
    stg = [p.sb([128, 2048], F32) for _ in range(2)]
    ident = p.sb([128, 128], BF16)
    snw_sb = p.sb([128, 4], F32)
    ones_f = p.sb([128, 2], F32)
    hTt = [p.sb([128, NCH, 128], BF16) for _ in range(2)]
    yTt = [p.sb([128, 16, 128], F32) for _ in range(2)]
    yTb = p.sb([128, 16, 128], BF16)
    ysq = p.sb([128, 4, 128], F32)
    gates = p.sb([128, 4096], BF16)
    u = p.sb([128, D], F32)
    tmp = [p.sb([128, 512], F32) for _ in range(2)]
    ub = p.sb([128, D], BF16)
    uT = p.sb([128, NCH, 128], BF16)
    xt = [p.sb([128, D], F32) for _ in range(2)]
    rstd = p.sb([128, 2], F32)
    ps = [p.ps([128, 512], F32) for _ in range(6)]
    psS = p.ps([128, 512], F32)
    psB = p.ps([128, 1024], BF16)

    p.dma("sp", ident[:], ident_d, writes=["ident"])
    p.dma("sp", snw_sb[:], snw, writes=["snw"])
    p.op("pool", lambda e: e.memset(ones_f[:], 1.0), writes=["ones_f"])
    k = 0
    wg_v = wg.rearrange("(c p) n -> p c n", p=128)
    for b in range(16):
        k = load_block(p, wg_sb, wg_v, NCH, b * 256, (b + 1) * 256, f"wg_b{b}", stg, k)
    k = load_rows(p, wbr_sb, wbr.rearrange("(c p) n -> p c n", p=128), 16, D, "wbr", stg, k0=k)
    k = load_rows(p, wo_sb, wo.rearrange("(c p) n -> p c n", p=128), NCH, D, "wo", stg, k0=k)

    hT_v = hT.rearrange("(c p) t -> p c t", p=128)
    yT_v = yT.rearrange("(c p) t -> p c t", p=128)
    pi = 0
    for t in range(NTOK // 128):
        s = t % 2
        ts_ = slice(t * 128, (t + 1) * 128)
        p.dma("sp", hTt[s][:], hT_v[:, :, ts_], writes=[f"hTt{s}"])
        p.dma("sp", yTt[s][:], yT_v[:, :, ts_], writes=[f"yTt{s}"])
        p.dma("sp", xt[s][:], x[ts_, :], writes=[f"xt{s}"])
        for fc in range(4):
            p.op("dve", lambda e, s=s, fc=fc: e.tensor_scalar(yTb[:, fc, :], yTt[s][:, fc, :], snw_sb[:, fc:fc + 1], None, ALU.mult),
                 reads=[f"yTt{s}", "snw"], writes=["yTb"])
        p.op("pool", lambda e, s=s: e.tensor_copy(yTb[:, 4:16, :], yTt[s][:, 4:16, :]), reads=[f"yTt{s}"], writes=["yTb"])
        p.act(ysq[:], yTt[s][:, 0:4, :], AF.Square, reads=[f"yTt{s}"], writes=["ysq"])
        for fc in range(4):
            p.mm(psS[:, 0:2], ysq[:, fc, :], ones_f[:], start=(fc == 0), stop=(fc == 3), reads=["ysq", "ones_f"], writes=["psS"])
        p.op("dve", lambda e: e.tensor_scalar(rstd[:], psS[:, 0:2], 1.0 / 512, EPS, ALU.mult, ALU.add), reads=["psS"], writes=["rstd"])
        p.act(rstd[:], rstd[:], AF.Sqrt, reads=["rstd"], writes=["rstd"])
        p.op("dve", lambda e: e.reciprocal(rstd[:], rstd[:]), reads=["rstd"], writes=["rstd"])
        for n in range(8):
            pp = ps[pi % 6]; pk = f"ps{pi % 6}"; pi += 1
            for c in range(NCH):
                p.mm(pp[:, :], hTt[s][:, c, :], wg_sb[:, c, n * 512:(n + 1) * 512], start=(c == 0), stop=(c == NCH - 1),
                     reads=[f"hTt{s}", f"wg_b{2 * n}", f"wg_b{2 * n + 1}"], writes=[pk])
            p.act(gates[:, n * 512:(n + 1) * 512], pp[:, :], AF.Sigmoid, reads=[pk], writes=["gates"])
        for i in range(4):
            for hf in range(2):
                pp = ps[pi % 6]; pk = f"ps{pi % 6}"; pi += 1
                for fc in range(4):
                    p.mm(pp[:, :], yTb[:, i * 4 + fc, :], wbr_sb[:, i * 4 + fc, hf * 512:(hf + 1) * 512], start=(fc == 0), stop=(fc == 3),
                         reads=["yTb", f"wbr_r{i * 4 + fc}"], writes=[pk])
                gsl = gates[:, i * 1024 + hf * 512:i * 1024 + (hf + 1) * 512]
                usl = u[:, hf * 512:(hf + 1) * 512]
                if i == 0:
                    p.op("dve", lambda e, pp=pp, gsl=gsl, usl=usl: e.scalar_tensor_tensor(usl, pp[:, :], rstd[:, 0:1], gsl, ALU.mult, ALU.mult),
                         reads=[pk, "rstd", "gates"], writes=["u"])
                else:
                    tm = tmp[hf]
                    p.op("dve", lambda e, pp=pp, gsl=gsl, tm=tm: e.tensor_tensor(tm[:], pp[:, :], gsl, ALU.mult),
                         reads=[pk, "gates"], writes=[f"tmp{hf}"])
                    p.op("pool", lambda e, tm=tm, usl=usl: e.tensor_tensor(usl, usl, tm[:], ALU.add),
                         reads=[f"tmp{hf}", "u"], writes=["u"])
        p.op("pool", lambda e: e.tensor_copy(ub[:], u[:]), reads=["u"], writes=["ub"])
        for c in range(NCH):
            p.tr(psB[:, c * 128:(c + 1) * 128], ub[:, c * 128:(c + 1) * 128], ident[:], reads=["ub", "ident"], writes=["psB"])
        p.op("act", lambda e: e.copy(uT[:].rearrange("p c t -> p (c t)"), psB[:, :]), reads=["psB"], writes=["uT"])
        for hf in range(2):
            pp = ps[pi % 6]; pk = f"ps{pi % 6}"; pi += 1
            for c in range(NCH):
                p.mm(pp[:, :], uT[:, c, :], wo_sb[:, c, hf * 512:(hf + 1) * 512], start=(c == 0), stop=(c == NCH - 1),
                     reads=["uT", f"wo_r{c}"], writes=[pk])
            p.op("dve", lambda e, pp=pp, s=s, hf=hf: e.tensor_tensor(xt[s][:, hf * 512:(hf + 1) * 512], pp[:, :], xt[s][:, hf * 512:(hf + 1) * 512], ALU.add),
                 reads=[pk, f"xt{s}"], writes=[f"xt{s}"])
        p.dma("sp", x1[ts_, :], xt[s][:], reads=[f"xt{s}"], is_out=True)


def build_c2():
    p = Prog("c2")
    c2_phase(p)
    return p.emit()


def c2_phase(p):
    x1 = p.dram("x1", [NTOK, D], F32)
    nw = p.dram("nw", [D], F32)
    nwf = p.dram("nwf", [D], F32)
    wgt = p.dram("wgt", [D, DFF], F32)
    wup = p.dram("wup", [D, DFF], F32)
    wdn = p.dram("wdn", [DFF, D], F32)
    ident_d = p.dram("ident", [128, 128], BF16)
    x2 = p.dram("x2", [NTOK, D], F32, kind="ExternalOutput")
    xn = p.dram("xn", [NTOK, D], F32, kind="ExternalOutput")

    wg_sb = p.sb([128, NCH, DFF], BF16)
    wu_sb = p.sb([128, NCH, DFF], BF16)
    wd_sb = p.sb([128, NFF, D], BF16)
    stg = [p.sb([128, 2048], F32) for _ in range(2)]
    ident = p.sb([128, 128], BF16)
    nwbc = p.sb([128, D], F32)
    nwfbc = p.sb([128, D], F32)
    xt = [p.sb([128, D], F32) for _ in range(3)]
    junk = p.sb([128, D], BF16)
    hb = p.sb([128, D], BF16)
    ss = p.sb([128, 1], F32)
    rs = p.sb([128, 1], F32)
    ss2 = p.sb([128, 1], F32)
    rs2 = p.sb([128, 1], F32)
    h2T = p.sb([128, NCH, 256], BF16)
    aT = p.sb([128, NFF, 256], BF16)
    sg = [p.sb([128, 256], F32) for _ in range(2)]
    xo = [p.sb([128, D], F32) for _ in range(1)]
    ps = [p.ps([128, 512], F32) for _ in range(6)]
    psB = p.ps([128, 1024], BF16)

    p.dma("sp", ident[:], ident_d, writes=["ident"])
    p.dma("sp", nwbc[:], nw.partition_broadcast(128), writes=["nwbc"])
    p.dma("sp", nwfbc[:], nwf.partition_broadcast(128), writes=["nwfbc"])
    k = 0
    wgt_v = wgt.rearrange("(c p) n -> p c n", p=128)
    wup_v = wup.rearrange("(c p) n -> p c n", p=128)
    for b in range(DFF // 256):
        k = load_block(p, wg_sb, wgt_v, NCH, b * 256, (b + 1) * 256, f"wgt_b{b}", stg, k)
        k = load_block(p, wu_sb, wup_v, NCH, b * 256, (b + 1) * 256, f"wup_b{b}", stg, k)
    k = load_rows(p, wd_sb, wdn.rearrange("(c p) n -> p c n", p=128), NFF, D, "wdn", stg, k0=k)

    pi = 0
    oi = 0
    for g in range(NTOK // 256):
        for j in range(2):
            t = g * 2 + j
            xs = xt[t % 3]
            xk = f"xt{t % 3}"
            p.dma("sp", xs[:], x1[t * 128:(t + 1) * 128, :], writes=[xk])
            rms_tile(p, xs, xk, nwbc, "nwbc", hb, "hb", junk, ss, rs, "a")
            for c in range(NCH):
                p.tr(psB[:, c * 128:(c + 1) * 128], hb[:, c * 128:(c + 1) * 128], ident[:], reads=["hb", "ident"], writes=["psB"])
            p.op("act", lambda e, j=j: e.copy(h2T[:, :, j * 128:(j + 1) * 128], psB[:, :].rearrange("p (c t) -> p c t", c=NCH)),
                 reads=["psB"], writes=["h2T"])
        for f in range(NFF):
            pa = ps[pi % 6]; pak = f"ps{pi % 6}"; pi += 1
            pb = ps[pi % 6]; pbk = f"ps{pi % 6}"; pi += 1
            for c in range(NCH):
                p.mm(pa[:, 0:256], wg_sb[:, c, f * 128:(f + 1) * 128], h2T[:, c, :], start=(c == 0), stop=(c == NCH - 1),
                     reads=[f"wgt_b{f // 2}", "h2T"], writes=[pak])
            for c in range(NCH):
                p.mm(pb[:, 0:256], wu_sb[:, c, f * 128:(f + 1) * 128], h2T[:, c, :], start=(c == 0), stop=(c == NCH - 1),
                     reads=[f"wup_b{f // 2}", "h2T"], writes=[pbk])
            sgt = sg[f % 2]
            p.act(sgt[:], pa[:, 0:256], AF.Silu, reads=[pak], writes=[f"sg{f % 2}"])
            p.op("dve", lambda e, pb=pb, sgt=sgt, f=f: e.tensor_tensor(aT[:, f, :], pb[:, 0:256], sgt[:], ALU.mult),
                 reads=[pbk, f"sg{f % 2}"], writes=["aT"])
        for j in range(2):
            t = g * 2 + j
            xs = xt[t % 3]
            xk = f"xt{t % 3}"
            o = xo[0]; ok = "xo0"; oi += 1
            for hf in range(2):
                pp = ps[pi % 6]; pk = f"ps{pi % 6}"; pi += 1
                for f in range(NFF):
                    p.mm(pp[:, :], aT[:, f, j * 128:(j + 1) * 128], wd_sb[:, f, hf * 512:(hf + 1) * 512], start=(f == 0), stop=(f == NFF - 1),
                         reads=["aT", f"wdn_r{f}"], writes=[pk])
                p.op("dve", lambda e, pp=pp, xs=xs, hf=hf: e.tensor_tensor(xs[:, hf * 512:(hf + 1) * 512], pp[:, :], xs[:, hf * 512:(hf + 1) * 512], ALU.add),
                     reads=[pk, xk], writes=[xk])
            p.dma("sp", x2[t * 128:(t + 1) * 128, :], xs[:], reads=[xk], is_out=True)
            p.act(junk[:], xs[:], AF.Square, accum_out=ss2[:], reads=[xk], writes=["junk", "ss2"])
            p.op("dve", lambda e: e.tensor_scalar(rs2[:], ss2[:], 1.0 / D, EPS, ALU.mult, ALU.add), reads=["ss2"], writes=["rs2"])
            p.act(rs2[:], rs2[:], AF.Sqrt, reads=["rs2"], writes=["rs2"])
            p.op("dve", lambda e: e.reciprocal(rs2[:], rs2[:]), reads=["rs2"], writes=["rs2"])
            p.op("dve", lambda e, o=o, xs=xs: e.scalar_tensor_tensor(o[:], xs[:], rs2[:, 0:1], nwfbc[:], ALU.mult, ALU.mult),
                 reads=[xk, "rs2", "nwfbc"], writes=[ok])
            p.dma("sp", xn[t * 128:(t + 1) * 128, :], o[:], reads=[ok], is_out=True)


MIX = ((0, "ssd"), (1, "moba"), (2, "fox"), (3, "swa"))


def build_ab():
    p = Prog("ab")
    hT_i = p.nc.dram_tensor("hT_scr", [D, S], BF16).ap()
    yT_all = p.nc.dram_tensor("yT", [4, 128, S], F32, kind="ExternalOutput").ap()
    p.dp, p.dov = "n_", {"hT": hT_i}
    p.begin_phase("norm")
    norm_phase(p, S, D)
    p.end_phase()
    for bi, mode in MIX:
        p.dp, p.dov = mode + "_", {"hT": hT_i, "yT": yT_all[bi]}
        p.begin_phase(mode)
        if mode == "ssd":
            ssd_phase(p)
        else:
            attn_phase(p, mode)
        p.end_phase()
    return p.emit()


def ab_inputs(x_b, inp, l, j):
    m = {"n_x": x_b, "n_nw": inp["norm_mix"][l], "n_ident": np.eye(128).astype(NPBF)}
    extra = dict(forget_bias=inp["forget_bias"][l], rel_bias=inp["rel_bias"], sinks=inp["sinks"][l])
    for bi, mode in MIX:
        sub = ssd_inputs(None, inp, l, j) if mode == "ssd" else attn_inputs(mode, None, inp["w_in"][l], j, extra)
        for k, v in sub.items():
            if k != "hT":
                m[f"{mode}_{k}"] = v
    return m


def build_cd():
    p = Prog("cd")
    hT_i = p.nc.dram_tensor("hT_scr", [D, NTOK], BF16).ap()
    x1_i = p.nc.dram_tensor("x1_scr", [NTOK, D], F32).ap()
    x_in = p.dram("x", [NTOK, D], F32)
    p.dp, p.dov = "n_", {"hT": hT_i, "x": x_in}
    p.begin_phase("norm")
    norm_phase(p, NTOK, D)
    p.end_phase()
    p.dp, p.dov = "c1_", {"hT": hT_i, "x": x_in, "x1": x1_i}
    p.begin_phase("c1")
    c1_phase(p)
    p.end_phase()
    p.dp, p.dov = "c2_", {"x1": x1_i}
    p.begin_phase("c2")
    c2_phase(p)
    p.end_phase()
    return p.emit()


def cd_inputs(x_slab, yT_slab, inp, l):
    ident = np.eye(128).astype(NPBF)
    return {
        "x": x_slab, "n_nw": inp["norm_mix"][l], "n_ident": ident,
        "c1_yT": yT_slab, "c1_wg": np.ascontiguousarray(inp["w_in"][l][:, 5392:]),
        "c1_wbr": np.ascontiguousarray(inp["w_branch"][l].reshape(2048, D)), "c1_wo": inp["w_out"][l],
        "c1_snw": np.ascontiguousarray(inp["ssm_norm_w"][l].reshape(4, 128).T), "c1_ident": ident,
        "c2_nw": inp["norm_ffn"][l], "c2_nwf": inp["norm_final"], "c2_wgt": inp["w_ffn_gate"][l],
        "c2_wup": inp["w_ffn_up"][l], "c2_wdn": inp["w_ffn_down"][l], "c2_ident": ident,
    }


N_CORES = 8


def _run(nc, in_maps):
    return run_bass_kernel_spmd(nc, in_maps, core_ids=list(range(N_CORES))).results


def kernel(x, w_in, conv_w, conv_b, dt_bias, a_log, d_skip, ssm_norm_w, forget_bias, sinks, rel_bias,
           w_branch, w_out, norm_mix, norm_ffn, w_ffn_gate, w_ffn_up, w_ffn_down, norm_final):
    f32 = lambda a: np.ascontiguousarray(np.asarray(a, dtype=np.float32))
    inp = dict(x=f32(x), w_in=f32(w_in), conv_w=f32(conv_w), conv_b=f32(conv_b), dt_bias=f32(dt_bias), a_log=f32(a_log),
               d_skip=f32(d_skip), ssm_norm_w=f32(ssm_norm_w), forget_bias=f32(forget_bias), sinks=f32(sinks),
               rel_bias=f32(rel_bias), w_branch=f32(w_branch), w_out=f32(w_out), norm_mix=f32(norm_mix),
               norm_ffn=f32(norm_ffn), w_ffn_gate=f32(w_ffn_gate), w_ffn_up=f32(w_ffn_up), w_ffn_down=f32(w_ffn_down),
               norm_final=f32(norm_final))
    xcur = inp["x"]
    xn = None
    for l in range(2):
        res = _run(build_ab(), [ab_inputs(np.ascontiguousarray(xcur[c // 4]), inp, l, c % 4) for c in range(N_CORES)])
        yT_b = [np.concatenate([np.asarray(res[b * 4 + j]["yT"])[bi] for bi in range(4) for j in range(4)], axis=0)
                for b in range(2)]
        maps = []
        for c in range(N_CORES):
            b, sl = c // 4, slice((c % 4) * NTOK, (c % 4 + 1) * NTOK)
            maps.append(cd_inputs(np.ascontiguousarray(xcur[b, sl]), np.ascontiguousarray(yT_b[b][:, sl]), inp, l))
        res = _run(build_cd(), maps)
        xcur = np.stack([np.concatenate([np.asarray(res[b * 4 + q]["c2_x2"]) for q in range(4)], axis=0) for b in range(2)])
        xn = np.stack([np.concatenate([np.asarray(res[b * 4 + q]["c2_xn"]) for q in range(4)], axis=0) for b in range(2)])
    return np.ascontiguousarray(xn.astype(np.float32))
```

```python
import numpy as np
import ml_dtypes
from contextlib import ExitStack
import concourse.bass as bass
import concourse.mybir as mybir
from concourse.bass_utils import run_bass_kernel_spmd

F32 = mybir.dt.float32
BF16 = mybir.dt.bfloat16
AF = mybir.ActivationFunctionType
ALU = mybir.AluOpType
AX = mybir.AxisListType
NPBF = ml_dtypes.bfloat16

import os
DBG = bool(os.environ.get('FWDBG'))
ENGS = ("pe", "act", "dve", "pool", "sp")


class Prog:
    NDS = 24

    def __init__(self, name="k"):
        self.nc = bass.Bass("TRN2", target_bir_lowering=False)
        self.es = ExitStack()
        self.ops = {e: [] for e in ENGS}
        self.lastw = {}
        self.readers = {}
        self.dma_rr = 0
        self.dma_val = [0] * self.NDS
        self.out_tokens = []
        self._n = 0
        self.phase_es = None
        self.kp = ""
        self.dp = ""
        self.dov = {}

    def dram(self, name, shape, dt, kind="ExternalInput"):
        if name in self.dov:
            return self.dov[name]
        return self.nc.dram_tensor(self.dp + name, list(shape), dt, kind=kind).ap()

    def sb(self, shape, dt, name=None):
        self._n += 1
        es = self.phase_es if self.phase_es is not None else self.es
        return es.enter_context(self.nc.sbuf_tensor(name or f"sb{self._n}", list(shape), dt))

    def ps(self, shape, dt, name=None):
        self._n += 1
        es = self.phase_es if self.phase_es is not None else self.es
        return es.enter_context(self.nc.psum_tensor(name or f"ps{self._n}", list(shape), dt))

    def begin_phase(self, tag):
        assert self.phase_es is None
        self.phase_es = ExitStack()
        self.kp = tag + ":"

    def end_phase(self):
        self.barrier()
        self.phase_es.close()
        self.phase_es = None
        self.kp = ""

    def barrier(self):
        toks = set()
        for e in ENGS:
            n = 0
            last = None
            for i, o in enumerate(self.ops[e]):
                if o["fn"] is not None and o["dma"] is None:
                    last = i
            if last is not None:
                toks.add(("e", e, last))
        for k in range(self.NDS):
            if self.dma_val[k] > 0:
                toks.add(("d", k, self.dma_val[k]))
        for e in ENGS:
            deps = set(t for t in toks if not (t[0] == "e" and t[1] == e))
            self.ops[e].append(dict(fn=None, deps=deps, dma=None))

    def _deps(self, eng, reads, writes):
        raw, war = set(), set()
        for k in reads:
            t = self.lastw.get(k)
            if t is not None:
                raw.add(t)
            if k.split(":")[-1].startswith("ps"):
                rd = self.readers.get(k)
                if rd:
                    for e2, idx in rd[0].items():
                        if e2 != eng:
                            raw.add(("e", e2, idx))
        for k in writes:
            t = self.lastw.get(k)
            if t is not None:
                raw.add(t)
            rd = self.readers.get(k)
            if rd:
                for e2, idx in rd[0].items():
                    war.add(("e", e2, idx))
                for t in rd[1]:
                    war.add(t)
        deps = set()
        for t in raw:
            if t[0] == "e" and t[1] == eng and eng == "pe":
                continue
            deps.add(t)
        for t in war:
            if t[0] == "e" and t[1] == eng:
                continue
            deps.add(t)
        return deps

    def _commit(self, tok, reads, writes):
        for k in reads:
            rd = self.readers.setdefault(k, [{}, []])
            if tok[0] == "e":
                rd[0][tok[1]] = tok[2]
            else:
                rd[1].append(tok)
        for k in writes:
            self.lastw[k] = tok
            self.readers[k] = [{}, []]

    def _k(self, keys):
        return [k if k.startswith("g:") else self.kp + k for k in keys]

    def op(self, eng, fn, reads=(), writes=()):
        reads, writes = self._k(reads), self._k(writes)
        deps = self._deps(eng, reads, writes)
        idx = len(self.ops[eng])
        tok = ("e", eng, idx)
        self.ops[eng].append(dict(fn=fn, deps=deps, dma=None))
        self._commit(tok, reads, writes)
        return tok

    def dma(self, eng, out, in_, reads=(), writes=(), is_out=False, **kw):
        reads, writes = self._k(reads), self._k(writes)
        deps = self._deps(eng, reads, writes)
        k = self.dma_rr
        self.dma_rr = (self.dma_rr + 1) % self.NDS
        prev = self.dma_val[k]
        self.dma_val[k] = prev + 16
        tok = ("d", k, prev + 16)
        fn = lambda e, out=out, in_=in_, kw=kw: e.dma_start(out=out, in_=in_, **kw)
        self.ops[eng].append(dict(fn=fn, deps=deps, dma=(k, prev)))
        self._commit(tok, reads, writes)
        if is_out:
            self.out_tokens.append(tok)
        return tok

    def cc(self, kind, in_ap, out_ap, groups, reads=(), writes=()):
        deps = self._deps("pool", reads, writes)
        k = self.dma_rr
        self.dma_rr = (self.dma_rr + 1) % self.NDS
        prev = self.dma_val[k]
        self.dma_val[k] = prev + 16
        tok = ("d", k, prev + 16)
        fn = lambda e: e.collective_compute(kind, ALU.bypass, groups, [in_ap], [out_ap])
        self.ops["pool"].append(dict(fn=fn, deps=deps, dma=(k, prev)))
        self._commit(tok, reads, writes)
        return tok

    def emit(self):
        nc = self.nc
        self.ops["sp"].append(dict(fn=None, deps=set(self.out_tokens), dma=None))
        ms = {e: set() for e in ENGS}
        for e in ENGS:
            for o in self.ops[e]:
                for t in o["deps"]:
                    if t[0] == "e":
                        ms[t[1]].add(t[2])
        rank = {e: {idx: i + 1 for i, idx in enumerate(sorted(ms[e]))} for e in ENGS}
        esem = {e: self.es.enter_context(nc.semaphore(f"s_{e}")) for e in ENGS}
        dsem = [self.es.enter_context(nc.semaphore(f"d_{i}")) for i in range(self.NDS)]
        ops = self.ops

        def run(e, eng):
            seen = {}
            for idx, o in enumerate(ops[e]):
                need = {}
                for t in o["deps"]:
                    if t[0] == "e":
                        key = ("e", t[1])
                        c = rank[t[1]][t[2]]
                    else:
                        key = ("d", t[1])
                        c = t[2]
                    if c > need.get(key, 0):
                        need[key] = c
                if o["dma"] is not None:
                    k, prev = o["dma"]
                    if prev > need.get(("d", k), 0):
                        need[("d", k)] = prev
                for key, c in need.items():
                    if c > seen.get(key, 0):
                        s = esem[key[1]] if key[0] == "e" else dsem[key[1]]
                        eng.wait_ge(s, c)
                        seen[key] = c
                        if DBG: print(f"[{e}] #{idx} wait {key} >= {c}")
                if o["fn"] is None:
                    continue
                ins = o["fn"](eng)
                if DBG: print(f"[{e}] #{idx} issue dma={o['dma']} inc={idx in ms[e]} rank={rank[e].get(idx)}")
                if o["dma"] is not None:
                    ins.then_inc(dsem[o["dma"][0]], 16)
                elif idx in ms[e]:
                    ins.then_inc(esem[e], 1)

        with nc.Block() as block:
            @block.tensor
            def _(eng):
                run("pe", eng)

            @block.scalar
            def _(eng):
                run("act", eng)

            @block.vector
            def _(eng):
                run("dve", eng)

            @block.gpsimd
            def _(eng):
                run("pool", eng)

            @block.sync
            def _(eng):
                run("sp", eng)
        self.es.close()
        return nc

    def mm(self, out, lhsT, rhs, start=True, stop=True, reads=(), writes=()):
        return self.op("pe", lambda e: e.matmul(out, lhsT, rhs, start=start, stop=stop), reads, writes)

    def tr(self, out, in_, ident, reads=(), writes=()):
        return self.op("pe", lambda e: e.transpose(out, in_, ident), reads, writes)

    def act(self, out, in_, func, reads=(), writes=(), **kw):
        return self.op("act", lambda e: e.activation(out, in_, func, **kw), reads, writes)


EPS = 1e-6


def norm_rows(p, x_ap_fn, nwbc, ident, hT_sb, ntiles, D, tag, xt_keys=None):
    pass


def build_norm(NT=2048, D=1024):
    p = Prog("norm")
    norm_phase(p, NT, D)
    return p.emit()


def norm_phase(p, NT=2048, D=1024):
    x = p.dram("x", [NT, D], F32)
    nw = p.dram("nw", [D], F32)
    ident_d = p.dram("ident", [128, 128], BF16)
    hT = p.dram("hT", [D, NT], BF16, kind="ExternalOutput")
    nch = D // 128
    ntiles = NT // 128
    xt = [p.sb([128, D], F32) for _ in range(2)]
    junk = p.sb([128, D], BF16)
    hb = [p.sb([128, D], BF16) for _ in range(2)]
    nwbc = p.sb([128, D], F32)
    ident = p.sb([128, 128], BF16)
    ss = [p.sb([128, 1], F32) for _ in range(2)]
    rs = [p.sb([128, 1], F32) for _ in range(2)]
    hT_sb = p.sb([128, nch, NT], BF16)
    pst = [p.ps([128, nch * 128], BF16) for _ in range(2)]

    p.dma("sp", nwbc[:], nw.partition_broadcast(128), writes=["nwbc"])
    p.dma("sp", ident[:], ident_d, writes=["ident"])
    for i in range(ntiles):
        s = i % 2
        p.dma("sp", xt[s][:], x[i * 128:(i + 1) * 128, :], writes=[f"xt{s}"])
        p.act(junk[:], xt[s][:], AF.Square, accum_out=ss[s][:], reads=[f"xt{s}"], writes=["junk", f"ss{s}"])
        p.op("dve", lambda e, s=s: e.tensor_scalar(rs[s][:], ss[s][:], 1.0 / D, EPS, ALU.mult, ALU.add),
             reads=[f"ss{s}"], writes=[f"rs{s}"])
        p.act(rs[s][:], rs[s][:], AF.Sqrt, reads=[f"rs{s}"], writes=[f"rs{s}"])
        p.op("dve", lambda e, s=s: e.reciprocal(rs[s][:], rs[s][:]), reads=[f"rs{s}"], writes=[f"rs{s}"])
        p.op("dve", lambda e, s=s: e.scalar_tensor_tensor(hb[s][:], xt[s][:], rs[s][:, 0:1], nwbc[:], ALU.mult, ALU.mult),
             reads=[f"xt{s}", f"rs{s}", "nwbc"], writes=[f"hb{s}"])
        for c in range(nch):
            p.tr(pst[s][:, c * 128:(c + 1) * 128], hb[s][:, c * 128:(c + 1) * 128], ident[:],
                 reads=[f"hb{s}", "ident"], writes=[f"pst{s}"])
        p.op("act", lambda e, s=s, i=i: e.copy(hT_sb[:, :, i * 128:(i + 1) * 128],
                                                 pst[s][:].rearrange("p (c t) -> p c t", c=nch)),
             reads=[f"pst{s}"], writes=["hT_sb"])
    p.dma("sp", hT.rearrange("(c p) t -> p c t", p=128), hT_sb[:], reads=["hT_sb"], is_out=True)


import os
VAR = os.environ.get('VAR', '')

S = 8192
D = 1024
NCH = 8
TCH = 512
NTC = S // TCH
NT = S // 128
NEG = -30000.0
MOBA_W = 1792
MOBA_DMIN = -384
SWA_W = 1024


def t5_bucket(d):
    d = np.asarray(d)
    dd = np.maximum(d, 1).astype(np.float32)
    large = 16 + (np.log(dd / np.float32(16)) / np.float32(np.log(1024 / 16)) * np.float32(16)).astype(np.int32)
    large = np.minimum(large, 31)
    return np.where(d < 16, np.maximum(d, 0), large)


def attn_consts(mode):
    c = {}
    c["ident"] = np.eye(128, dtype=NPBF)
    c["identf"] = np.eye(128, dtype=np.float32)
    if mode == "fox":
        c["tri"] = np.triu(np.ones((128, 128), np.float32))
        k = np.arange(128)[:, None]
        q = np.arange(512)[None, :]
        m = np.stack([np.where(q >= k + 128 * d, 0.0, NEG) for d in range(4)], 1)
        c["maskT"] = m.astype(NPBF)
    elif mode == "swa":
        c["jrev"] = np.eye(128, dtype=NPBF)[::-1].copy()
        n = SWA_W + 127
        dist = np.arange(n) - 127 + MOBA_DMIN
        oh = np.zeros((33, n), np.float32)
        b = t5_bucket(dist)
        for i in range(n):
            if 0 <= dist[i] < 128:
                oh[b[i], i] = 1.0
            else:
                oh[32, i] = NEG
        c["oh"] = oh
        kind = np.zeros((2, S), np.float32)
        kind[0, :] = 1.0
        c["kind"] = kind.astype(NPBF)
        vs = np.zeros((2, 2, 128), np.float32)
        vs[0, 0, 64:128] = 1.0
        vs[1, 1, 0:64] = 1.0
        c["vsink"] = vs.astype(NPBF)
    else:
        c["jrev"] = np.eye(128, dtype=NPBF)[::-1].copy()
        n = MOBA_W + 127
        dist = np.arange(n) - 127 + MOBA_DMIN
        oh = np.zeros((33, n), np.float32)
        b = t5_bucket(dist)
        for i in range(n):
            if dist[i] >= 0:
                oh[b[i], i] += 1.0
                oh[31, i] -= 1.0
            else:
                oh[32, i] = NEG
        c["oh"] = oh
        ind = np.zeros((33, S), np.float32)
        for j in range(32):
            ind[j, j * 256:(j + 1) * 256] = 1.0
        ind[32, :] = 1.0
        c["kind"] = ind.astype(NPBF)
    return c


def build_attn(mode, stage=9):
    p = Prog(mode)
    attn_phase(p, mode)
    return p.emit()


def attn_phase(p, mode, stage=9):
    fox = mode == "fox"
    swa = mode == "swa"
    UW = SWA_W if swa else MOBA_W
    KIN = 2 if swa else 33
    NTOK = 386 if fox else 384
    hT = p.dram("hT", [D, S], BF16)
    wqk = p.dram("wqk", [D, 256], F32)
    wtok = p.dram("wtok", [D, NTOK], F32)
    ident_d = p.dram("ident", [128, 128], BF16)
    identf_d = p.dram("identf", [128, 128], F32)
    yT = p.dram("yT", [128, S], F32, kind="ExternalOutput")
    if fox:
        fb_d = p.dram("fb", [2], F32)
        tri_d = p.dram("tri", [128, 128], F32)
        maskT_d = p.dram("maskT", [128, 4, 512], BF16)
    else:
        jrev_d = p.dram("jrev", [128, 128], BF16)
        oh_d = p.dram("oh", [33, UW + 127], F32)
        tab_d = p.dram("tab", [33, 2], F32)
        kind_d = p.dram("kind", [KIN, S], BF16)
        vec_d = p.dram("vecscr", [2, UW + 127], F32, kind="Internal")
        if swa:
            ksink_d = p.dram("ksink", [66, 2], F32)

    hT_v = hT.rearrange("(c p) t -> p c t", p=128)
    hc = [p.sb([128, NCH, TCH], BF16) for _ in range(2)]
    wqk_sb = p.sb([128, NCH, 256], BF16)
    wtok_sb = p.sb([128, NCH, NTOK], BF16)
    ident = p.sb([128, 128], BF16)
    identf = p.sb([128, 128], F32)
    QTa = [p.sb([128, S], BF16) for _ in range(2)]
    KTa = [p.sb([128, S], BF16) for _ in range(2)]
    V = p.sb([128, NT, 2, 128], BF16)
    nrm = p.sb([128, NT, 4], F32)
    sq = [p.sb([128, 256], F32) for _ in range(2)]
    NA = 96 if fox else (2 if swa else 33)
    A = [p.sb([128, NT, NA], BF16) for _ in range(2)]
    pT = [p.sb([128, TCH], BF16) for _ in range(4)]
    rl = [p.sb([128, TCH], F32) for _ in range(2)]
    yst = [p.sb([128, TCH], F32) for _ in range(2)]
    ones_f = p.sb([128, 128], F32)
    sbnd = p.sb([128, NT, 2], F32)
    small = p.sb([128, 64], F32)
    psA = [p.ps([128, 512], F32) for _ in range(2)]
    psO = [p.ps([128, 512], F32) for _ in range(2)]
    psT = [p.ps([128, 512], F32) for _ in range(2)]
    psB = p.ps([128, 1024], BF16)
    psM = p.ps([128, 512], F32)

    p.dma("sp", ident[:], ident_d, writes=["ident"])
    p.dma("sp", identf[:], identf_d, writes=["identf"])
    stg = p.sb([128, NCH, NTOK], F32)
    p.dma("sp", stg[:, :, 0:256], wqk.rearrange("(c p) n -> p c n", p=128), writes=["stg"])
    p.op("pool", lambda e: e.tensor_copy(wqk_sb[:], stg[:, :, 0:256]), reads=["stg"], writes=["wqk"])
    p.dma("sp", stg[:], wtok.rearrange("(c p) n -> p c n", p=128), writes=["stg"])
    p.op("pool", lambda e: e.tensor_copy(wtok_sb[:], stg[:]), reads=["stg"], writes=["wtok"])
    p.op("pool", lambda e: e.memset(V[:], 1.0), writes=["V"])
    p.op("pool", lambda e: e.memset(ones_f[:], 1.0), writes=["ones_f"])
    for h in range(2):
        p.op("pool", lambda e, h=h: e.memset(A[h][:], 0.0), writes=[f"A{h}"])
    if fox:
        f_sb = p.sb([128, NT, 2], F32)
        tri = p.sb([128, 128], F32)
        maskT = p.sb([128, 4, 512], BF16)
        fbb = p.sb([128, 2], F32)
        p.dma("sp", tri[:], tri_d, writes=["tri"])
        p.dma("sp", maskT[:], maskT_d, writes=["maskT"])
        p.dma("sp", fbb[:], fb_d.partition_broadcast(128), writes=["fbb"])
    else:
        jrev = p.sb([128, 128], BF16)
        U = [p.sb([128, UW], BF16) for _ in range(2)]
        oh = p.sb([33, UW + 127], F32)
        tab = p.sb([33, 2], F32)
        vec_sb = p.sb([2, UW + 127], F32)
        if swa:
            ksf = p.sb([66, 2], F32)
            ksink = p.sb([66, 2], BF16)
            vsink = p.sb([2, 2, 128], BF16)
            psink = p.sb([2, TCH], BF16)
            vsink_d = p.dram("vsink", [2, 2, 128], BF16)
            p.dma("sp", vsink[:], vsink_d, writes=["vsink"])
            p.dma("sp", ksf[:], ksink_d, writes=["ksf"])
            p.op("dve", lambda e: e.tensor_copy(ksink[:], ksf[:]), reads=["ksf"], writes=["ksink"])

        kmT = [p.sb([64, 32], F32) for _ in range(2)]
        qf = [p.sb([64, TCH], F32) for _ in range(2)]
        gsb = p.sb([128, 32], F32)
        top8 = p.sb([128, 8], F32)
        mb01 = p.sb([128, 32], F32)
        p.dma("sp", jrev[:], jrev_d, writes=["jrev"])
        p.dma("sp", oh[:], oh_d, writes=["oh"])
        p.dma("sp", tab[:], tab_d, writes=["tab"])
        for h in range(2):
            p.dma("sp", KTa[h][64:64 + KIN, :], kind_d, writes=[f"KTa{h}"])
            p.op("pool", lambda e, h=h: e.memset(kmT[h][:], 0.0), writes=[f"kmT{h}"])
        W = UW + 127
        for c0 in range(0, W, 512):
            c1 = min(W, c0 + 512)
            p.mm(psM[0:2, 0:c1 - c0], tab[:, :], oh[:, c0:c1], reads=["tab", "oh"], writes=["psM"])
            p.op("dve", lambda e, c0=c0, c1=c1: e.tensor_copy(vec_sb[:, c0:c1], psM[0:2, 0:c1 - c0]),
                 reads=["psM"], writes=["vec_sb"])
        p.dma("sp", vec_d, vec_sb[:], reads=["vec_sb"], writes=["vec_d"])
        for h in range(2):
            src = bass.AP(vec_d.tensor, h * W, [[1, 128], [1, UW]])
            stgf = stg[:].rearrange("p c n -> p (c n)")
            p.dma("sp", stgf[:, 0:UW], src, reads=["vec_d"], writes=["stg"])
            p.op("pool", lambda e, h=h, stgf=stgf: e.tensor_copy(U[h][:], stgf[:, 0:UW]), reads=["stg"], writes=[f"U{h}"])

    for tc in range(NTC):
        s = tc % 2
        p.dma("sp", hc[s][:], hT_v[:, :, tc * TCH:(tc + 1) * TCH], writes=[f"hc{s}"])
        for g in range(4):
            if VAR in ('B', 'C', 'D'):
                break
            ps = psA[g % 2]
            for c in range(NCH):
                p.mm(ps[0:64, :], wqk_sb[:, c, g * 64:(g + 1) * 64], hc[s][:, c, :], start=(c == 0), stop=(c == NCH - 1),
                     reads=[f"hc{s}", "wqk"], writes=[f"psA{g % 2}"])
            h = g % 2
            cs = slice(tc * TCH, (tc + 1) * TCH)
            if g < 2:
                p.op("act", lambda e, ps=ps, h=h, cs=cs: e.mul(QTa[h][0:64, cs], ps[0:64, :], 0.125),
                     reads=[f"psA{g % 2}"], writes=[f"QTa{h}"])
                if not fox and not swa:
                    p.op("dve", lambda e, ps=ps, h=h: e.tensor_copy(qf[h][:], ps[0:64, :]),
                         reads=[f"psA{g % 2}"], writes=[f"qf{h}"])
            else:
                p.op("dve", lambda e, ps=ps, h=h, cs=cs: e.tensor_copy(KTa[h][0:64, cs], ps[0:64, :]),
                     reads=[f"psA{g % 2}"], writes=[f"KTa{h}"])
                if not fox and not swa:
                    p.op("dve", lambda e, ps=ps, h=h, tc=tc: e.tensor_reduce(
                        kmT[h][:, 2 * tc:2 * tc + 2], ps[0:64, :].rearrange("p (b t) -> p b t", b=2), AX.X, ALU.add),
                        reads=[f"psA{g % 2}"], writes=[f"kmT{h}"])
        for j in range(4):
            if VAR == 'A':
                break
            i = tc * 4 + j
            ps = psT[j % 2]
            for c in range(NCH):
                p.mm(ps[:, 0:NTOK], hc[s][:, c, j * 128:(j + 1) * 128], wtok_sb[:, c, :], start=(c == 0), stop=(c == NCH - 1),
                     reads=[f"hc{s}", "wtok"], writes=[f"psT{j % 2}"])
            if os.environ.get("EXP") == "8":
                p.act(sq[j % 2][:], ps[:, 128:384], AF.Square, reads=[f"psT{j % 2}"], writes=[f"sq{j % 2}"])
                continue
            p.op("dve", lambda e, ps=ps, i=i: e.tensor_copy(V[:, i, 0, 0:64], ps[:, 0:64]),
                 reads=[f"psT{j % 2}"], writes=["V"])
            p.op("dve", lambda e, ps=ps, i=i: e.tensor_copy(V[:, i, 1, 64:128], ps[:, 64:128]),
                 reads=[f"psT{j % 2}"], writes=["V"])
            if VAR == 'B':
                continue
            EXP = os.environ.get("EXP", "")
            if EXP == "1":
                p.act(sq[j % 2][:], ps[:, 0:256], AF.Square, reads=[f"psT{j % 2}"], writes=[f"sq{j % 2}"])
            elif EXP == "3":
                p.act(sq[j % 2][:], ps[:, 128:384], AF.Square, reads=[f"psT{j % 2}", "V"], writes=[f"sq{j % 2}"])
            elif EXP == "5":
                p.act(sq[j % 2][:], ps[:, 128:384], AF.Square, reads=[f"psT{j % 2}"], writes=[f"sqx{i}"])
            elif EXP == "6":
                p.op("act", lambda e, ps=ps, j=j: e.mul(sq[j % 2][:], ps[:, 128:384], 1.0), reads=[f"psT{j % 2}"], writes=[f"sq{j % 2}"])
            elif EXP == "7":
                p.op("act", lambda e, ps=ps, j=j: e.mul(sq[j % 2][0:64, :], ps[0:64, 128:384], 1.0), reads=[f"psT{j % 2}"], writes=[f"sq{j % 2}"])
            elif EXP == "4":
                p.op("dve", lambda e, ps=ps, j=j: e.tensor_copy(sq[j % 2][:], ps[:, 128:384]), reads=[f"psT{j % 2}"], writes=[f"sq{j % 2}"])
            else:
                p.act(sq[j % 2][:], ps[:, 128:384], AF.Square, reads=[f"psT{j % 2}"], writes=[f"sq{j % 2}"])
            if VAR == 'C':
                continue
            p.op("dve", lambda e, i=i, j=j: e.tensor_reduce(nrm[:, i, :], sq[j % 2][:].rearrange("p (g d) -> p g d", d=64), AX.X, ALU.add),
                 reads=[f"sq{j % 2}"], writes=["nrm"])
            if fox:
                p.op("dve", lambda e, ps=ps, i=i: e.tensor_copy(f_sb[:, i, :], ps[:, 384:386]),
                     reads=[f"psT{j % 2}"], writes=["f_sb"])
        if not fox and not swa:
            for h in range(2):
                for j in range(4):
                    i = tc * 4 + j
                    n = i // 2
                    p.mm(psM[:, 0:32], qf[h][:, j * 128:(j + 1) * 128], kmT[h][:, :], reads=[f"qf{h}", f"kmT{h}"], writes=["psM"])
                    p.op("dve", lambda e: e.tensor_copy(gsb[:], psM[:, 0:32]), reads=["psM"], writes=["gsb"])
                    p.op("dve", lambda e, n=n: e.memset(gsb[:, n:32], -1e30), writes=["gsb"])
                    p.op("dve", lambda e: e.max(top8[:], gsb[:]), reads=["gsb"], writes=["top8"])
                    p.op("dve", lambda e: e.tensor_scalar(mb01[:], gsb[:], top8[:, 2:3], 1.0, ALU.is_ge, ALU.subtract),
                         reads=["gsb", "top8"], writes=["mb01"])
                    p.op("dve", lambda e, h=h, i=i: e.tensor_scalar(A[h][:, i, 0:32], mb01[:], -NEG, None, ALU.mult),
                         reads=["mb01"], writes=[f"A{h}"])
                    p.op("dve", lambda e, h=h, i=i, n=n: e.memset(A[h][:, i, n:n + 1], 0.0), writes=[f"A{h}"])

    kmx = small[:, 0:2]
    p.op("dve", lambda e: e.tensor_reduce(kmx, nrm[:, :, 2:4].rearrange("p i g -> p g i"), AX.X, ALU.max),
         reads=["nrm"], writes=["small"])
    p.tr(psM[0:2, 0:128], kmx, identf[:], reads=["small", "identf"], writes=["psM"])
    kmx2 = small[0:2, 8:9]
    p.op("dve", lambda e: e.tensor_reduce(kmx2, psM[0:2, 0:128], AX.X, ALU.max), reads=["psM"], writes=["small2"])
    kbr = small[0:2, 16:48]
    krow = p.sb([2, 128], F32)
    p.op("dve", lambda e: e.tensor_scalar(krow[:], ones_f[0:2, :], kmx2, None, ALU.mult),
         reads=["small2", "ones_f"], writes=["krow"])
    p.mm(psM[:, 0:2], krow[:], identf[0:2, 0:2], reads=["krow", "identf"], writes=["psM"])
    kbc = small[:, 4:6]
    p.op("dve", lambda e: e.tensor_copy(kbc, psM[:, 0:2]), reads=["psM"], writes=["kbc"])
    for h in range(2):
        p.op("dve", lambda e, h=h: e.tensor_scalar(sbnd[:, :, h], nrm[:, :, h], small[:, 4 + h:5 + h], None, ALU.mult),
             reads=["nrm", "kbc"], writes=["sbnd"])
    p.act(sbnd[:], sbnd[:], AF.Sqrt, reads=["sbnd"], writes=["sbnd"], scale=(0.125 * 1.02) ** 2)

    if fox:
        nfb = p.sb([128, 2], F32)
        e1 = p.sb([128, NT, 2], F32)
        g = p.sb([128, NT, 2], F32)
        incl = p.sb([128, NT, 2], F32)
        G = p.sb([128, NT, 2], F32)
        r1 = p.sb([128, NT, 2], F32)
        Gh = p.sb([128, NT, 2], BF16)
        Gm = p.sb([128, NT, 2], BF16)
        Gl = p.sb([128, NT, 2], BF16)
        p.op("dve", lambda e: e.tensor_scalar(nfb[:], fbb[:], -1.0, None, ALU.mult), reads=["fbb"], writes=["nfb"])
        for h in range(2):
            p.act(e1[:, :, h], f_sb[:, :, h], AF.Exp, reads=["f_sb", "nfb"], writes=["e1"], scale=-1.0, bias=nfb[:, h:h + 1])
        p.act(g[:], e1[:], AF.Ln, reads=["e1"], writes=["g"], bias=1.0)
        gf = g[:].rearrange("p i h -> p (i h)")
        p.mm(psM[:, 0:128], tri[:], gf, reads=["tri", "g"], writes=["psM"])
        p.mm(psM[:, 128:256], ones_f[:], gf, reads=["ones_f", "g"], writes=["psM"])
        W_v = psM[:, 0:128].rearrange("p (i h) -> p i h", h=2)
        B_v = psM[:, 128:256].rearrange("p (i h) -> p i h", h=2)
        for h in range(2):
            p.op("dve", lambda e, h=h: e.tensor_tensor_scan(incl[:, :, h], ones_f[:, 0:NT], B_v[:, :, h], 0.0, ALU.mult, ALU.add),
                 reads=["psM", "ones_f"], writes=["incl"])
        p.op("dve", lambda e: e.tensor_tensor(r1[:], incl[:], B_v, ALU.subtract), reads=["incl", "psM"], writes=["r1"])
        p.op("dve", lambda e: e.tensor_tensor(G[:], r1[:], W_v, ALU.add), reads=["r1", "psM"], writes=["G"])
        p.op("dve", lambda e: e.tensor_copy(Gh[:], G[:]), reads=["G"], writes=["Gh"])
        p.op("dve", lambda e: e.tensor_tensor(r1[:], G[:], Gh[:], ALU.subtract), reads=["G", "Gh"], writes=["r1"])
        p.op("dve", lambda e: e.tensor_copy(Gm[:], r1[:]), reads=["r1"], writes=["Gm"])
        p.op("dve", lambda e: e.tensor_tensor(r1[:], r1[:], Gm[:], ALU.subtract), reads=["r1", "Gm"], writes=["r1"])
        p.op("dve", lambda e: e.tensor_copy(Gl[:], r1[:]), reads=["r1"], writes=["Gl"])
        for h in range(2):
            Ah = A[h]
            for col in (0, 1, 2, 35, 36, 37, 38):
                p.op("pool", lambda e, Ah=Ah, col=col: e.memset(Ah[:, :, col:col + 1], 1.0), writes=[f"A{h}"])
            for k3, Gx in enumerate((Gh, Gm, Gl)):
                p.op("dve", lambda e, Ah=Ah, Gx=Gx, k3=k3, h=h: e.tensor_scalar(Ah[:, :, 3 + k3], Gx[:, :, h], -1.0, None, ALU.mult),
                     reads=["Gh", "Gm", "Gl"], writes=[f"A{h}"])
                p.op("dve", lambda e, Ah=Ah, Gx=Gx, k3=k3, h=h: e.tensor_copy(Ah[:, :, 32 + k3], Gx[:, :, h]),
                     reads=["Gh", "Gm", "Gl"], writes=[f"A{h}"])
            p.op("dve", lambda e, Ah=Ah, h=h: e.tensor_scalar(Ah[:, :, 6], sbnd[:, :, h], -1.0, None, ALU.mult),
                 reads=["sbnd"], writes=[f"A{h}"])
    elif swa:
        for h in range(2):
            p.op("dve", lambda e, h=h: e.tensor_scalar(A[h][:, :, 0], sbnd[:, :, h], -1.0, None, ALU.mult),
                 reads=["sbnd"], writes=[f"A{h}"])
            p.op("dve", lambda e, h=h: e.memset(A[h][:, :, 1:2], 1.0), writes=[f"A{h}"])
    else:
        for h in range(2):
            p.op("dve", lambda e, h=h: e.tensor_scalar(A[h][:, :, 32], sbnd[:, :, h], -1.0, None, ALU.mult),
                 reads=["sbnd"], writes=[f"A{h}"])

    for h in range(2):
        for g8 in range(NT // 8):
            for t in range(8):
                i = g8 * 8 + t
                p.tr(psB[0:NA, t * 128:(t + 1) * 128], A[h][:, i, :], ident[:], reads=[f"A{h}", "ident"], writes=["psB"])
            cs = slice(g8 * 1024, (g8 + 1) * 1024)
            if fox:
                p.op("dve", lambda e, h=h, cs=cs: e.tensor_copy(QTa[h][64:71, cs], psB[0:7, :]), reads=["psB"], writes=[f"QTa{h}"])
                p.op("act", lambda e, h=h, cs=cs: e.copy(KTa[h][64:71, cs], psB[32:39, :]), reads=["psB"], writes=[f"KTa{h}"])
            else:
                p.op("dve", lambda e, h=h, cs=cs: e.tensor_copy(QTa[h][64:64 + NA, cs], psB[0:NA, :]), reads=["psB"], writes=[f"QTa{h}"])

    KR = 71 if fox else (66 if swa else 97)
    SB = [(psA[0], "psA0"), (psA[1], "psA1"), (psT[0], "psT0"), (psT[1], "psT1")]
    NB = len(SB)
    LOOK = 2
    jobs = []
    oi = 0
    for h in range(2):
        for qc in range(NTC):
            nk = 4 * qc + 4
            kt0 = max(0, 4 * qc - 1) if swa else 0
            for kt in range(kt0, nk):
                jobs.append((h, qc, kt, kt == kt0, kt == nk - 1, oi))
            oi += 1

    def s_stage(i):
        h, qc, kt, first, last, oi = jobs[i]
        pa, pak = SB[i % NB]
        qs = slice(qc * TCH, (qc + 1) * TCH)
        ks = slice(kt * 128, (kt + 1) * 128)
        d0 = qc * TCH - kt * 128
        extra = (kt >= 4 * qc) if fox else (swa or d0 <= 896)
        p.mm(pa[:, :], KTa[h][0:KR, ks], QTa[h][0:KR, qs], start=True, stop=not extra,
             reads=[f"KTa{h}", f"QTa{h}"], writes=[pak])
        if extra:
            if fox:
                p.mm(pa[:, :], ident[:], maskT[:, kt - 4 * qc, :], start=False, stop=True,
                     reads=["ident", "maskT"], writes=[pak])
            else:
                off = d0 - MOBA_DMIN
                p.mm(pa[:, :], jrev[:], U[h][:, off:off + TCH], start=False, stop=True,
                     reads=["jrev", f"U{h}"], writes=[pak])

    def pv_stage(i):
        h, qc, kt, first, last, oi = jobs[i]
        pa, pak = SB[i % NB]
        pt = pT[i % len(pT)]
        ptk = f"pT{i % len(pT)}"
        po = psO[oi % 2]
        pok = f"psO{oi % 2}"
        qs = slice(qc * TCH, (qc + 1) * TCH)
        p.act(pt[:], pa[:, :], AF.Exp, reads=[pak], writes=[ptk])
        p.mm(po[:, :], V[:, kt, h, :], pt[:], start=first, stop=(last and not swa),
             reads=["V", ptk], writes=[pok])
        if not last:
            return
        if swa:
            p.mm(psM[0:2, :], ksink[:, 0:2], QTa[h][0:66, qs], reads=["ksink", f"QTa{h}"], writes=["psM"])
            p.act(psink[:], psM[0:2, :], AF.Exp, reads=["psM"], writes=["psink"])
            p.mm(po[:, :], vsink[:, h, :], psink[:], start=False, stop=True, reads=["vsink", "psink"], writes=[pok])
        r = rl[oi % 2]
        y = yst[oi % 2]
        if h == 0:
            num, den = slice(0, 64), slice(64, 128)
        else:
            num, den = slice(64, 128), slice(0, 64)
        p.op("dve", lambda e: e.reciprocal(r[den, :], po[den, :]), reads=[pok], writes=[f"rl{oi % 2}"])
        p.op("dve", lambda e: e.tensor_tensor(y[num, :], po[num, :], r[den, :], ALU.mult),
             reads=[pok, f"rl{oi % 2}"], writes=[f"yst{oi % 2}"])
        p.dma("sp", yT[num, qs], y[num, :], reads=[f"yst{oi % 2}"], is_out=True)

    for i in range(len(jobs) + LOOK):
        if i < len(jobs):
            s_stage(i)
        if i - LOOK >= 0:
            pv_stage(i - LOOK)


def attn_inputs(mode, hT_b, w_in_l, j, extra):
    off_moba = 512 + 1024 + 8
    off_fox = off_moba + 1536
    off_f = off_fox + 1536
    base = off_fox if mode == "fox" else off_moba
    hs = [2 * j, 2 * j + 1]
    if mode == "swa":
        off_q = off_f + 8
        off_kv = off_q + 512
        kv = j // 2
        q = [w_in_l[:, off_q + h * 64: off_q + (h + 1) * 64] for h in hs]
        k = [w_in_l[:, off_kv + kv * 64: off_kv + (kv + 1) * 64]] * 2
        v = [w_in_l[:, off_kv + 128 + kv * 64: off_kv + 128 + (kv + 1) * 64]] * 2
    else:
        q = [w_in_l[:, base + h * 64: base + (h + 1) * 64] for h in hs]
        k = [w_in_l[:, base + 512 + h * 64: base + 512 + (h + 1) * 64] for h in hs]
        v = [w_in_l[:, base + 1024 + h * 64: base + 1024 + (h + 1) * 64] for h in hs]
    m = dict(attn_consts(mode))
    m["hT"] = hT_b
    m["wqk"] = np.ascontiguousarray(np.concatenate(q + k, axis=1))
    if mode == "fox":
        f = [w_in_l[:, off_f + h: off_f + h + 1] for h in hs]
        m["wtok"] = np.ascontiguousarray(np.concatenate(v + q + k + f, axis=1))
        m["fb"] = np.ascontiguousarray(extra["forget_bias"][hs])
    else:
        m["wtok"] = np.ascontiguousarray(np.concatenate(v + q + k, axis=1))
        tab = extra["rel_bias"][:, hs] if mode == "moba" else extra["rel_bias"][:, [8 + hh for hh in hs]]
        if mode == "swa":
            ks = np.zeros((66, 2), np.float32)
            ks[64, :] = 1.0
            ks[65, :] = extra["sinks"][hs]
            m["ksink"] = ks
        m["tab"] = np.ascontiguousarray(np.concatenate([tab, np.ones((1, 2), np.float32)], axis=0))
    return m


S = 8192
D = 1024
NCH = 8
TCH = 512
NTC = S // TCH
NEG = -30000.0


def ssd_consts():
    c = {}
    c["ident"] = np.eye(128, dtype=NPBF)
    c["identf"] = np.eye(128, dtype=np.float32)
    c["tri"] = np.triu(np.ones((128, 128), np.float32))
    s = np.arange(128)[:, None]
    l = np.arange(256)[None, :]
    c["m01"] = np.where(l >= s, 1.0, 0.0).astype(np.float32)
    return c


def build_ssd(nchunks=NTC):
    p = Prog("ssd")
    ssd_phase(p, nchunks)
    return p.emit()


def ssd_phase(p, nchunks=NTC):
    hT = p.dram("hT", [D, S], BF16)
    wfm = p.dram("wfm", [D, 512], F32)
    wdt = p.dram("wdt", [D, 2], F32)
    cw_d = p.dram("cw", [128, 3, 4], F32)
    cb_d = p.dram("cb", [128, 3], F32)
    dtb_d = p.dram("dtb", [2], F32)
    alog_d = p.dram("alog", [2], F32)
    dcol_d = p.dram("dcol", [128, 1], F32)
    ident_d = p.dram("ident", [128, 128], BF16)
    identf_d = p.dram("identf", [128, 128], F32)
    tri_d = p.dram("tri", [128, 128], F32)
    m01_d = p.dram("m01", [128, 256], F32)
    yT = p.dram("yT", [128, S], F32, kind="ExternalOutput")
    hT_v = hT.rearrange("(c p) t -> p c t", p=128)

    hc = [p.sb([128, NCH, TCH], BF16) for _ in range(2)]
    stg = p.sb([128, NCH, 512], F32)
    wfm_sb = p.sb([128, NCH, 512], BF16)
    wdt_f = p.sb([128, NCH, 2], F32)
    wdt_sb = p.sb([128, NCH, 2], BF16)
    cw = p.sb([128, 3, 4], F32)
    cb = p.sb([128, 3], F32)
    dcol = p.sb([128, 1], F32)
    ident = p.sb([128, 128], BF16)
    identf = p.sb([128, 128], F32)
    tri = p.sb([128, 128], F32)
    ones_f = p.sb([128, 128], F32)
    dtb4 = p.sb([128, 4, 2], F32)
    negA4 = p.sb([128, 4, 2], F32)
    ub = [[p.sb([128, 3 + TCH], F32) for _ in range(2)] for _ in range(3)]
    acc = p.sb([128, TCH], F32)
    zs = p.sb([128, TCH], F32)
    xT = p.sb([128, TCH], F32)
    xTb = p.sb([128, TCH], BF16)
    BT = p.sb([128, TCH], BF16)
    CT = p.sb([128, TCH], BF16)
    Btok = p.sb([128, 4, 128], BF16)
    xdtp = [p.sb([128, 4, 128], BF16) for _ in range(2)]
    xdd = p.sb([128, 4, 128], BF16)
    zt = p.sb([128, 4, 2], F32)
    dt = p.sb([128, 4, 2], F32)
    a = p.sb([128, 4, 2], F32)
    acum = p.sb([128, 4, 2], F32)
    last = p.sb([128, 2, 2], F32)
    dte = p.sb([128, 4, 2], F32)
    dd = p.sb([128, 4, 2], F32)
    cd = p.sb([128, 2, 2], F32)
    nac = p.sb([128, 4, 2], F32)
    dg = [p.sb([128, 128], F32) for _ in range(2)]
    Dc = [p.sb([128, 256], F32) for _ in range(2)]
    Gm = [p.sb([128, 256], F32) for _ in range(2)]
    m01 = p.sb([128, 256], F32)
    dg2 = [[p.sb([128, 128], F32) for _ in range(2)] for _ in range(2)]
    Dc2 = [[p.sb([128, 256], F32) for _ in range(2)] for _ in range(2)]
    LT2 = [[p.sb([128, 256], F32) for _ in range(2)] for _ in range(2)]
    WT2 = [[p.sb([128, 256], BF16) for _ in range(2)] for _ in range(2)]
    Eb2 = [p.sb([128, 256], F32) for _ in range(2)]
    LT = [p.sb([128, 256], F32) for _ in range(2)]
    WT = [p.sb([128, 256], BF16) for _ in range(2)]
    Eb = p.sb([128, 256], F32)
    CdT = [p.sb([128, 256], BF16) for _ in range(2)]
    hst = p.sb([128, 128], F32)
    hpad = [p.sb([128, 128], BF16) for _ in range(2)]
    t1 = p.sb([128, 256], F32)
    yst = [p.sb([128, 256], F32) for _ in range(2)]

    ps = [p.ps([128, 512], F32) for _ in range(7)]
    psB = p.ps([128, 1024], BF16)
    K = lambda i: f"ps{i}"

    for t, d_, k in ((ident, ident_d, "ident"), (identf, identf_d, "identf"), (tri, tri_d, "tri"), (m01, m01_d, "m01"),
                     (cw, cw_d, "cw"), (cb, cb_d, "cb"), (dcol, dcol_d, "dcol")):
        p.dma("sp", t[:], d_, writes=[k])
    p.dma("sp", stg[:], wfm.rearrange("(c p) n -> p c n", p=128), writes=["stg"])
    p.op("pool", lambda e: e.tensor_copy(wfm_sb[:], stg[:]), reads=["stg"], writes=["wfm"])
    p.dma("sp", wdt_f[:], wdt.rearrange("(c p) n -> p c n", p=128), writes=["wdt_f"])
    p.op("pool", lambda e: e.tensor_copy(wdt_sb[:], wdt_f[:]), reads=["wdt_f"], writes=["wdt"])
    p.op("pool", lambda e: e.memset(ones_f[:], 1.0), writes=["ones_f"])
    p.op("pool", lambda e: e.memset(hst[:], 0.0), writes=["hst"])
    for h in range(2):
        p.op("pool", lambda e, h=h: e.memset(hpad[h][:], 0.0), writes=[f"hpad{h}"])
        p.op("pool", lambda e, h=h: e.memset(xdtp[h][:], 0.0), writes=[f"xdtp{h}"])
    for g in range(3):
        p.op("pool", lambda e, g=g: e.memset(ub[g][1][:, TCH:TCH + 3], 0.0), writes=[f"ub{g}_1"])
    for j in range(4):
        p.dma("sp", dtb4[:, j, :], dtb_d.partition_broadcast(128), writes=["dtb4"])
        p.dma("sp", negA4[:, j, :], alog_d.partition_broadcast(128), writes=["negA4"])
    p.act(negA4[:], negA4[:], AF.Exp, reads=["negA4"], writes=["negA4"])
    p.op("dve", lambda e: e.tensor_scalar(negA4[:], negA4[:], -1.0, None, ALU.mult), reads=["negA4"], writes=["negA4"])

    for tc in range(nchunks):
        s = tc % 2
        p.dma("sp", hc[s][:], hT_v[:, :, tc * TCH:(tc + 1) * TCH], writes=[f"hc{s}"])
        for g in range(4):
            pp = ps[g % 2]
            for c in range(NCH):
                p.mm(pp[:, :], wfm_sb[:, c, g * 128:(g + 1) * 128], hc[s][:, c, :], start=(c == 0), stop=(c == NCH - 1),
                     reads=[f"hc{s}", "wfm"], writes=[K(g % 2)])
            if g == 0:
                p.act(zs[:], pp[:, :], AF.Silu, reads=[K(0)], writes=["zs"])
            else:
                gi = g - 1
                u = ub[gi][s]
                uo = ub[gi][1 - s]
                p.op("dve", lambda e, u=u, uo=uo: e.tensor_copy(u[:, 0:3], uo[:, TCH:TCH + 3]),
                     reads=[f"ub{gi}_{1 - s}"], writes=[f"ub{gi}_{s}"])
                p.op("dve", lambda e, u=u, pp=pp: e.tensor_copy(u[:, 3:3 + TCH], pp[:, :]),
                     reads=[K(g % 2)], writes=[f"ub{gi}_{s}"])
                p.op("dve", lambda e, u=u, gi=gi: e.tensor_scalar(acc[:], u[:, 0:TCH], cw[:, gi, 0:1], None, ALU.mult),
                     reads=[f"ub{gi}_{s}", "cw"], writes=["acc"])
                for i in range(1, 4):
                    p.op("dve", lambda e, u=u, gi=gi, i=i: e.scalar_tensor_tensor(acc[:], u[:, i:i + TCH], cw[:, gi, i:i + 1], acc[:], ALU.mult, ALU.add),
                         reads=[f"ub{gi}_{s}", "cw", "acc"], writes=["acc"])
                dst, dk = ((xT, "xT"), (BT, "BT"), (CT, "CT"))[gi]
                p.act(dst[:], acc[:], AF.Silu, reads=["acc", "cb"], writes=[dk], bias=cb[:, gi:gi + 1])
                if gi == 0:
                    p.op("pool", lambda e: e.tensor_copy(xTb[:], xT[:]), reads=["xT"], writes=["xTb"])
        for j in range(4):
            for c in range(NCH):
                p.mm(ps[6][:, 2 * j:2 * j + 2], hc[s][:, c, j * 128:(j + 1) * 128], wdt_sb[:, c, :], start=(c == 0), stop=(c == NCH - 1),
                     reads=[f"hc{s}", "wdt"], writes=[K(6)])
        p.op("dve", lambda e: e.tensor_tensor(zt[:], ps[6][:, 0:8].rearrange("p (j h) -> p j h", h=2), dtb4[:], ALU.add),
             reads=[K(6), "dtb4"], writes=["zt"])
        p.act(zt[:], zt[:], AF.Exp, reads=["zt"], writes=["zt"])
        p.act(dt[:], zt[:], AF.Ln, reads=["zt"], writes=["dt"], bias=1.0)
        p.op("dve", lambda e: e.tensor_tensor(a[:], dt[:], negA4[:], ALU.mult), reads=["dt", "negA4"], writes=["a"])
        af = a[:].rearrange("p j h -> p (j h)")
        p.mm(ps[6][:, 16:24], tri[:], af, reads=["tri", "a"], writes=[K(6)])
        p.mm(ps[6][:, 32:40], ones_f[:], af, reads=["ones_f", "a"], writes=[K(6)])
        Wv = ps[6][:, 16:24].rearrange("p (j h) -> p j h", h=2)
        Bv = ps[6][:, 32:40].rearrange("p (c a h) -> p c a h", a=2, h=2)
        p.op("dve", lambda e: e.tensor_copy(acum[:], Wv), reads=[K(6)], writes=["acum"])
        acv = acum[:].rearrange("p (c a) h -> p c a h", a=2)
        p.op("dve", lambda e: e.tensor_tensor(acv[:, :, 1, :], acv[:, :, 1, :], Bv[:, :, 0, :], ALU.add),
             reads=["acum", K(6)], writes=["acum"])
        p.op("dve", lambda e: e.tensor_copy(last[:], Bv[:, :, 0, :]), reads=[K(6)], writes=["last"])
        p.op("dve", lambda e: e.tensor_tensor(last[:], last[:], Bv[:, :, 1, :], ALU.add), reads=["last", K(6)], writes=["last"])
        for ch in range(2):
            for a2 in range(2):
                p.op("dve", lambda e, ch=ch, a2=a2: e.tensor_tensor(dd[:, 2 * ch + a2, :], last[:, ch, :], acum[:, 2 * ch + a2, :], ALU.subtract),
                     reads=["last", "acum"], writes=["dd"])
        p.act(dte[:], dd[:], AF.Exp, reads=["dd"], writes=["dte"])
        p.act(cd[:], last[:], AF.Exp, reads=["last"], writes=["cd"])
        p.op("dve", lambda e: e.tensor_tensor(dte[:], dte[:], dt[:], ALU.mult), reads=["dte", "dt"], writes=["dte"])
        p.op("dve", lambda e: e.tensor_scalar(nac[:], acum[:], -1.0, None, ALU.mult), reads=["acum"], writes=["nac"])
        for j in range(4):
            p.tr(psB[:, j * 128:(j + 1) * 128], xTb[:, j * 128:(j + 1) * 128], ident[:], reads=["xTb", "ident"], writes=["psB"])
            p.tr(psB[:, 512 + j * 128:512 + (j + 1) * 128], BT[:, j * 128:(j + 1) * 128], ident[:], reads=["BT", "ident"], writes=["psB"])
        p.op("act", lambda e: e.copy(Btok[:].rearrange("p j n -> p (j n)"), psB[:, 512:1024]), reads=["psB"], writes=["Btok"])
        for j in range(4):
            for h in range(2):
                hs = slice(h * 64, (h + 1) * 64)
                p.op("dve", lambda e, j=j, h=h, hs=hs: e.tensor_scalar(xdtp[h][:, j, hs], psB[:, j * 128 + h * 64:j * 128 + (h + 1) * 64],
                                                                       dt[:, j, h:h + 1], None, ALU.mult),
                     reads=["psB", "dt"], writes=[f"xdtp{h}"])
                p.op("dve", lambda e, j=j, h=h, hs=hs: e.tensor_scalar(xdd[:, j, hs], psB[:, j * 128 + h * 64:j * 128 + (h + 1) * 64],
                                                                       dte[:, j, h:h + 1], None, ALU.mult),
                     reads=["psB", "dte"], writes=["xdd"])
        for ch in range(2):
            c0 = ch * 256
            j0, j1 = 2 * ch, 2 * ch + 1
            XB = ((ps[5], K(5)), (ps[0], K(0)))
            p.mm(ps[2][:, 0:256], BT[:, c0:c0 + 128], CT[:, c0:c0 + 256], reads=["BT", "CT"], writes=[K(2)])
            p.mm(ps[3][:, 0:128], BT[:, c0 + 128:c0 + 256], CT[:, c0 + 128:c0 + 256], reads=["BT", "CT"], writes=[K(3)])
            for h in range(2):
                xb, xk = XB[h]
                for a2 in range(2):
                    j = 2 * ch + a2
                    dgt = dg2[h][a2]
                    p.op("dve", lambda e, dgt=dgt, j=j, h=h: e.tensor_scalar(dgt[:], identf[:], acum[:, j, h:h + 1], None, ALU.mult),
                         reads=["identf", "acum"], writes=[f"dg{h}{a2}"])
                    p.mm(xb[:, a2 * 128:(a2 + 1) * 128], ones_f[:], dgt[:], reads=["ones_f", f"dg{h}{a2}"], writes=[xk])
            p.mm(ps[6][:, 128:256], Btok[:, j0, :], xdd[:, j0, :], start=True, stop=False, reads=["Btok", "xdd"], writes=[K(6)])
            p.mm(ps[6][:, 128:256], Btok[:, j1, :], xdd[:, j1, :], start=False, stop=True, reads=["Btok", "xdd"], writes=[K(6)])
            p.op("dve", lambda e: e.tensor_tensor(Gm[0][:, 0:256], ps[2][:, 0:256], m01[:, 0:256], ALU.mult),
                 reads=[K(2), "m01"], writes=["Gm0"])
            p.op("dve", lambda e: e.tensor_tensor(Gm[1][:, 0:128], ps[3][:, 0:128], m01[:, 0:128], ALU.mult),
                 reads=[K(3), "m01"], writes=["Gm1"])
            for h in range(2):
                xb, xk = XB[h]
                Dc0, Dc1, LT0, LT1, WT0, WT1, Ebh = Dc2[h][0], Dc2[h][1], LT2[h][0], LT2[h][1], WT2[h][0], WT2[h][1], Eb2[h]
                p.op("dve", lambda e, xb=xb, Dc0=Dc0, h=h, j0=j0: e.tensor_scalar(Dc0[:, 0:256], xb[:, 0:256], nac[:, j0, h:h + 1], 0.0, ALU.add, ALU.min),
                     reads=[xk, "nac"], writes=[f"Dc{h}0"])
                p.op("dve", lambda e, xb=xb, Dc1=Dc1, h=h, j1=j1: e.tensor_scalar(Dc1[:, 0:128], xb[:, 128:256], nac[:, j1, h:h + 1], 0.0, ALU.add, ALU.min),
                     reads=[xk, "nac"], writes=[f"Dc{h}1"])
                p.act(LT0[:, 0:256], Dc0[:, 0:256], AF.Exp, reads=[f"Dc{h}0"], writes=[f"LT{h}0"])
                p.act(LT1[:, 0:128], Dc1[:, 0:128], AF.Exp, reads=[f"Dc{h}1"], writes=[f"LT{h}1"])
                p.act(Ebh[:], xb[:, 0:256], AF.Exp, reads=[xk], writes=[f"Eb{h}"])
                p.op("dve", lambda e, WT0=WT0, LT0=LT0: e.tensor_tensor(WT0[:, 0:256], Gm[0][:, 0:256], LT0[:, 0:256], ALU.mult),
                     reads=["Gm0", f"LT{h}0"], writes=[f"WT{h}0"])
                p.op("dve", lambda e, WT1=WT1, LT1=LT1: e.tensor_tensor(WT1[:, 0:128], Gm[1][:, 0:128], LT1[:, 0:128], ALU.mult),
                     reads=["Gm1", f"LT{h}1"], writes=[f"WT{h}1"])
                p.op("dve", lambda e, h=h, Ebh=Ebh, c0=c0: e.tensor_tensor(CdT[h][:], CT[:, c0:c0 + 256], Ebh[:], ALU.mult),
                     reads=["CT", f"Eb{h}"], writes=[f"CdT{h}"])
            for h in range(2):
                p.mm(ps[4][:, 0:256], hpad[h][:], CdT[h][:], start=(h == 0), stop=False,
                     reads=[f"hpad{h}", f"CdT{h}"], writes=[K(4)])
                p.mm(ps[4][:, 0:256], xdtp[h][:, j0, :], WT2[h][0][:, 0:256], start=False, stop=False,
                     reads=[f"xdtp{h}", f"WT{h}0"], writes=[K(4)])
                p.mm(ps[4][:, 128:256], xdtp[h][:, j1, :], WT2[h][1][:, 0:128], start=False, stop=(h == 1),
                     reads=[f"xdtp{h}", f"WT{h}1"], writes=[K(4)])
            for h in range(2):
                hs = slice(h * 64, (h + 1) * 64)
                p.op("dve", lambda e, h=h, hs=hs, ch=ch: e.scalar_tensor_tensor(hst[:, hs], hst[:, hs], cd[:, ch, h:h + 1],
                                                                        ps[6][:, 128 + h * 64:128 + (h + 1) * 64], ALU.mult, ALU.add),
                     reads=["hst", "cd", K(6)], writes=["hst"])
                p.op("dve", lambda e, h=h, hs=hs: e.tensor_copy(hpad[h][:, hs], hst[:, hs]), reads=["hst"], writes=[f"hpad{h}"])
            o = yst[ch]
            p.op("dve", lambda e, c0=c0: e.scalar_tensor_tensor(t1[:], xT[:, c0:c0 + 256], dcol[:, 0:1], ps[4][:, 0:256], ALU.mult, ALU.add),
                 reads=["xT", "dcol", K(4)], writes=["t1"])
            p.op("dve", lambda e, o=o, c0=c0: e.tensor_tensor(o[:], t1[:], zs[:, c0:c0 + 256], ALU.mult),
                 reads=["t1", "zs"], writes=[f"yst{ch}"])
            p.dma("sp", yT[:, tc * TCH + c0:tc * TCH + c0 + 256], o[:], reads=[f"yst{ch}"], is_out=True)


def ssd_inputs(hT_b, inp, l, j):
    w_in_l = inp["w_in"][l]
    g = j // 2
    hs = [2 * j, 2 * j + 1]
    cz = slice(j * 128, (j + 1) * 128)
    cx = slice(512 + j * 128, 512 + (j + 1) * 128)
    cB = slice(512 + 512 + g * 128, 512 + 512 + (g + 1) * 128)
    cC = slice(512 + 768 + g * 128, 512 + 768 + (g + 1) * 128)
    m = dict(ssd_consts())
    m["hT"] = hT_b
    m["wfm"] = np.ascontiguousarray(np.concatenate([w_in_l[:, cz], w_in_l[:, cx], w_in_l[:, cB], w_in_l[:, cC]], axis=1))
    m["wdt"] = np.ascontiguousarray(w_in_l[:, 1536 + 2 * j:1536 + 2 * j + 2])
    chans = [slice(j * 128, (j + 1) * 128), slice(512 + g * 128, 512 + (g + 1) * 128), slice(768 + g * 128, 768 + (g + 1) * 128)]
    cwl = inp["conv_w"][l]
    cbl = inp["conv_b"][l]
    m["cw"] = np.ascontiguousarray(np.stack([cwl[:, c].T for c in chans], axis=1))
    m["cb"] = np.ascontiguousarray(np.stack([cbl[c] for c in chans], axis=1))
    m["dtb"] = np.ascontiguousarray(inp["dt_bias"][l][hs])
    m["alog"] = np.ascontiguousarray(inp["a_log"][l][hs])
    m["dcol"] = np.ascontiguousarray(np.repeat(inp["d_skip"][l][hs], 64)[:, None])
    return m


D = 1024
NTOK = 2048
NCH = 8
DFF = 2816
NFF = DFF // 128
EPS = 1e-6


def load_rows(p, dst, src, C, N, key, stg, k0=0, engines=("pool", "dve"), cs=None):
    k = k0
    for c in (cs if cs is not None else range(C)):
        for n0 in range(0, N, 2048):
            n1 = min(N, n0 + 2048)
            st = stg[k % len(stg)]
            sk = f"stg{k % len(stg)}"
            p.dma("sp", st[:, 0:n1 - n0], src[:, c, n0:n1], writes=[sk])
            eng = engines[k % len(engines)]
            p.op(eng, lambda e, st=st, c=c, n0=n0, n1=n1: e.tensor_copy(dst[:, c, n0:n1], st[:, 0:n1 - n0]),
                 reads=[sk], writes=[f"{key}_r{c}"])
            k += 1
    return k


def load_block(p, dst, src, C, n0, n1, key, stg, k, engines=("pool", "dve")):
    w = n1 - n0
    st = stg[k % len(stg)]
    sk = f"stg{k % len(stg)}"
    stv = st[:, 0:C * w].rearrange("p (c n) -> p c n", c=C)
    p.dma("sp", stv, src[:, :, n0:n1], writes=[sk])
    eng = engines[k % len(engines)]
    p.op(eng, lambda e: e.tensor_copy(dst[:, :, n0:n1], stv), reads=[sk], writes=[key])
    return k + 1


def rms_tile(p, xt, xk, nwbc, nwk, hb, hbk, junk, ss, rs, tag):
    p.act(junk[:], xt[:], AF.Square, accum_out=ss[:], reads=[xk], writes=["junk", f"ss{tag}"])
    p.op("dve", lambda e: e.tensor_scalar(rs[:], ss[:], 1.0 / D, EPS, ALU.mult, ALU.add), reads=[f"ss{tag}"], writes=[f"rs{tag}"])
    p.act(rs[:], rs[:], AF.Sqrt, reads=[f"rs{tag}"], writes=[f"rs{tag}"])
    p.op("dve", lambda e: e.reciprocal(rs[:], rs[:]), reads=[f"rs{tag}"], writes=[f"rs{tag}"])
    p.op("dve", lambda e: e.scalar_tensor_tensor(hb[:], xt[:], rs[:, 0:1], nwbc[:], ALU.mult, ALU.mult),
         reads=[xk, f"rs{tag}", nwk], writes=[hbk])


def build_c1():
    p = Prog("c1")
    c1_phase(p)
    return p.emit()


def c1_phase(p):
    hT = p.dram("hT", [D, NTOK], BF16)
    x = p.dram("x", [NTOK, D], F32)
    yT = p.dram("yT", [2048, NTOK], F32)
    wg = p.dram("wg", [D, 4096], F32)
    wbr = p.dram("wbr", [2048, D], F32)
    wo = p.dram("wo", [D, D], F32)
    snw = p.dram("snw", [128, 4], F32)
    ident_d = p.dram("ident", [128, 128], BF16)
    x1 = p.dram("x1", [NTOK, D], F32, kind="ExternalOutput")

    wg_sb = p.sb([128, NCH, 4096], BF16)
    wbr_sb = p.sb([128, 16, D], BF16)
    wo_sb = p.sb([128, NCH, D], BF16)
    stg = [p.sb([128, 2048], F32) for _ in range(2)]
    ident = p.sb([128, 128], BF16)
    snw_sb = p.sb([128, 4], F32)
    ones_f = p.sb([128, 2], F32)
    hTt = [p.sb([128, NCH, 128], BF16) for _ in range(2)]
    yTt = [p.sb([128, 16, 128], F32) for _ in range(2)]
    yTb = p.sb([128, 16, 128], BF16)
    ysq = p.sb([128, 4, 128], F32)
    gates = p.sb([128, 4096], BF16)
    u = p.sb([128, D], F32)
    tmp = [p.sb([128, 512], F32) for _ in range(2)]
    ub = p.sb([128, D], BF16)
    uT = p.sb([128, NCH, 128], BF16)
    xt = [p.sb([128, D], F32) for _ in range(2)]
    rstd = p.sb([128, 2], F32)
    ps = [p.ps([128, 512], F32) for _ in range(6)]
    psS = p.ps([128, 512], F32)
    psB = p.ps([128, 1024], BF16)

    p.dma("sp", ident[:], ident_d, writes=["ident"])
    p.dma("sp", snw_sb[:], snw, writes=["snw"])
    p.op("pool", lambda e: e.memset(ones_f[:], 1.0), writes=["ones_f"])
    k = 0
    wg_v = wg.rearrange("(c p) n -> p c n", p=128)
    for b in range(16):
        k = load_block(p, wg_sb, wg_v, NCH, b * 256, (b + 1) * 256, f"wg_b{b}", stg, k)
    k = load_rows(p, wbr_sb, wbr.rearrange("(c p) n -> p c n", p=128), 16, D, "wbr", stg, k0=k)
    k = load_rows(p, wo_sb, wo.rearrange("(c p) n -> p c n", p=128), NCH, D, "wo", stg, k0=k)

    hT_v = hT.rearrange("(c p) t -> p c t", p=128)
    yT_v = yT.rearrange("(c p) t -> p c t", p=128)
    pi = 0
    for t in range(NTOK // 128):
        s = t % 2
        ts_ = slice(t * 128, (t + 1) * 128)
        p.dma("sp", hTt[s][:], hT_v[:, :, ts_], writes=[f"hTt{s}"])
        p.dma("sp", yTt[s][:], yT_v[:, :, ts_], writes=[f"yTt{s}"])
        p.dma("sp", xt[s][:], x[ts_, :], writes=[f"xt{s}"])
        for fc in range(4):
            p.op("dve", lambda e, s=s, fc=fc: e.tensor_scalar(yTb[:, fc, :], yTt[s][:, fc, :], snw_sb[:, fc:fc + 1], None, ALU.mult),
                 reads=[f"yTt{s}", "snw"], writes=["yTb"])
        p.op("pool", lambda e, s=s: e.tensor_copy(yTb[:, 4:16, :], yTt[s][:, 4:16, :]), reads=[f"yTt{s}"], writes=["yTb"])
        p.act(ysq[:], yTt[s][:, 0:4, :], AF.Square, reads=[f"yTt{s}"], writes=["ysq"])
        for fc in range(4):
            p.mm(psS[:, 0:2], ysq[:, fc, :], ones_f[:], start=(fc == 0), stop=(fc == 3), reads=["ysq", "ones_f"], writes=["psS"])
        p.op("dve", lambda e: e.tensor_scalar(rstd[:], psS[:, 0:2], 1.0 / 512, EPS, ALU.mult, ALU.add), reads=["psS"], writes=["rstd"])
        p.act(rstd[:], rstd[:], AF.Sqrt, reads=["rstd"], writes=["rstd"])
        p.op("dve", lambda e: e.reciprocal(rstd[:], rstd[:]), reads=["rstd"], writes=["rstd"])
        for n in range(8):
            pp = ps[pi % 6]; pk = f"ps{pi % 6}"; pi += 1
            for c in range(NCH):
                p.mm(pp[:, :], hTt[s][:, c, :], wg_sb[:, c, n * 512:(n + 1) * 512], start=(c == 0), stop=(c == NCH - 1),
                     reads=[f"hTt{s}", f"wg_b{2 * n}", f"wg_b{2 * n + 1}"], writes=[pk])
            p.act(gates[:, n * 512:(n + 1) * 512], pp[:, :], AF.Sigmoid, reads=[pk], writes=["gates"])
        for i in range(4):
            for hf in range(2):
                pp = ps[pi % 6]; pk = f"ps{pi % 6}"; pi += 1
                for fc in range(4):
                    p.mm(pp[:, :], yTb[:, i * 4 + fc, :], wbr_sb[:, i * 4 + fc, hf * 512:(hf + 1) * 512], start=(fc == 0), stop=(fc == 3),
                         reads=["yTb", f"wbr_r{i * 4 + fc}"], writes=[pk])
                gsl = gates[:, i * 1024 + hf * 512:i * 1024 + (hf + 1) * 512]
                usl = u[:, hf * 512:(hf + 1) * 512]
                if i == 0:
                    p.op("dve", lambda e, pp=pp, gsl=gsl, usl=usl: e.scalar_tensor_tensor(usl, pp[:, :], rstd[:, 0:1], gsl, ALU.mult, ALU.mult),
                         reads=[pk, "rstd", "gates"], writes=["u"])
                else:
                    tm = tmp[hf]
                    p.op("dve", lambda e, pp=pp, gsl=gsl, tm=tm: e.tensor_tensor(tm[:], pp[:, :], gsl, ALU.mult),
                         reads=[pk, "gates"], writes=[f"tmp{hf}"])
                    p.op("pool", lambda e, tm=tm, usl=usl: e.tensor_tensor(usl, usl, tm[:], ALU.add),
                         reads=[f"tmp{hf}", "u"], writes=["u"])
        p.op("pool", lambda e: e.tensor_copy(ub[:], u[:]), reads=["u"], writes=["ub"])
        for c in range(NCH):
            p.tr(psB[:, c * 128:(c + 1) * 128], ub[:, c * 128:(c + 1) * 128], ident[:], reads=["ub", "ident"], writes=["psB"])
        p.op("act", lambda e: e.copy(uT[:].rearrange("p c t -> p (c t)"), psB[:, :]), reads=["psB"], writes=["uT"])
        for hf in range(2):
            pp = ps[pi % 6]; pk = f"ps{pi % 6}"; pi += 1
            for c in range(NCH):
                p.mm(pp[:, :], uT[:, c, :], wo_sb[:, c, hf * 512:(hf + 1) * 512], start=(c == 0), stop=(c == NCH - 1),
                     reads=["uT", f"wo_r{c}"], writes=[pk])
            p.op("dve", lambda e, pp=pp, s=s, hf=hf: e.tensor_tensor(xt[s][:, hf * 512:(hf + 1) * 512], pp[:, :], xt[s][:, hf * 512:(hf + 1) * 512], ALU.add),
                 reads=[pk, f"xt{s}"], writes=[f"xt{s}"])
        p.dma("sp", x1[ts_, :], xt[s][:], reads=[f"xt{s}"], is_out=True)


def build_c2():
    p = Prog("c2")
    c2_phase(p)
    return p.emit()


def c2_phase(p):
    x1 = p.dram("x1", [NTOK, D], F32)
    nw = p.dram("nw", [D], F32)
    nwf = p.dram("nwf", [D], F32)
    wgt = p.dram("wgt", [D, DFF], F32)
    wup = p.dram("wup", [D, DFF], F32)
    wdn = p.dram("wdn", [DFF, D], F32)
    ident_d = p.dram("ident", [128, 128], BF16)
    x2 = p.dram("x2", [NTOK, D], F32, kind="ExternalOutput")
    xn = p.dram("xn", [NTOK, D], F32, kind="ExternalOutput")

    wg_sb = p.sb([128, NCH, DFF], BF16)
    wu_sb = p.sb([128, NCH, DFF], BF16)
    wd_sb = p.sb([128, NFF, D], BF16)
    stg = [p.sb([128, 2048], F32) for _ in range(2)]
    ident = p.sb([128, 128], BF16)
    nwbc = p.sb([128, D], F32)
    nwfbc = p.sb([128, D], F32)
    xt = [p.sb([128, D], F32) for _ in range(3)]
    junk = p.sb([128, D], BF16)
    hb = p.sb([128, D], BF16)
    ss = p.sb([128, 1], F32)
    rs = p.sb([128, 1], F32)
    ss2 = p.sb([128, 1], F32)
    rs2 = p.sb([128, 1], F32)
    h2T = p.sb([128, NCH, 256], BF16)
    aT = p.sb([128, NFF, 256], BF16)
    sg = [p.sb([128, 256], F32) for _ in range(2)]
    xo = [p.sb([128, D], F32) for _ in range(1)]
    ps = [p.ps([128, 512], F32) for _ in range(6)]
    psB = p.ps([128, 1024], BF16)

    p.dma("sp", ident[:], ident_d, writes=["ident"])
    p.dma("sp", nwbc[:], nw.partition_broadcast(128), writes=["nwbc"])
    p.dma("sp", nwfbc[:], nwf.partition_broadcast(128), writes=["nwfbc"])
    k = 0
    wgt_v = wgt.rearrange("(c p) n -> p c n", p=128)
    wup_v = wup.rearrange("(c p) n -> p c n", p=128)
    for b in range(DFF // 256):
        k = load_block(p, wg_sb, wgt_v, NCH, b * 256, (b + 1) * 256, f"wgt_b{b}", stg, k)
        k = load_block(p, wu_sb, wup_v, NCH, b * 256, (b + 1) * 256, f"wup_b{b}", stg, k)
    k = load_rows(p, wd_sb, wdn.rearrange("(c p) n -> p c n", p=128), NFF, D, "wdn", stg, k0=k)

    pi = 0
    oi = 0
    for g in range(NTOK // 256):
        for j in range(2):
            t = g * 2 + j
            xs = xt[t % 3]
            xk = f"xt{t % 3}"
            p.dma("sp", xs[:], x1[t * 128:(t + 1) * 128, :], writes=[xk])
            rms_tile(p, xs, xk, nwbc, "nwbc", hb, "hb", junk, ss, rs, "a")
            for c in range(NCH):
                p.tr(psB[:, c * 128:(c + 1) * 128], hb[:, c * 128:(c + 1) * 128], ident[:], reads=["hb", "ident"], writes=["psB"])
            p.op("act", lambda e, j=j: e.copy(h2T[:, :, j * 128:(j + 1) * 128], psB[:, :].rearrange("p (c t) -> p c t", c=NCH)),
                 reads=["psB"], writes=["h2T"])
        for f in range(NFF):
            pa = ps[pi % 6]; pak = f"ps{pi % 6}"; pi += 1
            pb = ps[pi % 6]; pbk = f"ps{pi % 6}"; pi += 1
            for c in range(NCH):
                p.mm(pa[:, 0:256], wg_sb[:, c, f * 128:(f + 1) * 128], h2T[:, c, :], start=(c == 0), stop=(c == NCH - 1),
                     reads=[f"wgt_b{f // 2}", "h2T"], writes=[pak])
            for c in range(NCH):
                p.mm(pb[:, 0:256], wu_sb[:, c, f * 128:(f + 1) * 128], h2T[:, c, :], start=(c == 0), stop=(c == NCH - 1),
                     reads=[f"wup_b{f // 2}", "h2T"], writes=[pbk])
            sgt = sg[f % 2]
            p.act(sgt[:], pa[:, 0:256], AF.Silu, reads=[pak], writes=[f"sg{f % 2}"])
            p.op("dve", lambda e, pb=pb, sgt=sgt, f=f: e.tensor_tensor(aT[:, f, :], pb[:, 0:256], sgt[:], ALU.mult),
                 reads=[pbk, f"sg{f % 2}"], writes=["aT"])
        for j in range(2):
            t = g * 2 + j
            xs = xt[t % 3]
            xk = f"xt{t % 3}"
            o = xo[0]; ok = "xo0"; oi += 1
            for hf in range(2):
                pp = ps[pi % 6]; pk = f"ps{pi % 6}"; pi += 1
                for f in range(NFF):
                    p.mm(pp[:, :], aT[:, f, j * 128:(j + 1) * 128], wd_sb[:, f, hf * 512:(hf + 1) * 512], start=(f == 0), stop=(f == NFF - 1),
                         reads=["aT", f"wdn_r{f}"], writes=[pk])
                p.op("dve", lambda e, pp=pp, xs=xs, hf=hf: e.tensor_tensor(xs[:, hf * 512:(hf + 1) * 512], pp[:, :], xs[:, hf * 512:(hf + 1) * 512], ALU.add),
                     reads=[pk, xk], writes=[xk])
            p.dma("sp", x2[t * 128:(t + 1) * 128, :], xs[:], reads=[xk], is_out=True)
            p.act(junk[:], xs[:], AF.Square, accum_out=ss2[:], reads=[xk], writes=["junk", "ss2"])
            p.op("dve", lambda e: e.tensor_scalar(rs2[:], ss2[:], 1.0 / D, EPS, ALU.mult, ALU.add), reads=["ss2"], writes=["rs2"])
            p.act(rs2[:], rs2[:], AF.Sqrt, reads=["rs2"], writes=["rs2"])
            p.op("dve", lambda e: e.reciprocal(rs2[:], rs2[:]), reads=["rs2"], writes=["rs2"])
            p.op("dve", lambda e, o=o, xs=xs: e.scalar_tensor_tensor(o[:], xs[:], rs2[:, 0:1], nwfbc[:], ALU.mult, ALU.mult),
                 reads=[xk, "rs2", "nwfbc"], writes=[ok])
            p.dma("sp", xn[t * 128:(t + 1) * 128, :], o[:], reads=[ok], is_out=True)


MIX = ((0, "ssd"), (1, "moba"), (2, "fox"), (3, "swa"))


def build_ab():
    p = Prog("ab")
    hT_i = p.nc.dram_tensor("hT_scr", [D, S], BF16).ap()
    yT_all = p.nc.dram_tensor("yT", [4, 128, S], F32, kind="ExternalOutput").ap()
    p.dp, p.dov = "n_", {"hT": hT_i}
    p.begin_phase("norm")
    norm_phase(p, S, D)
    p.end_phase()
    for bi, mode in MIX:
        p.dp, p.dov = mode + "_", {"hT": hT_i, "yT": yT_all[bi]}
        p.begin_phase(mode)
        if mode == "ssd":
            ssd_phase(p)
        else:
            attn_phase(p, mode)
        p.end_phase()
    return p.emit()


def ab_inputs(x_b, inp, l, j):
    m = {"n_x": x_b, "n_nw": inp["norm_mix"][l], "n_ident": np.eye(128).astype(NPBF)}
    extra = dict(forget_bias=inp["forget_bias"][l], rel_bias=inp["rel_bias"], sinks=inp["sinks"][l])
    for bi, mode in MIX:
        sub = ssd_inputs(None, inp, l, j) if mode == "ssd" else attn_inputs(mode, None, inp["w_in"][l], j, extra)
        for k, v in sub.items():
            if k != "hT":
                m[f"{mode}_{k}"] = v
    return m


def build_cd():
    p = Prog("cd")
    hT_i = p.nc.dram_tensor("hT_scr", [D, NTOK], BF16).ap()
    x1_i = p.nc.dram_tensor("x1_scr", [NTOK, D], F32).ap()
    x_in = p.dram("x", [NTOK, D], F32)
    p.dp, p.dov = "n_", {"hT": hT_i, "x": x_in}
    p.begin_phase("norm")
    norm_phase(p, NTOK, D)
    p.end_phase()
    p.dp, p.dov = "c1_", {"hT": hT_i, "x": x_in, "x1": x1_i}
    p.begin_phase("c1")
    c1_phase(p)
    p.end_phase()
    p.dp, p.dov = "c2_", {"x1": x1_i}
    p.begin_phase("c2")
    c2_phase(p)
    p.end_phase()
    return p.emit()


def cd_inputs(x_slab, yT_slab, inp, l):
    ident = np.eye(128).astype(NPBF)
    return {
        "x": x_slab, "n_nw": inp["norm_mix"][l], "n_ident": ident,
        "c1_yT": yT_slab, "c1_wg": np.ascontiguousarray(inp["w_in"][l][:, 5392:]),
        "c1_wbr": np.ascontiguousarray(inp["w_branch"][l].reshape(2048, D)), "c1_wo": inp["w_out"][l],
        "c1_snw": np.ascontiguousarray(inp["ssm_norm_w"][l].reshape(4, 128).T), "c1_ident": ident,
        "c2_nw": inp["norm_ffn"][l], "c2_nwf": inp["norm_final"], "c2_wgt": inp["w_ffn_gate"][l],
        "c2_wup": inp["w_ffn_up"][l], "c2_wdn": inp["w_ffn_down"][l], "c2_ident": ident,
    }


N_CORES = 8


def _run(nc, in_maps):
    return run_bass_kernel_spmd(nc, in_maps, core_ids=list(range(N_CORES))).results


def kernel(x, w_in, conv_w, conv_b, dt_bias, a_log, d_skip, ssm_norm_w, forget_bias, sinks, rel_bias,
           w_branch, w_out, norm_mix, norm_ffn, w_ffn_gate, w_ffn_up, w_ffn_down, norm_final):
    f32 = lambda a: np.ascontiguousarray(np.asarray(a, dtype=np.float32))
    inp = dict(x=f32(x), w_in=f32(w_in), conv_w=f32(conv_w), conv_b=f32(conv_b), dt_bias=f32(dt_bias), a_log=f32(a_log),
               d_skip=f32(d_skip), ssm_norm_w=f32(ssm_norm_w), forget_bias=f32(forget_bias), sinks=f32(sinks),
               rel_bias=f32(rel_bias), w_branch=f32(w_branch), w_out=f32(w_out), norm_mix=f32(norm_mix),
               norm_ffn=f32(norm_ffn), w_ffn_gate=f32(w_ffn_gate), w_ffn_up=f32(w_ffn_up), w_ffn_down=f32(w_ffn_down),
               norm_final=f32(norm_final))
    xcur = inp["x"]
    xn = None
    for l in range(2):
        res = _run(build_ab(), [ab_inputs(np.ascontiguousarray(xcur[c // 4]), inp, l, c % 4) for c in range(N_CORES)])
        yT_b = [np.concatenate([np.asarray(res[b * 4 + j]["yT"])[bi] for bi in range(4) for j in range(4)], axis=0)
                for b in range(2)]
        maps = []
        for c in range(N_CORES):
            b, sl = c // 4, slice((c % 4) * NTOK, (c % 4 + 1) * NTOK)
            maps.append(cd_inputs(np.ascontiguousarray(xcur[b, sl]), np.ascontiguousarray(yT_b[b][:, sl]), inp, l))
        res = _run(build_cd(), maps)
        xcur = np.stack([np.concatenate([np.asarray(res[b * 4 + q]["c2_x2"]) for q in range(4)], axis=0) for b in range(2)])
        xn = np.stack([np.concatenate([np.asarray(res[b * 4 + q]["c2_xn"]) for q in range(4)], axis=0) for b in range(2)])
    return np.ascontiguousarray(xn.astype(np.float32))
```
